# Optimizing a Trainium2 kernel written in Bass

```python
import math
import jax, jax.numpy as jnp
from jax import lax
import numpy as np

D_MODEL = 1024
BATCH = 8
SEQ = 2048
DEPTH = 1

LRU_WIDTH = D_MODEL
LRU_BLOCKS = 16
LRU_BLOCK_W = LRU_WIDTH // LRU_BLOCKS
CONV_WIDTH = 4
LRU_C = 8.0
HEAD_DIM = 64
N_Q_HEADS = 16
N_KV_HEADS = 2
GQA_GROUP = N_Q_HEADS // N_KV_HEADS
ATTN_WIDTH = N_Q_HEADS * HEAD_DIM
KV_WIDTH = N_KV_HEADS * HEAD_DIM
WINDOW = 128
BLOCK = 128
IN_WIDTH = 2 * LRU_WIDTH + ATTN_WIDTH + 2 * KV_WIDTH
MIX_WIDTH = LRU_WIDTH + ATTN_WIDTH
D_FF = 4 * D_MODEL
EPS = 1e-6
NEG_INF = -1e30

kernel_name = "hymba_rglru_swa_sink_hybrid"


def rmsnorm(x, g):
    x32 = x.astype(jnp.float32)
    y = x32 * lax.rsqrt(jnp.mean(x32 * x32, axis=-1, keepdims=True) + EPS)
    return (y * g.astype(jnp.float32)).astype(x.dtype)


def causal_depthwise_conv(x, w, b):
    t = x.shape[1]
    xp = jnp.pad(x, ((0, 0), (CONV_WIDTH - 1, 0), (0, 0)))
    y = sum(w[k] * xp[:, k:k + t] for k in range(CONV_WIDTH))
    return y + b


def rg_lru(x, w_a, b_a, w_x, b_x, lam):
    bsz, t, _ = x.shape
    xb = x.reshape(bsz, t, LRU_BLOCKS, LRU_BLOCK_W)
    gate_r = jax.nn.sigmoid(jnp.einsum('btnc,ncd->btnd', xb, w_a).reshape(bsz, t, LRU_WIDTH) + b_a)
    gate_i = jax.nn.sigmoid(jnp.einsum('btnc,ncd->btnd', xb, w_x).reshape(bsz, t, LRU_WIDTH) + b_x)
    r32 = gate_r.astype(jnp.float32)
    log_a = -LRU_C * r32 * jax.nn.softplus(-lam.astype(jnp.float32))
    a = jnp.exp(log_a)
    mult = jnp.sqrt(-jnp.expm1(2.0 * log_a))
    bterm = mult * (gate_i * x).astype(jnp.float32)

    def combine(lhs, rhs):
        a_l, b_l = lhs
        a_r, b_r = rhs
        return a_l * a_r, a_r * b_l + b_r

    _, h = lax.associative_scan(combine, (a, bterm), axis=1)
    return h.astype(x.dtype)


def sliding_window_sink_attention(q, k, v, sinks):
    bsz, t, _ = q.shape
    nb = t // BLOCK
    qb = q.reshape(bsz, nb, BLOCK, N_KV_HEADS, GQA_GROUP, HEAD_DIM)
    kb = k.reshape(bsz, nb, BLOCK, N_KV_HEADS, HEAD_DIM)
    vb = v.reshape(bsz, nb, BLOCK, N_KV_HEADS, HEAD_DIM)
    zero_blk = jnp.zeros_like(kb[:, :1])
    kp = jnp.concatenate([zero_blk, kb], axis=1)
    vp = jnp.concatenate([zero_blk, vb], axis=1)
    kwin = jnp.concatenate([kp[:, :-1], kp[:, 1:]], axis=2)
    vwin = jnp.concatenate([vp[:, :-1], vp[:, 1:]], axis=2)

    scale = 1.0 / math.sqrt(HEAD_DIM)
    scores = jnp.einsum('bnqhgd,bnkhd->bnhgqk', qb, kwin).astype(jnp.float32) * scale
    blk = jnp.arange(nb)[:, None, None]
    qpos = blk * BLOCK + jnp.arange(BLOCK)[None, :, None]
    kpos = (blk - 1) * BLOCK + jnp.arange(2 * BLOCK)[None, None, :]
    mask = (kpos <= qpos) & (kpos > qpos - WINDOW) & (kpos >= 0)
    scores = jnp.where(mask[None, :, None, None], scores, NEG_INF)

    sink = sinks.astype(jnp.float32).reshape(1, 1, N_KV_HEADS, GQA_GROUP, 1, 1)
    m = jnp.maximum(jnp.max(scores, axis=-1, keepdims=True), sink)
    p = jnp.exp(scores - m)
    denom = jnp.sum(p, axis=-1, keepdims=True) + jnp.exp(sink - m)
    probs = (p / denom).astype(v.dtype)
    out = jnp.einsum('bnhgqk,bnkhd->bnqhgd', probs, vwin)
    return out.reshape(bsz, t, ATTN_WIDTH)


def setup_inputs(seed: int = 0) -> dict:
    key = jax.random.key(seed)
    ks = jax.random.split(key, 24)
    f32 = jnp.float32

    def nrm(k, shape, scale):
        return jax.random.normal(k, shape, f32) * scale

    def gain(k, n):
        return jnp.ones((DEPTH, n), f32) + 0.02 * jax.random.normal(k, (DEPTH, n), f32)

    x = jax.random.normal(ks[0], (BATCH, SEQ, D_MODEL), f32)
    base = jax.random.uniform(ks[9], (DEPTH, LRU_WIDTH), f32, minval=0.9, maxval=0.999)
    s = base ** (1.0 / LRU_C)
    lru_lambda = jnp.log(s) - jnp.log1p(-s)
    return {
        "x": x,
        "norm_mix_g": gain(ks[1], D_MODEL),
        "w_in": nrm(ks[2], (DEPTH, D_MODEL, IN_WIDTH), D_MODEL ** -0.5),
        "conv_w": nrm(ks[3], (DEPTH, CONV_WIDTH, LRU_WIDTH), CONV_WIDTH ** -0.5),
        "conv_b": nrm(ks[4], (DEPTH, LRU_WIDTH), 0.02),
        "w_gate_a": nrm(ks[5], (DEPTH, LRU_BLOCKS, LRU_BLOCK_W, LRU_BLOCK_W), LRU_BLOCK_W ** -0.5),
        "b_gate_a": nrm(ks[6], (DEPTH, LRU_WIDTH), 0.02),
        "w_gate_x": nrm(ks[7], (DEPTH, LRU_BLOCKS, LRU_BLOCK_W, LRU_BLOCK_W), LRU_BLOCK_W ** -0.5),
        "b_gate_x": nrm(ks[8], (DEPTH, LRU_WIDTH), 0.02),
        "lru_lambda": lru_lambda,
        "attn_sinks": nrm(ks[10], (DEPTH, N_Q_HEADS), 0.5),
        "lru_out_g": gain(ks[11], LRU_WIDTH),
        "attn_out_g": gain(ks[12], ATTN_WIDTH),
        "w_out": nrm(ks[13], (DEPTH, MIX_WIDTH, D_MODEL), MIX_WIDTH ** -0.5),
        "norm_mlp_g": gain(ks[14], D_MODEL),
        "w_mlp_up": nrm(ks[15], (DEPTH, D_MODEL, D_FF), D_MODEL ** -0.5),
        "w_mlp_down": nrm(ks[16], (DEPTH, D_FF, D_MODEL), D_FF ** -0.5),
        "norm_final_g": jnp.ones((D_MODEL,), f32) + 0.02 * jax.random.normal(ks[17], (D_MODEL,), f32),
    }


def reference(x, norm_mix_g, w_in, conv_w, conv_b, w_gate_a, b_gate_a, w_gate_x, b_gate_x,
              lru_lambda, attn_sinks, lru_out_g, attn_out_g, w_out, norm_mlp_g,
              w_mlp_up, w_mlp_down, norm_final_g):
    s1 = LRU_WIDTH
    s2 = 2 * LRU_WIDTH
    s3 = s2 + ATTN_WIDTH
    s4 = s3 + KV_WIDTH
    for l in range(DEPTH):
        hn = rmsnorm(x, norm_mix_g[l])
        proj = jnp.einsum('btd,de->bte', hn, w_in[l])
        x_lru = proj[..., :s1]
        g_lru = proj[..., s1:s2]
        q = proj[..., s2:s3]
        k = proj[..., s3:s4]
        v = proj[..., s4:]

        xc = causal_depthwise_conv(x_lru, conv_w[l], conv_b[l])
        h = rg_lru(xc, w_gate_a[l], b_gate_a[l], w_gate_x[l], b_gate_x[l], lru_lambda[l])
        y_lru = h * jax.nn.gelu(g_lru, approximate=True)

        y_attn = sliding_window_sink_attention(q, k, v, attn_sinks[l])

        mixed = jnp.concatenate([rmsnorm(y_lru, lru_out_g[l]),
                                 rmsnorm(y_attn, attn_out_g[l])], axis=-1)
        x = x + jnp.einsum('bte,ed->btd', mixed, w_out[l])

        hm = rmsnorm(x, norm_mlp_g[l])
        up = jnp.einsum('btd,df->btf', hm, w_mlp_up[l])
        x = x + jnp.einsum('btf,fd->btd', jnp.square(jax.nn.relu(up)), w_mlp_down[l])
    return rmsnorm(x, norm_final_g)
```

```python
import numpy as np
from contextlib import ExitStack
import concourse.bass as bass
import concourse.mybir as mybir
from concourse.bass_utils import run_bass_kernel_spmd

F32 = mybir.dt.float32
BF16 = mybir.dt.bfloat16
AF = mybir.ActivationFunctionType
ALU = mybir.AluOpType

D = 1024
T = 2048
NT = T // 128
NTT = T // 512
INW = 3328
DFF = 4096
EPS = 1e-6
NV = 11
(V_CW0, V_CW1, V_CW2, V_CW3, V_CB, V_BA, V_BX, V_LAM, V_GL, V_GA, V_SINK) = range(NV)


class Sched:
    ENGS = ("pe", "act", "dve", "pool", "sp")

    def __init__(self, nc, stack, n_dma_sems=8):
        self.nc = nc
        self.e = {"pe": nc.tensor, "act": nc.scalar, "dve": nc.vector, "pool": nc.gpsimd, "sp": nc.sync}
        self.sem, self.cnt = {}, {}
        for n in self.ENGS:
            self.sem[n] = stack.enter_context(nc.semaphore("s_" + n))
            self.cnt[n] = 0
        self.dma_pool, self.dma_rr = {}, {}
        for q in ("sp", "pool", "act"):
            lst = []
            for i in range(n_dma_sems):
                nm = "d_%s%d" % (q, i)
                self.sem[nm] = stack.enter_context(nc.semaphore(nm))
                self.cnt[nm] = 0
                lst.append(nm)
            self.dma_pool[q] = lst
            self.dma_rr[q] = 0
        self.clock = {n: {} for n in self.ENGS}
        self.opclock = {}
        self.state = {}

    def _deps(self, reads, writes):
        deps = {}

        def add(d):
            if d is not None and deps.get(d[0], 0) < d[1]:
                deps[d[0]] = d[1]
        for k in reads:
            st = self.state.get(k)
            if st:
                add(st[0])
        for k in writes:
            st = self.state.get(k)
            if st:
                add(st[0])
                for r in st[1]:
                    add(r)
        return deps

    def _wait(self, eng, deps):
        ck = self.clock[eng]
        for p, n in deps.items():
            if ck.get(p, 0) >= n:
                continue
            self.e[eng].wait_ge(self.sem[p], n)
            oc = self.opclock.get((p, n))
            if oc:
                for q, m in oc.items():
                    if ck.get(q, 0) < m:
                        ck[q] = m
            ck[p] = n

    def _record(self, opid, reads, writes):
        for k in reads:
            self.state.setdefault(k, [None, []])[1].append(opid)
        for k in writes:
            self.state[k] = [opid, []]

    def op(self, eng, fn, reads=(), writes=()):
        self._wait(eng, self._deps(reads, writes))
        ins = fn(self.e[eng])
        self.cnt[eng] += 1
        ins.then_inc(self.sem[eng], 1)
        opid = (eng, self.cnt[eng])
        self.opclock[opid] = dict(self.clock[eng])
        self._record(opid, reads, writes)

    def dma(self, q, out, in_, reads=(), writes=()):
        pool = self.dma_pool[q]
        s = pool[self.dma_rr[q] % len(pool)]
        self.dma_rr[q] += 1
        deps = self._deps(reads, writes)
        if self.cnt[s] > 0 and deps.get(s, 0) < self.cnt[s]:
            deps[s] = self.cnt[s]
        self._wait(q, deps)
        ins = self.e[q].dma_start(out=out, in_=in_)
        self.cnt[s] += 16
        ins.then_inc(self.sem[s], 16)
        opid = (s, self.cnt[s])
        self.opclock[opid] = dict(self.clock[q])
        self._record(opid, reads, writes)

    def wait_keys(self, eng, keys):
        self._wait(eng, self._deps(keys, ()))

    def wait_readers(self, eng, keys):
        self._wait(eng, self._deps((), keys))

    def barrier(self):
        allk = list(self.state.keys())
        deps = self._deps((), allk)
        for eng in self.ENGS:
            self._wait(eng, dict(deps))


def build_program(t_tokens=T):
    assert t_tokens == T
    nc = bass.Bass("TRN2", target_bir_lowering=False)
    dt_in = lambda name, shape: nc.dram_tensor(name, shape, F32, kind="ExternalInput").ap()
    x_d = dt_in("x", [T, D])
    gmix_d = dt_in("norm_mix_g", [D])
    win_d = dt_in("w_in", [D, INW])
    wga_d = dt_in("w_gate_a", [16, 64, 64])
    wgx_d = dt_in("w_gate_x", [16, 64, 64])
    vec_d = dt_in("vecs", [128, NV * 8])
    wout_d = dt_in("w_out", [2 * D, D])
    gmlp_d = dt_in("norm_mlp_g", [D])
    wup_d = dt_in("w_mlp_up", [D, DFF])
    wdn_d = dt_in("w_mlp_down", [DFF, D])
    gfin_d = dt_in("norm_final_g", [D])
    out_d = nc.dram_tensor("out", [T, D], F32, kind="ExternalOutput").ap()

    win_r = win_d.rearrange("(k p) e -> p k e", p=128)
    wout_r = wout_d.rearrange("(k p) e -> p k e", p=128)
    wup_r = wup_d.rearrange("(k p) e -> p k e", p=128)
    wdn_r = wdn_d.rearrange("(k p) e -> p k e", p=128)

    with ExitStack() as st:
        S = Sched(nc, st)
        SB = lambda name, shape, dt: st.enter_context(nc.sbuf_tensor(name, shape, dt))
        hT = SB("hT", [128, 8, T], BF16)
        mix = SB("mix", [128, 16 * T], BF16)
        big = SB("big", [128, 16 * D], F32)
        wo = SB("wo", [128, 16 * D], BF16)
        aux = SB("aux", [128, 4096], BF16)
        wst = [aux[:, i * 2048:(i + 1) * 2048].rearrange("p (k e) -> p k e", k=8) for i in range(2)]
        vec = SB("vec", [128, NV, 8], F32)
        dv = SB("dv", [128, 6, 8], F32)
        gbc = aux[:, 2048:4096].bitcast(F32)
        ident = SB("ident", [128, 128], BF16)
        identf = SB("identf", [128, 128], F32)
        maskf = SB("maskf", [128, 2, 128], F32)
        mask = SB("mask", [128, 2, 128], BF16)
        onec = SB("onec", [128, 1], BF16)
        stat = SB("stat", [128, 4, NT], F32)
        hn = [aux[:, i * 1024:(i + 1) * 1024] for i in range(2)]

        mixv = mix[:].rearrange("p (e t) -> p e t", e=16)
        xres = big[:].rearrange("p (j d) -> p j d", j=NT)
        wov = wo[:].rearrange("p (e d) -> p e d", e=16)

        bigbf = big[:].bitcast(BF16)

        def f32s(off, n):
            return big[:, off:off + n]
        XL = f32s(0, 8 + T)
        o_ = 8 + T
        CH = {}
        for nm in ("A", "W", "GX", "GG"):
            CH[nm] = f32s(o_, T)
            o_ += T
        tmp = {}
        for nm, nb_ in (("xc", 2), ("ut", 2), ("tr", 1), ("ti", 1), ("hh", 2)):
            for b in range(nb_):
                tmp[(nm, b)] = f32s(o_, 512)
                o_ += 512
        ob = 2 * o_
        xcb = [bigbf[:, ob + i * 512: ob + (i + 1) * 512] for i in range(2)]
        ob += 1024
        ysq = [bigbf[:, ob + i * 512: ob + (i + 1) * 512] for i in range(2)]
        ob += 1024
        assert ob <= 32768
        wg_a = SB("wg_a", [128, 8, 128], BF16)
        wg_x = SB("wg_x", [128, 8, 128], BF16)
        WH = 8192
        kTd = wo[:, WH:WH + 2 * T].rearrange("p (h t) -> p h t", h=2)
        vaug = wo[:, WH + 2 * T:WH + 2 * T + NT * 128].rearrange("p (j h m) -> p j h m", j=NT, h=2)
        wq = [wo[:, WH + 6144 + i * 1024:WH + 6144 + (i + 1) * 1024].rearrange("p (k e) -> p k e", k=8) for i in range(2)]
        qT = mixv[:, 8:16, :]
        ptb = [aux[:, i * 1024:(i + 1) * 1024].rearrange("p (e f) -> p e f", e=2) for i in range(2)]
        rden = [aux[:, 2048 + i * 512:2048 + (i + 1) * 512].bitcast(F32) for i in range(2)]
        yab = [aux[:, 3072 + i * 256:3072 + (i + 1) * 256] for i in range(2)]

        PS = [st.enter_context(nc.psum_tensor("ps%d" % i, [128, 512], F32)) for i in range(8)]
        PT = PS[7][:].bitcast(BF16)

        S.dma("sp", vec[:].rearrange("p v c -> p (v c)"), vec_d, writes=["vec"])
        S.dma("sp", gbc, gmix_d.partition_broadcast(128), writes=["gbc"])
        S.op("pool", lambda e: e.memset(identf[:], 1.0), writes=["identf"])
        S.op("pool", lambda e: e.affine_select(out=identf[:], in_=identf[:], pattern=[[-1, 128]],
                                                 compare_op=ALU.is_equal, fill=0.0, base=0, channel_multiplier=1),
             reads=["identf"], writes=["identf"])
        S.op("pool", lambda e: e.memset(maskf[:], 1.0), writes=["maskf"])
        S.op("pool", lambda e: e.affine_select(out=maskf[:, 0, :], in_=maskf[:, 0, :], pattern=[[-1, 128]],
                                                 compare_op=ALU.is_gt, fill=0.0, base=0, channel_multiplier=1),
             reads=["maskf"], writes=["maskf"])
        S.op("pool", lambda e: e.affine_select(out=maskf[:, 1, :], in_=maskf[:, 1, :], pattern=[[1, 128]],
                                                 compare_op=ALU.is_ge, fill=0.0, base=0, channel_multiplier=-1),
             reads=["maskf"], writes=["maskf"])
        S.op("dve", lambda e: e.tensor_copy(out=ident[:], in_=identf[:]), reads=["identf"], writes=["ident"])
        S.op("dve", lambda e: e.tensor_copy(out=mask[:], in_=maskf[:]), reads=["maskf"], writes=["mask"])
        S.op("dve", lambda e: e.memset(onec[:], 1.0), writes=["onec"])
        S.op("dve", lambda e: e.memset(stat[:], 0.0), writes=["stat"])
        S.op("dve", lambda e: e.memset(wg_a[:], 0.0), writes=["wg_a"])
        S.op("dve", lambda e: e.memset(wg_x[:], 0.0), writes=["wg_x"])
        for gd, gt, nm in ((wga_d, wg_a, "wg_a"), (wgx_d, wg_x, "wg_x")):
            gr = gd.rearrange("(c t) i j -> t i c j", t=2)
            for t2 in range(2):
                S.dma("pool", gt[t2 * 64:(t2 + 1) * 64, :, t2 * 64:(t2 + 1) * 64], gr[t2], writes=[nm])
        S.op("dve", lambda e: e.tensor_scalar(out=dv[:, 0, :], in0=vec[:, V_BA, :], scalar1=0.5, scalar2=None,
                                              op0=ALU.mult), reads=["vec"], writes=["dv0"])
        S.op("dve", lambda e: e.tensor_scalar(out=dv[:, 1, :], in0=vec[:, V_BX, :], scalar1=0.5, scalar2=None,
                                              op0=ALU.mult), reads=["vec"], writes=["dv1"])
        S.op("act", lambda e: e.activation(out=dv[:, 5, :], in_=vec[:, V_LAM, :], func=AF.Exp, scale=-1.0),
             reads=["vec"], writes=["dv5"])
        S.op("act", lambda e: e.activation(out=dv[:, 5, :], in_=dv[:, 5, :], func=AF.Ln, bias=1.0),
             reads=["dv5"], writes=["dv5"])
        S.op("dve", lambda e: e.tensor_scalar(out=dv[:, 2, :], in0=dv[:, 5, :], scalar1=-4.0, scalar2=None,
                                              op0=ALU.mult), reads=["dv5"], writes=["dv2"])
        S.op("dve", lambda e: e.tensor_scalar(out=dv[:, 3, :], in0=dv[:, 5, :], scalar1=-8.0, scalar2=None,
                                              op0=ALU.mult), reads=["dv5"], writes=["dv3"])
        S.op("act", lambda e: e.activation(out=dv[:, 4, :], in_=vec[:, V_SINK, :], func=AF.Exp),
             reads=["vec"], writes=["dv4"])
        CONST_R = ["vec", "dv0", "dv1", "dv2", "dv3", "dv4"]

        def rstd_from_ss(ss_ap, out_ap, key_r, key_w):
            S.op("dve", lambda e: e.tensor_scalar(out=out_ap, in0=ss_ap, scalar1=1.0 / D, scalar2=EPS,
                                                  op0=ALU.mult, op1=ALU.add), reads=[key_r, "stat"], writes=[key_w])
            S.op("act", lambda e: e.activation(out=out_ap, in_=out_ap, func=AF.Ln), reads=[key_w], writes=[key_w])
            S.op("act", lambda e: e.activation(out=out_ap, in_=out_ap, func=AF.Exp, scale=-0.5),
                 reads=[key_w], writes=[key_w])

        S.dma("pool", wq[0], win_r[:, :, 0:128], writes=["wq0"])
        S.dma("pool", wq[1], win_r[:, :, D:D + 128], writes=["wq1"])
        TRB = ((4, 5), (6, 7))

        def tr_mm(b):
            banks = TRB[b]

            def f(e):
                ins = None
                for k in range(8):
                    ins = e.matmul(PS[banks[k // 4]][:, (k % 4) * 128:(k % 4 + 1) * 128],
                                   lhsT=hn[b][:, k * 128:(k + 1) * 128], rhs=ident[:], start=True, stop=True)
                return ins
            S.op("pe", f, reads=["hn%d" % b, "ident"], writes=["PS%d" % banks[0], "PS%d" % banks[1]])

        def tr_evac(b, j):
            banks = TRB[b]
            S.op("act", lambda e: e.activation(out=hT[:, 0:4, j * 128:(j + 1) * 128],
                                               in_=PS[banks[0]][:].rearrange("p (k t) -> p k t", k=4), func=AF.Copy),
                 reads=["PS%d" % banks[0]], writes=["hT%d" % j])
            S.op("act", lambda e: e.activation(out=hT[:, 4:8, j * 128:(j + 1) * 128],
                                               in_=PS[banks[1]][:].rearrange("p (k t) -> p k t", k=4), func=AF.Copy),
                 reads=["PS%d" % banks[1]], writes=["hT%d" % j])

        for j in range(NT):
            S.dma("sp" if j % 2 == 0 else "act", xres[:, j, :], x_d[j * 128:(j + 1) * 128, :], writes=["xt0_%d" % j])
        junk0 = wo[:, 0:1024]

        def P0A(g):
            for j in range(g * 4, g * 4 + 4):
                S.op("act", lambda e, j=j: e.activation(out=junk0, in_=xres[:, j, :], func=AF.Square,
                                                        accum_out=stat[:, 0, j:j + 1]),
                     reads=["xt0_%d" % j, "stat"], writes=["p0ss%d" % g])
            rstd_from_ss(stat[:, 0, g * 4:g * 4 + 4], stat[:, 1, g * 4:g * 4 + 4], "p0ss%d" % g, "p0rs%d" % g)

        def P0T1(j):
            b = j % 2
            S.op("dve", lambda e: e.scalar_tensor_tensor(out=hn[b], in0=xres[:, j, :], scalar=stat[:, 1, j:j + 1], in1=gbc,
                                                         op0=ALU.mult, op1=ALU.mult),
                 reads=["xt0_%d" % j, "p0rs%d" % (j // 4), "gbc"], writes=["hn%d" % b])
            tr_mm(b)

        def P0T2(j):
            tr_evac(j % 2, j)
        P0A(0)
        P0A(1)
        for j in range(NT + 1):
            if j < NT:
                P0T1(j)
            if j >= 1:
                P0T2(j - 1)
            if j < NT and j % 4 == 3 and j // 4 + 2 < 4:
                P0A(j // 4 + 2)

        S.barrier()
        S.op("dve", lambda e: e.memset(XL[:, 0:8], 0.0), writes=["XLhalo"])

        def hT_keys(tt):
            return ["hT%d" % j for j in range(tt * 4, tt * 4 + 4)]

        wslot = [0]

        def load_w_cols(col_specs):
            s = wslot[0] % 2
            wslot[0] += 1
            for (c0, n, d0) in col_specs:
                S.dma("pool", wst[s][:, :, d0:d0 + n], win_r[:, :, c0:c0 + n], writes=["wst%d" % s])
            return s

        def proj(ps_ap, s, d0, tt, m=128):
            def f(e):
                ins = None
                for k in range(8):
                    ins = e.matmul(ps_ap, lhsT=wst[s][:, k, d0:d0 + m], rhs=hT[:, k, tt * 512:(tt + 1) * 512],
                                   start=(k == 0), stop=(k == 7))
                return ins
            return f

        for e4 in range(0, 2):
            S.dma("pool", wov[:, e4 * 4:(e4 + 1) * 4, :], wout_r[:, e4 * 4:(e4 + 1) * 4, :], writes=["wo_lo"])

        C_GELU = 0.7978845608028654
        steps = [(c, tt) for c in range(8) for tt in range(NTT)]
        slots = {}

        def lru_load(c):
            slots[c] = load_w_cols([(c * 128, 128, 0), (D + c * 128, 128, 128)])

        XBS, GBS, ZA, ZX, SSB, PJ = (0, 0), (1, 2, 3, 7), 4, 5, 6, 6

        def projw(ps_ap, w3, tt):
            def f(e):
                ins = None
                for k in range(8):
                    ins = e.matmul(ps_ap, lhsT=w3[:, k, :], rhs=hT[:, k, tt * 512:(tt + 1) * 512],
                                   start=(k == 0), stop=(k == 7))
                return ins
            return f

        def S0(n):
            c, tt = steps[n]
            gb = GBS[n % 4]
            XB = XBS[n % 2]
            if c == 0:
                S.op("pe", projw(PS[XB][:], wq[0], tt), reads=["wq0"] + hT_keys(tt), writes=["PS%d" % XB])
                S.op("pe", projw(PS[gb][:], wq[1], tt), reads=["wq1"] + hT_keys(tt), writes=["PS%d" % gb])
                return
            s_ = slots[c]
            S.op("pe", proj(PS[XB][:], s_, 0, tt), reads=["wst%d" % s_] + hT_keys(tt), writes=["PS%d" % XB])
            S.op("pe", proj(PS[gb][:], s_, 128, tt), reads=["wst%d" % s_] + hT_keys(tt), writes=["PS%d" % gb])

        def S1a(n):
            c, tt = steps[n]
            b = n % 2
            XB = XBS[n % 2]
            xps, gps = PS[XB], PS[GBS[n % 4]]
            kx, kg = "PS%d" % XB, "PS%d" % GBS[n % 4]
            t0 = 8 + tt * 512
            xc, ut = tmp[("xc", b)], tmp[("ut", b)]
            kxc, kut = "xc%d" % b, "ut%d" % b
            S.op("act", lambda e: e.activation(out=XL[:, t0:t0 + 512], in_=xps[:], func=AF.Copy),
                 reads=[kx], writes=["XL%d" % tt])
            S.op("act", lambda e: e.activation(out=xc, in_=xps[:], func=AF.Identity,
                                               bias=vec[:, V_CB, c:c + 1], scale=vec[:, V_CW3, c:c + 1]),
                 reads=[kx] + CONST_R, writes=[kxc])
            S.op("act", lambda e: e.activation(out=ut, in_=gps[:], func=AF.Square, scale=0.21145921),
                 reads=[kg], writes=[kut])
            xlk = ["XL%d" % tt, "XLhalo"] + (["XL%d" % (tt - 1)] if tt > 0 else [])
            for kk, sh in ((V_CW2, 1), (V_CW1, 2), (V_CW0, 3)):
                S.op("dve", lambda e, kk=kk, sh=sh: e.scalar_tensor_tensor(
                    out=xc, in0=XL[:, t0 - sh:t0 - sh + 512], scalar=vec[:, kk, c:c + 1], in1=xc,
                    op0=ALU.mult, op1=ALU.add), reads=xlk + [kxc] + CONST_R, writes=[kxc])
            S.op("dve", lambda e: e.scalar_tensor_tensor(out=ut, in0=ut, scalar=1.0, in1=gps[:], op0=ALU.add, op1=ALU.mult),
                 reads=[kut, kg], writes=[kut])

        def S1b(n):
            c, tt = steps[n]
            b = n % 2
            xc = tmp[("xc", b)]
            S.op("act", lambda e: e.activation(out=xcb[b], in_=xc, func=AF.Copy), reads=["xc%d" % b], writes=["xcb%d" % b])
            S.op("pe", lambda e: e.matmul(PS[ZA][:], lhsT=wg_a[:, c, :], rhs=xcb[b], start=True, stop=True),
                 reads=["wg_a", "xcb%d" % b], writes=["PS%d" % ZA])
            S.op("pe", lambda e: e.matmul(PS[ZX][:], lhsT=wg_x[:, c, :], rhs=xcb[b], start=True, stop=True),
                 reads=["wg_x", "xcb%d" % b], writes=["PS%d" % ZX])

        def S2(n):
            c, tt = steps[n]
            b = n % 2
            gps, kg = PS[GBS[n % 4]], "PS%d" % GBS[n % 4]
            sl = slice(tt * 512, (tt + 1) * 512)
            xc, ut, tr, ti = tmp[("xc", b)], tmp[("ut", b)], tmp[("tr", 0)], tmp[("ti", 0)]
            A, W, GX, GG = CH["A"][:, sl], CH["W"][:, sl], CH["GX"][:, sl], CH["GG"][:, sl]
            kA, kW, kGX, kGG = "A%d" % tt, "W%d" % tt, "GX%d" % tt, "GG%d" % tt
            S.op("act", lambda e: e.activation(out=tr, in_=PS[ZA][:], func=AF.Tanh, bias=dv[:, 0, c:c + 1], scale=0.5),
                 reads=["PS%d" % ZA] + CONST_R, writes=["tr"])
            S.op("act", lambda e: e.activation(out=ti, in_=PS[ZX][:], func=AF.Tanh, bias=dv[:, 1, c:c + 1], scale=0.5),
                 reads=["PS%d" % ZX] + CONST_R, writes=["ti"])
            S.op("act", lambda e: e.activation(out=ut, in_=ut, func=AF.Tanh, scale=C_GELU), reads=["ut%d" % b], writes=["ut%d" % b])
            S.op("act", lambda e: e.activation(out=A, in_=tr, func=AF.Exp, bias=dv[:, 2, c:c + 1], scale=dv[:, 2, c:c + 1]),
                 reads=["tr"] + CONST_R, writes=[kA])
            S.op("dve", lambda e: e.scalar_tensor_tensor(out=GX, in0=ti, scalar=1.0, in1=xc, op0=ALU.add, op1=ALU.mult),
                 reads=["ti", "xc%d" % b], writes=[kGX])
            S.op("dve", lambda e: e.scalar_tensor_tensor(out=W, in0=A, scalar=-1.0, in1=A, op0=ALU.mult, op1=ALU.mult),
                 reads=[kA], writes=[kW])
            S.op("dve", lambda e: e.scalar_tensor_tensor(out=GG, in0=ut, scalar=1.0, in1=gps[:], op0=ALU.add, op1=ALU.mult),
                 reads=["ut%d" % b, kg], writes=[kGG])

        def LNEXP(c):
            keys = ["W%d" % tt for tt in range(NTT)]
            S.op("act", lambda e: e.activation(out=CH["W"], in_=CH["W"], func=AF.Sqrt, bias=1.0, scale=1.0),
                 reads=keys, writes=keys)

        def S3(n):
            c, tt = steps[n]
            b = n % 2
            sl = slice(tt * 512, (tt + 1) * 512)
            A, W, GX, GG = CH["A"][:, sl], CH["W"][:, sl], CH["GX"][:, sl], CH["GG"][:, sl]
            kA, kW, kGX, kGG = "A%d" % tt, "W%d" % tt, "GX%d" % tt, "GG%d" % tt
            hh = tmp[("hh", b)]
            S.op("dve", lambda e: e.tensor_tensor(out=GX, in0=GX, in1=W, op=ALU.mult), reads=[kGX, kW], writes=[kGX])
            init = 0.0 if tt == 0 else tmp[("hh", 1 - b)][:, 511:512]
            S.op("dve", lambda e: e.tensor_tensor_scan(out=hh, data0=A, data1=GX, initial=init, op0=ALU.mult, op1=ALU.add),
                 reads=[kA, kGX, "hh%d" % (1 - b)], writes=["hh%d" % b])
            S.op("dve", lambda e: e.scalar_tensor_tensor(out=mixv[:, c, tt * 512:(tt + 1) * 512], in0=hh, scalar=0.25, in1=GG,
                                                         op0=ALU.mult, op1=ALU.mult),
                 reads=["hh%d" % b, kGG], writes=["mix_%d_%d" % (c, tt)])

        def S4(n):
            c, tt = steps[n]
            b = n % 2
            hh = tmp[("hh", b)]
            S.op("act", lambda e: e.activation(out=ysq[b], in_=mixv[:, c, tt * 512:(tt + 1) * 512], func=AF.Square),
                 reads=["mix_%d_%d" % (c, tt)], writes=["ysq%d" % b])

            def ssmm(e):
                ins = None
                for t4 in range(4):
                    ins = e.matmul(PS[SSB][:, t4:t4 + 1], lhsT=ysq[b][:, t4 * 128:(t4 + 1) * 128], rhs=onec[:, 0:1],
                                   start=True, stop=True)
                return ins
            S.op("pe", ssmm, reads=["onec", "ysq%d" % b], writes=["PS%d" % SSB])
            dst = stat[:, 2, tt * 4:(tt + 1) * 4]
            S.op("dve", lambda e: e.tensor_tensor(out=dst, in0=PS[SSB][:, 0:4], in1=dst, op=ALU.add),
                 reads=["PS%d" % SSB, "ssl%d" % tt, "stat"], writes=["ssl%d" % tt])

        s3 = 2 * D + D
        s4 = s3 + 128
        xjobs = []
        wqslot = [0]

        def xload(col_specs):
            sq = wqslot[0] % 2
            wqslot[0] += 1

            def f():
                for (c0, n_, d0) in col_specs:
                    S.dma("pool", wq[sq][:, :, d0:d0 + n_], win_r[:, :, c0:c0 + n_], writes=["wq%d" % sq])
            return sq, f

        def xproj_T(sq, tt, dst_ap, dst_key, scale):
            def pe_f(bank):
                def mm(e):
                    ins = None
                    for k in range(8):
                        ins = e.matmul(PS[bank][:], lhsT=wq[sq][:, k, :], rhs=hT[:, k, tt * 512:(tt + 1) * 512],
                                       start=(k == 0), stop=(k == 7))
                    return ins
                S.op("pe", mm, reads=["wq%d" % sq] + hT_keys(tt), writes=["PS%d" % bank])

            def act_f(bank):
                S.op("act", lambda e: e.activation(out=dst_ap, in_=PS[bank][:], func=AF.Copy, scale=scale),
                     reads=["PS%d" % bank], writes=[dst_key])
            return pe_f, act_f

        def xproj_V(sq, j4):
            def pe_f(bank):
                def mm(e):
                    ins = None
                    for jj in range(4):
                        j = j4 * 4 + jj
                        for k in range(8):
                            ins = e.matmul(PS[bank][:, jj * 128:(jj + 1) * 128], lhsT=hT[:, k, j * 128:(j + 1) * 128],
                                           rhs=wq[sq][:, k, :], start=(k == 0), stop=(k == 7))
                    return ins
                S.op("pe", mm, reads=["wq%d" % sq] + hT_keys(j4), writes=["PS%d" % bank])

            def act_f(bank):
                S.op("act", lambda e: e.activation(
                    out=vaug[:, j4 * 4:(j4 + 1) * 4, :, :].rearrange("p j h m -> p j (h m)"),
                    in_=PS[bank][:].rearrange("p (j f) -> p j f", j=4), func=AF.Copy),
                    reads=["PS%d" % bank], writes=["vaug"])
            return pe_f, act_f

        for h in range(2):
            sq, f = xload([(s3 + h * 64, 64, 0), (s3 + h * 64, 64, 64)])
            xjobs.append(("load", f))
            for tt in range(NTT):
                xjobs.append(("proj", xproj_T(sq, tt, kTd[:, h, tt * 512:(tt + 1) * 512], "kTd", 1.0)))
        sq, f = xload([(s4, 128, 0)])
        xjobs.append(("load", f))
        for j4 in range(4):
            xjobs.append(("proj", xproj_V(sq, j4)))
        for c in range(8):
            sq, f = xload([(2 * D + c * 128, 128, 0)])
            xjobs.append(("load", f))
            for tt in range(NTT):
                xjobs.append(("proj", xproj_T(sq, tt, qT[:, c, tt * 512:(tt + 1) * 512], "qT%d" % c, 0.125)))

        xstate = {"i": 0, "pend": None}

        def xjob_act():
            if xstate["pend"] is not None:
                act_f, bank = xstate["pend"]
                act_f(bank)
                xstate["pend"] = None

        def xjob_pe(bank):
            while xstate["i"] < len(xjobs):
                kind, job = xjobs[xstate["i"]]
                xstate["i"] += 1
                if kind == "load":
                    job()
                    continue
                pe_f, act_f = job
                pe_f(bank)
                xstate["pend"] = (act_f, bank)
                return

        NS = len(steps)
        LAG3, LAG4 = 6, 7
        ATT_START = NS + 1

        def lru_emit(att_step):
            lru_load(1)
            lru_load(2)
            S0(0)
            for s_ in range(NS + LAG4 + 1):
                if s_ < ATT_START:
                    xjob_act()
                if s_ >= LAG3 and (s_ - LAG3) % 4 == 0 and (s_ - LAG3) // 4 < 8:
                    LNEXP((s_ - LAG3) // 4)
                if 0 <= s_ - LAG3 < NS:
                    S3(s_ - LAG3)
                if 0 <= s_ - LAG4 < NS:
                    S4(s_ - LAG4)
                if 0 <= s_ - 2 < NS:
                    S2(s_ - 2)
                if 3 <= s_ < ATT_START - 1:
                    xjob_pe(PJ)
                if 0 <= s_ - 1 < NS:
                    S1b(s_ - 1)
                if s_ < NS:
                    S1a(s_)
                if s_ + 1 < NS:
                    S0(s_ + 1)
                if s_ < NS:
                    c, tt = steps[s_]
                    if tt == 3 and c >= 1 and c + 2 < 8:
                        lru_load(c + 2)
                if s_ >= ATT_START:
                    att_step()
                    att_step()

        onesw = SB("onesw", [128, 64], BF16)
        S.op("dve", lambda e: e.memset(onesw[:], 1.0), writes=["onesw"])
        kv = {"k": kTd, "v": vaug, "kk": "kTd", "vk": "vaug"}
        its = [(c, tt, m2) for c in range(8) for tt in range(NTT) for m2 in range(2)]
        NI = len(its)
        maskb = mask[:].rearrange("p k q -> p (k q)").unsqueeze(1).broadcast_to([128, 4, 256])

        def AQ(i):
            c, tt, m2 = its[i]
            h = c // 4
            pb = i % 2
            n0 = tt * 4 + m2 * 2

            def qk(e):
                ins = None
                segs = [(max(n0 - 1, 0), 0, n0 * 128, 128), (n0, 128, n0 * 128, 256), (n0 + 1, 384, (n0 + 1) * 128, 128)]
                for (kblk, col, q0, nq) in segs:
                    for ee in range(2):
                        ins = e.matmul(PS[pb * 2 + ee][:, col:col + nq],
                                       lhsT=kv["k"][ee * 64:(ee + 1) * 64, h, kblk * 128:(kblk + 1) * 128],
                                       rhs=qT[ee * 64:(ee + 1) * 64, c, q0:q0 + nq], start=True, stop=True)
                return ins
            S.op("pe", qk, reads=[kv["kk"], "qT%d" % c, "qblk%d" % i], writes=["PS%d" % (pb * 2), "PS%d" % (pb * 2 + 1)])
            for ee in range(2):
                S.op("act", lambda e, ee=ee: e.activation(out=ptb[pb][:, ee, :], in_=PS[pb * 2 + ee][:], func=AF.Exp),
                     reads=["PS%d" % (pb * 2 + ee)], writes=["pt%d_%d" % (pb, ee)])
            ptv = ptb[pb][:].rearrange("p e (n f) -> p (e n) f", n=2)
            S.op("dve", lambda e: e.tensor_tensor(out=ptv, in0=ptv, in1=maskb, op=ALU.mult),
                 reads=["pt%d_0" % pb, "pt%d_1" % pb, "mask"], writes=["pt%d_0" % pb, "pt%d_1" % pb])

        def AV(i):
            c, tt, m2 = its[i]
            h = c // 4
            pb = i % 2
            od = PS[4 + pb]
            n0 = tt * 4 + m2 * 2

            def pv(e):
                ins = None
                merged_den = n0 > 0
                if merged_den:
                    for ee in range(2):
                        o_ee = od[ee * 64:(ee + 1) * 64, 0:256]
                        e.matmul(o_ee[:, 0:128], lhsT=kv["v"][:, n0 - 1, h, :], rhs=ptb[pb][:, ee, 0:128], start=True, stop=False,
                                 skip_group_check=True)
                        e.matmul(o_ee[:, 0:256], lhsT=kv["v"][:, n0, h, :], rhs=ptb[pb][:, ee, 128:384], start=False, stop=False,
                                 skip_group_check=True)
                        e.matmul(o_ee[:, 128:256], lhsT=kv["v"][:, n0 + 1, h, :], rhs=ptb[pb][:, ee, 384:512],
                                 start=False, stop=True, skip_group_check=True)
                for part in range(0 if merged_den else 2):
                    for nn in range(2):
                        n = n0 + nn
                        col = part * 256 + nn * 128
                        for ee in range(2):
                            kbs = [1] if n == 0 else [0, 1]
                            for idx, kb in enumerate(kbs):
                                kblk = n - 1 + kb
                                rhs = ptb[pb][:, ee, (nn * 2 + kb) * 128:(nn * 2 + kb + 1) * 128]
                                lhsT = kv["v"][:, kblk, h, :] if part == 0 else onesw[:, :]
                                ins = e.matmul(od[ee * 64:(ee + 1) * 64, col:col + 128], lhsT=lhsT, rhs=rhs,
                                               start=(idx == 0), stop=(idx == len(kbs) - 1))
                if merged_den:
                    for ee in range(2):
                        pt4 = ptb[pb][:, ee, :].rearrange("p (n k q) -> p n k q", n=2, k=2)
                        for kb in range(2):
                            ins = e.matmul(od[ee * 64:(ee + 1) * 64, 256:512].rearrange("p (n q) -> p n q", n=2),
                                           lhsT=onesw[:, :], rhs=pt4[:, :, kb, :], start=(kb == 0), stop=(kb == 1))
                return ins
            S.op("pe", pv, reads=[kv["vk"], "onesw", "pt%d_0" % pb, "pt%d_1" % pb], writes=["PS%d" % (4 + pb)])

        def AN1(i):
            c, tt, m2 = its[i]
            pb = i % 2
            od = PS[4 + pb]
            rd = rden[pb]
            ya = rd
            S.op("act", lambda e: e.activation(out=rd, in_=od[:, 256:512], func=AF.Ln, bias=dv[:, 4, c:c + 1]),
                 reads=["PS%d" % (4 + pb)] + CONST_R, writes=["rden%d" % pb])
            S.op("act", lambda e: e.activation(out=rd, in_=rd, func=AF.Exp, scale=-1.0),
                 reads=["rden%d" % pb], writes=["rden%d" % pb])
            cs = slice(tt * 512 + m2 * 256, tt * 512 + m2 * 256 + 256)
            S.op("dve", lambda e: e.tensor_tensor(out=mixv[:, 8 + c, cs], in0=od[:, 0:256], in1=rd, op=ALU.mult),
                 reads=["PS%d" % (4 + pb), "rden%d" % pb], writes=["qblk%d" % i])

        def AN2(i):
            c, tt, m2 = its[i]
            pb = i % 2
            cs = slice(tt * 512 + m2 * 256, tt * 512 + m2 * 256 + 256)
            ya, yb = rden[pb], yab[pb]
            S.op("act", lambda e: e.activation(out=yb, in_=mixv[:, 8 + c, cs], func=AF.Square),
                 reads=["qblk%d" % i], writes=["yab%d" % pb])

            def ssmm2(e):
                ins = None
                for t2 in range(2):
                    ins = e.matmul(PS[6][:, t2:t2 + 1], lhsT=yb[:, t2 * 128:(t2 + 1) * 128], rhs=onec[:, 0:1],
                                   start=True, stop=True)
                return ins
            S.op("pe", ssmm2, reads=["onec", "yab%d" % pb], writes=["PS6"])
            dst = stat[:, 3, tt * 4 + m2 * 2:tt * 4 + m2 * 2 + 2]
            S.op("dve", lambda e: e.tensor_tensor(out=dst, in0=PS[6][:, 0:2], in1=dst, op=ALU.add),
                 reads=["PS6", "ssa%d_%d" % (tt, m2), "stat"], writes=["ssa%d_%d" % (tt, m2)])

        mixw = mix[:].rearrange("p (s w f) -> p s w f", s=2, w=2)

        def mlp_load(q):
            sl = q % 2
            wup_s = mixw[:, sl, 0, :].rearrange("p (k f) -> p k f", k=8)
            wdn_s = mixw[:, sl, 1, :].rearrange("p (k d) -> p k d", k=8)
            for k2 in range(2):
                S.dma("pool", wup_s[:, k2 * 4:(k2 + 1) * 4, :], wup_r[:, k2 * 4:(k2 + 1) * 4, q * 1024:(q + 1) * 1024],
                      writes=["wup%d" % sl, "mixhalf%d" % sl])
                S.dma("pool", wdn_s[:, k2 * 4:(k2 + 1) * 4, :], wdn_r[:, q * 8 + k2 * 4:q * 8 + (k2 + 1) * 4, :],
                      writes=["wdn%d" % sl, "mixhalf%d" % sl])

        att_state = {"step": 0, "lru_done": False, "h": 0, "pend": None, "started": False}

        def wol_start():
            S.wait_keys("dve", ["ssl%d" % t_ for t_ in range(NTT)])
            rstd_from_ss(stat[:, 2, :], stat[:, 2, :], "ssl_all", "rl")
            for k in range(8):
                S.op("dve", lambda e, k=k: e.tensor_scalar(out=wov[:, k, :], in0=wov[:, k, :], scalar1=vec[:, V_GL, k:k + 1],
                                                           scalar2=None, op0=ALU.mult),
                     reads=["wo_lo"] + CONST_R, writes=["wo_s%d" % k])
            hkeys = ["hT%d" % j_ for j_ in range(NT)]
            S.wait_readers("dve", hkeys)
            S.op("dve", lambda e: e.tensor_copy(out=hT[:, 0:2, :], in_=kTd), reads=["kTd"], writes=["kTd2"])
            S.op("dve", lambda e: e.tensor_copy(out=hT[:, 2, :], in_=wo[:, WH + 2 * T:WH + 2 * T + NT * 128]),
                 reads=["vaug"], writes=["vaug2"])
            kv["k"], kv["kk"] = hT[:, 0:2, :], "kTd2"
            kv["v"], kv["vk"] = hT[:, 2, :].rearrange("p (j h m) -> p j h m", j=NT, h=2), "vaug2"
            S._wait("sp", S._deps((), list(S.state.keys())))
            for j in range(NT):
                S.dma("sp", xres[:, j, :], x_d[j * 128:(j + 1) * 128, :], writes=["xres%d" % j])
            for e4 in range(2, 4):
                S.dma("pool", wov[:, e4 * 4:(e4 + 1) * 4, :], wout_r[:, e4 * 4:(e4 + 1) * 4, :],
                      writes=["wo_hi", "kTd", "vaug", "wq0", "wq1"])

        def wol_dve():
            if att_state["pend"] is not None:
                h = att_state["pend"]
                j, hf = h // 2, h % 2
                S.op("dve", lambda e: e.scalar_tensor_tensor(
                    out=xres[:, j, hf * 512:(hf + 1) * 512], in0=PS[7][:], scalar=stat[:, 2, j:j + 1],
                    in1=xres[:, j, hf * 512:(hf + 1) * 512], op0=ALU.mult, op1=ALU.add),
                    reads=["PS7", "rl", "xres%d" % j], writes=["xres%d" % j])
                att_state["pend"] = None
                if h == 2 * NT - 1:
                    mlp_load(0)

        def wol_pe():
            h = att_state["h"]
            if h >= 2 * NT:
                return
            att_state["h"] += 1
            j, hf = h // 2, h % 2

            def mm(e):
                ins = None
                for k in range(8):
                    ins = e.matmul(PS[7][:], lhsT=mixv[:, k, j * 128:(j + 1) * 128], rhs=wov[:, k, hf * 512:(hf + 1) * 512],
                                   start=(k == 0), stop=(k == 7))
                return ins
            S.op("pe", mm, reads=["wo_s%d" % k for k in range(8)] + ["mixhalf0"], writes=["PS7"])
            att_state["pend"] = h

        def att_step():
            step = att_state["step"]
            if step >= NI + 3:
                return
            att_state["step"] += 1
            if step == 0:
                for eng_ in ("act", "dve", "pool"):
                    S.wait_readers(eng_, ["wst0", "wst1"])
            xjobs_done = xstate["i"] >= len(xjobs) and xstate["pend"] is None
            if att_state["lru_done"] and xjobs_done and not att_state["started"]:
                att_state["started"] = True
                wol_start()
            if att_state["started"]:
                wol_dve()
            xjob_act()
            if 0 <= step - 3 < NI:
                AN2(step - 3)
            if 0 <= step - 2 < NI:
                AN1(step - 2)
            if step < NI:
                AQ(step)
            if 0 <= step - 1 < NI:
                AV(step - 1)
            xjob_pe(7)
            if att_state["started"]:
                wol_pe()

        lru_emit(att_step)
        att_state["lru_done"] = True
        while att_state["step"] < NI + 3:
            att_step()
        assert xstate["i"] >= len(xjobs) and xstate["pend"] is None
        assert att_state["started"]
        while att_state["h"] < 2 * NT or att_state["pend"] is not None:
            wol_dve()
            wol_pe()
        S.barrier()
        S.dma("sp", gbc, gmlp_d.partition_broadcast(128), writes=["gbc"])
        rstd_from_ss(stat[:, 3, :], stat[:, 3, :], "ssrow_r", "rla")
        for k in range(8, 16):
            S.op("dve", lambda e, k=k: e.tensor_scalar(out=wov[:, k, :], in0=wov[:, k, :], scalar1=vec[:, V_GA, k - 8:k - 7],
                                                       scalar2=None, op0=ALU.mult),
                 reads=["wo_hi"] + CONST_R, writes=["wo_s%d" % k])
        NG = NT

        def PA(g):
            br, j = 1, g
            bk = (g % 2) * 2

            def wo_mm(e):
                ins = None
                for hf in range(2):
                    for k in range(8):
                        ins = e.matmul(PS[bk + hf][:], lhsT=mixv[:, br * 8 + k, j * 128:(j + 1) * 128],
                                       rhs=wov[:, br * 8 + k, hf * 512:(hf + 1) * 512], start=(k == 0), stop=(k == 7))
                return ins
            S.op("pe", wo_mm, reads=["wo_s%d" % (br * 8 + k) for k in range(8)] + ["mixhalf%d" % br], writes=["PS%d" % bk, "PS%d" % (bk + 1)])

        def PB(g):
            br, j = 1, g
            bk = (g % 2) * 2
            for hf in range(2):
                S.op("dve", lambda e, hf=hf: e.scalar_tensor_tensor(
                    out=xres[:, j, hf * 512:(hf + 1) * 512], in0=PS[bk + hf][:], scalar=stat[:, 2 + br, j:j + 1],
                    in1=xres[:, j, hf * 512:(hf + 1) * 512], op0=ALU.mult, op1=ALU.add),
                    reads=["PS%d" % (bk + hf), "rla", "xres%d" % j], writes=["xres%d" % j])

        def PC1a(j):
            b = j % 2
            S.op("act", lambda e: e.activation(out=hn[b], in_=xres[:, j, :], func=AF.Square, accum_out=stat[:, 0, j:j + 1]),
                 reads=["xres%d" % j, "stat"], writes=["hn%d" % b, "p2ss%d" % j])

        def PC1c(j):
            rstd_from_ss(stat[:, 0, j:j + 1], stat[:, 1, j:j + 1], "p2ss%d" % j, "p2rs%d" % j)

        def PC1b(j):
            b = j % 2
            S.op("dve", lambda e: e.scalar_tensor_tensor(out=hn[b], in0=xres[:, j, :], scalar=stat[:, 1, j:j + 1], in1=gbc,
                                                         op0=ALU.mult, op1=ALU.mult),
                 reads=["xres%d" % j, "p2rs%d" % j, "gbc"], writes=["hn%d" % b])

        def PC2(j):
            b = j % 2
            tr_mm(b)
            tr_evac(b, j)

        for g in range(NG + 4):
            if 0 <= g - 3 < NG:
                PC2(g - 3)
            if g < NG:
                PA(g)
            if 0 <= g - 1 < NG:
                PB(g - 1)
                PC1a(g - 1)
            if 0 <= g - 2 < NG:
                PC1b(g - 2)
            if 0 <= g - 1 < NG:
                PC1c(g - 1)

        S.barrier()
        S.dma("sp", gbc, gfin_d.partition_broadcast(128), writes=["gbc"])
        actb = wo[:].rearrange("p (s f) -> p s f", s=2)
        NQ = 4
        units = [(q, tt) for q in range(NQ) for tt in range(NTT)]

        def wviews(q):
            sl = q % 2
            return (sl, mixw[:, sl, 0, :].rearrange("p (k f) -> p k f", k=8),
                    mixw[:, sl, 1, :].rearrange("p (k d) -> p k d", k=8))

        def MUP(u):
            q, tt = units[u]
            sl, wup_s, wdn_s = wviews(q)
            ab = u % 2
            act_s = actb[:, ab, 0:4096].rearrange("p (c t) -> p c t", c=8)
            for fc in range(8):
                ub = fc % 2

                def up(e, fc=fc, ub=ub):
                    ins = None
                    for k in range(8):
                        ins = e.matmul(PS[ub][:], lhsT=wup_s[:, k, fc * 128:(fc + 1) * 128],
                                       rhs=hT[:, k, tt * 512:(tt + 1) * 512], start=(k == 0), stop=(k == 7))
                    return ins
                S.op("pe", up, reads=["wup%d" % sl] + hT_keys(tt), writes=["PS%d" % ub])
                S.op("act", lambda e, ub=ub: e.activation(out=hn[ub].bitcast(F32), in_=PS[ub][:], func=AF.Relu),
                     reads=["PS%d" % ub], writes=["relu%d" % ub])
                S.op("dve", lambda e, ub=ub, fc=fc: e.tensor_tensor(out=act_s[:, fc, :], in0=hn[ub].bitcast(F32),
                                                                    in1=hn[ub].bitcast(F32), op=ALU.mult),
                     reads=["relu%d" % ub], writes=["act%d_%d" % (ab, fc)])

        def MDN(u):
            q, tt = units[u]
            sl, wup_s, wdn_s = wviews(q)
            ab = u % 2
            act_s = actb[:, ab, 0:4096].rearrange("p (c t) -> p c t", c=8)
            for t4 in range(4):
                j = tt * 4 + t4
                db = 2 + (t4 % 2) * 2

                def dn(e, t4=t4, db=db):
                    ins = None
                    for hf in range(2):
                        for fc in range(8):
                            ins = e.matmul(PS[db + hf][:], lhsT=act_s[:, fc, t4 * 128:(t4 + 1) * 128],
                                           rhs=wdn_s[:, fc, hf * 512:(hf + 1) * 512], start=(fc == 0), stop=(fc == 7))
                    return ins
                S.op("pe", dn, reads=["wdn%d" % sl] + ["act%d_%d" % (ab, fc) for fc in range(8)],
                     writes=["PS%d" % db, "PS%d" % (db + 1)])
                for hf in range(2):
                    S.op("dve", lambda e, hf=hf, db=db, j=j: e.tensor_tensor(
                        out=xres[:, j, hf * 512:(hf + 1) * 512], in0=PS[db + hf][:],
                        in1=xres[:, j, hf * 512:(hf + 1) * 512], op=ALU.add),
                        reads=["PS%d" % (db + hf), "xres%d" % j], writes=["xres%d" % j])
                if q == NQ - 1:
                    fj = wo[:, 4096:5120]
                    S.op("act", lambda e, j=j: e.activation(out=fj, in_=xres[:, j, :], func=AF.Square,
                                                            accum_out=stat[:, 0, j:j + 1]),
                         reads=["xres%d" % j, "stat"], writes=["fss%d" % j])
                    rstd_from_ss(stat[:, 0, j:j + 1], stat[:, 1, j:j + 1], "fss%d" % j, "frs%d" % j)
                    S.op("dve", lambda e, j=j: e.scalar_tensor_tensor(
                        out=xres[:, j, :], in0=xres[:, j, :], scalar=stat[:, 1, j:j + 1], in1=gbc,
                        op0=ALU.mult, op1=ALU.mult),
                        reads=["xres%d" % j, "frs%d" % j, "gbc"], writes=["xres%d" % j])
                    S.dma("sp", out_d[j * 128:(j + 1) * 128, :], xres[:, j, :], reads=["xres%d" % j],
                          writes=["out%d" % j])

        NU = len(units)
        mlp_load(1)
        MUP(0)
        for u in range(NU):
            if u + 1 < NU:
                MUP(u + 1)
            MDN(u)
            q, tt = units[u]
            if tt == NTT - 1 and q + 2 < NQ:
                mlp_load(q + 2)
        S.wait_keys("sp", ["out%d" % j for j in range(NT)])
    return nc


def _pack_vecs(inp):
    fm = lambda v: np.ascontiguousarray(np.asarray(v, np.float32).reshape(8, 128).T)
    cw = np.asarray(inp["conv_w"], np.float32)[0]
    vs = [fm(cw[0]), fm(cw[1]), fm(cw[2]), fm(cw[3]), fm(inp["conv_b"][0]), fm(inp["b_gate_a"][0]),
          fm(inp["b_gate_x"][0]), fm(inp["lru_lambda"][0]), fm(inp["lru_out_g"][0]), fm(inp["attn_out_g"][0]),
          fm(np.repeat(np.asarray(inp["attn_sinks"], np.float32)[0], 64))]
    return np.ascontiguousarray(np.stack(vs, axis=1).reshape(128, NV * 8))


_NC_CACHE = {}


def kernel(**inputs):
    x = np.asarray(inputs["x"], np.float32)
    nb = x.shape[0]
    if "nc" not in _NC_CACHE:
        _NC_CACHE["nc"] = build_program()
    nc = _NC_CACHE["nc"]
    f = lambda k: np.ascontiguousarray(np.asarray(inputs[k], np.float32)[0])
    shared = {
        "norm_mix_g": f("norm_mix_g"), "w_in": f("w_in"), "w_gate_a": f("w_gate_a"), "w_gate_x": f("w_gate_x"),
        "vecs": _pack_vecs(inputs), "w_out": f("w_out"), "norm_mlp_g": f("norm_mlp_g"),
        "w_mlp_up": f("w_mlp_up"), "w_mlp_down": f("w_mlp_down"),
        "norm_final_g": np.ascontiguousarray(np.asarray(inputs["norm_final_g"], np.float32)),
    }
    in_maps = [dict(shared, x=np.ascontiguousarray(x[b])) for b in range(nb)]
    res = run_bass_kernel_spmd(nc, in_maps, core_ids=list(range(nb)))
    return np.stack([np.asarray(r["out"], np.float32) for r in res.results], axis=0)
```

```python
import numpy as np
from contextlib import ExitStack
import concourse.bass as bass
import concourse.mybir as mybir
from concourse.bass_utils import run_bass_kernel_spmd

F32 = mybir.dt.float32
BF16 = mybir.dt.bfloat16
AF = mybir.ActivationFunctionType
ALU = mybir.AluOpType

D = 1024
T = 2048
NT = T // 128
NTT = T // 512
INW = 3328
DFF = 4096
EPS = 1e-6
NV = 11
(V_CW0, V_CW1, V_CW2, V_CW3, V_CB, V_BA, V_BX, V_LAM, V_GL, V_GA, V_SINK) = range(NV)


class Sched:
    ENGS = ("pe", "act", "dve", "pool", "sp")

    def __init__(self, nc, stack, n_dma_sems=8):
        self.nc = nc
        self.e = {"pe": nc.tensor, "act": nc.scalar, "dve": nc.vector, "pool": nc.gpsimd, "sp": nc.sync}
        self.sem, self.cnt = {}, {}
        for n in self.ENGS:
            self.sem[n] = stack.enter_context(nc.semaphore("s_" + n))
            self.cnt[n] = 0
        self.dma_pool, self.dma_rr = {}, {}
        for q in ("sp", "pool", "act"):
            lst = []
            for i in range(n_dma_sems):
                nm = "d_%s%d" % (q, i)
                self.sem[nm] = stack.enter_context(nc.semaphore(nm))
                self.cnt[nm] = 0
                lst.append(nm)
            self.dma_pool[q] = lst
            self.dma_rr[q] = 0
        self.clock = {n: {} for n in self.ENGS}
        self.opclock = {}
        self.state = {}

    def _deps(self, reads, writes):
        deps = {}

        def add(d):
            if d is not None and deps.get(d[0], 0) < d[1]:
                deps[d[0]] = d[1]
        for k in reads:
            st = self.state.get(k)
            if st:
                add(st[0])
        for k in writes:
            st = self.state.get(k)
            if st:
                add(st[0])
                for r in st[1]:
                    add(r)
        return deps

    def _wait(self, eng, deps):
        ck = self.clock[eng]
        for p, n in deps.items():
            if ck.get(p, 0) >= n:
                continue
            self.e[eng].wait_ge(self.sem[p], n)
            oc = self.opclock.get((p, n))
            if oc:
                for q, m in oc.items():
                    if ck.get(q, 0) < m:
                        ck[q] = m
            ck[p] = n

    def _record(self, opid, reads, writes):
        for k in reads:
            self.state.setdefault(k, [None, []])[1].append(opid)
        for k in writes:
            self.state[k] = [opid, []]

    def op(self, eng, fn, reads=(), writes=()):
        self._wait(eng, self._deps(reads, writes))
        ins = fn(self.e[eng])
        self.cnt[eng] += 1
        ins.then_inc(self.sem[eng], 1)
        opid = (eng, self.cnt[eng])
        self.opclock[opid] = dict(self.clock[eng])
        self._record(opid, reads, writes)

    def dma(self, q, out, in_, reads=(), writes=()):
        pool = self.dma_pool[q]
        s = pool[self.dma_rr[q] % len(pool)]
        self.dma_rr[q] += 1
        deps = self._deps(reads, writes)
        if self.cnt[s] > 0 and deps.get(s, 0) < self.cnt[s]:
            deps[s] = self.cnt[s]
        self._wait(q, deps)
        ins = self.e[q].dma_start(out=out, in_=in_)
        self.cnt[s] += 16
        ins.then_inc(self.sem[s], 16)
        opid = (s, self.cnt[s])
        self.opclock[opid] = dict(self.clock[q])
        self._record(opid, reads, writes)

    def wait_keys(self, eng, keys):
        self._wait(eng, self._deps(keys, ()))

    def wait_readers(self, eng, keys):
        self._wait(eng, self._deps((), keys))

    def barrier(self):
        allk = list(self.state.keys())
        deps = self._deps((), allk)
        for eng in self.ENGS:
            self._wait(eng, dict(deps))


def build_program(t_tokens=T):
    assert t_tokens == T
    nc = bass.Bass("TRN2", target_bir_lowering=False)
    dt_in = lambda name, shape: nc.dram_tensor(name, shape, F32, kind="ExternalInput").ap()
    x_d = dt_in("x", [T, D])
    gmix_d = dt_in("norm_mix_g", [D])
    win_d = dt_in("w_in", [D, INW])
    wga_d = dt_in("w_gate_a", [16, 64, 64])
    wgx_d = dt_in("w_gate_x", [16, 64, 64])
    vec_d = dt_in("vecs", [128, NV * 8])
    wout_d = dt_in("w_out", [2 * D, D])
    gmlp_d = dt_in("norm_mlp_g", [D])
    wup_d = dt_in("w_mlp_up", [D, DFF])
    wdn_d = dt_in("w_mlp_down", [DFF, D])
    gfin_d = dt_in("norm_final_g", [D])
    out_d = nc.dram_tensor("out", [T, D], F32, kind="ExternalOutput").ap()

    win_r = win_d.rearrange("(k p) e -> p k e", p=128)
    wout_r = wout_d.rearrange("(k p) e -> p k e", p=128)
    wup_r = wup_d.rearrange("(k p) e -> p k e", p=128)
    wdn_r = wdn_d.rearrange("(k p) e -> p k e", p=128)

    with ExitStack() as st:
        S = Sched(nc, st)
        SB = lambda name, shape, dt: st.enter_context(nc.sbuf_tensor(name, shape, dt))
        hT = SB("hT", [128, 8, T], BF16)
        mix = SB("mix", [128, 16 * T], BF16)
        big = SB("big", [128, 16 * D], F32)
        wo = SB("wo", [128, 16 * D], BF16)
        aux = SB("aux", [128, 4096], BF16)
        wst = [aux[:, i * 2048:(i + 1) * 2048].rearrange("p (k e) -> p k e", k=8) for i in range(2)]
        vec = SB("vec", [128, NV, 8], F32)
        dv = SB("dv", [128, 6, 8], F32)
        gbc = aux[:, 2048:4096].bitcast(F32)
        ident = SB("ident", [128, 128], BF16)
        identf = SB("identf", [128, 128], F32)
        maskf = SB("maskf", [128, 2, 128], F32)
        mask = SB("mask", [128, 2, 128], BF16)
        onec = SB("onec", [128, 1], BF16)
        stat = SB("stat", [128, 4, NT], F32)
        hn = [aux[:, i * 1024:(i + 1) * 1024] for i in range(2)]

        mixv = mix[:].rearrange("p (e t) -> p e t", e=16)
        xres = big[:].rearrange("p (j d) -> p j d", j=NT)
        wov = wo[:].rearrange("p (e d) -> p e d", e=16)

        bigbf = big[:].bitcast(BF16)

        def f32s(off, n):
            return big[:, off:off + n]
        XL = f32s(0, 8 + T)
        o_ = 8 + T
        CH = {}
        for nm in ("A", "W", "GX", "GG"):
            CH[nm] = f32s(o_, T)
            o_ += T
        tmp = {}
        for nm, nb_ in (("xc", 2), ("ut", 2), ("tr", 1), ("ti", 1), ("hh", 2)):
            for b in range(nb_):
                tmp[(nm, b)] = f32s(o_, 512)
                o_ += 512
        ob = 2 * o_
        xcb = [bigbf[:, ob + i * 512: ob + (i + 1) * 512] for i in range(2)]
        ob += 1024
        ysq = [bigbf[:, ob + i * 512: ob + (i + 1) * 512] for i in range(2)]
        ob += 1024
        assert ob <= 32768
        wg_a = SB("wg_a", [128, 8, 128], BF16)
        wg_x = SB("wg_x", [128, 8, 128], BF16)
        WH = 8192
        kTd = wo[:, WH:WH + 2 * T].rearrange("p (h t) -> p h t", h=2)
        vaug = wo[:, WH + 2 * T:WH + 2 * T + NT * 128].rearrange("p (j h m) -> p j h m", j=NT, h=2)
        wq = [wo[:, WH + 6144 + i * 1024:WH + 6144 + (i + 1) * 1024].rearrange("p (k e) -> p k e", k=8) for i in range(2)]
        qT = mixv[:, 8:16, :]
        ptb = [aux[:, i * 1024:(i + 1) * 1024].rearrange("p (e f) -> p e f", e=2) for i in range(2)]
        rden = [aux[:, 2048 + i * 512:2048 + (i + 1) * 512].bitcast(F32) for i in range(2)]
        yab = [aux[:, 3072 + i * 256:3072 + (i + 1) * 256] for i in range(2)]

        PS = [st.enter_context(nc.psum_tensor("ps%d" % i, [128, 512], F32)) for i in range(8)]
        PT = PS[7][:].bitcast(BF16)

        S.dma("sp", vec[:].rearrange("p v c -> p (v c)"), vec_d, writes=["vec"])
        S.dma("sp", gbc, gmix_d.partition_broadcast(128), writes=["gbc"])
        S.op("pool", lambda e: e.memset(identf[:], 1.0), writes=["identf"])
        S.op("pool", lambda e: e.affine_select(out=identf[:], in_=identf[:], pattern=[[-1, 128]],
                                                 compare_op=ALU.is_equal, fill=0.0, base=0, channel_multiplier=1),
             reads=["identf"], writes=["identf"])
        S.op("pool", lambda e: e.memset(maskf[:], 1.0), writes=["maskf"])
        S.op("pool", lambda e: e.affine_select(out=maskf[:, 0, :], in_=maskf[:, 0, :], pattern=[[-1, 128]],
                                                 compare_op=ALU.is_gt, fill=0.0, base=0, channel_multiplier=1),
             reads=["maskf"], writes=["maskf"])
        S.op("pool", lambda e: e.affine_select(out=maskf[:, 1, :], in_=maskf[:, 1, :], pattern=[[1, 128]],
                                                 compare_op=ALU.is_ge, fill=0.0, base=0, channel_multiplier=-1),
             reads=["maskf"], writes=["maskf"])
        S.op("dve", lambda e: e.tensor_copy(out=ident[:], in_=identf[:]), reads=["identf"], writes=["ident"])
        S.op("dve", lambda e: e.tensor_copy(out=mask[:], in_=maskf[:]), reads=["maskf"], writes=["mask"])
        S.op("dve", lambda e: e.memset(onec[:], 1.0), writes=["onec"])
        S.op("dve", lambda e: e.memset(stat[:], 0.0), writes=["stat"])
        S.op("dve", lambda e: e.memset(wg_a[:], 0.0), writes=["wg_a"])
        S.op("dve", lambda e: e.memset(wg_x[:], 0.0), writes=["wg_x"])
        for gd, gt, nm in ((wga_d, wg_a, "wg_a"), (wgx_d, wg_x, "wg_x")):
            gr = gd.rearrange("(c t) i j -> t i c j", t=2)
            for t2 in range(2):
                S.dma("pool", gt[t2 * 64:(t2 + 1) * 64, :, t2 * 64:(t2 + 1) * 64], gr[t2], writes=[nm])
        S.op("dve", lambda e: e.tensor_scalar(out=dv[:, 0, :], in0=vec[:, V_BA, :], scalar1=0.5, scalar2=None,
                                              op0=ALU.mult), reads=["vec"], writes=["dv0"])
        S.op("dve", lambda e: e.tensor_scalar(out=dv[:, 1, :], in0=vec[:, V_BX, :], scalar1=0.5, scalar2=None,
                                              op0=ALU.mult), reads=["vec"], writes=["dv1"])
        S.op("act", lambda e: e.activation(out=dv[:, 5, :], in_=vec[:, V_LAM, :], func=AF.Exp, scale=-1.0),
             reads=["vec"], writes=["dv5"])
        S.op("act", lambda e: e.activation(out=dv[:, 5, :], in_=dv[:, 5, :], func=AF.Ln, bias=1.0),
             reads=["dv5"], writes=["dv5"])
        S.op("dve", lambda e: e.tensor_scalar(out=dv[:, 2, :], in0=dv[:, 5, :], scalar1=-4.0, scalar2=None,
                                              op0=ALU.mult), reads=["dv5"], writes=["dv2"])
        S.op("dve", lambda e: e.tensor_scalar(out=dv[:, 3, :], in0=dv[:, 5, :], scalar1=-8.0, scalar2=None,
                                              op0=ALU.mult), reads=["dv5"], writes=["dv3"])
        S.op("act", lambda e: e.activation(out=dv[:, 4, :], in_=vec[:, V_SINK, :], func=AF.Exp),
             reads=["vec"], writes=["dv4"])
        CONST_R = ["vec", "dv0", "dv1", "dv2", "dv3", "dv4"]

        def rstd_from_ss(ss_ap, out_ap, key_r, key_w):
            S.op("dve", lambda e: e.tensor_scalar(out=out_ap, in0=ss_ap, scalar1=1.0 / D, scalar2=EPS,
                                                  op0=ALU.mult, op1=ALU.add), reads=[key_r, "stat"], writes=[key_w])
            S.op("act", lambda e: e.activation(out=out_ap, in_=out_ap, func=AF.Ln), reads=[key_w], writes=[key_w])
            S.op("act", lambda e: e.activation(out=out_ap, in_=out_ap, func=AF.Exp, scale=-0.5),
                 reads=[key_w], writes=[key_w])

        S.dma("pool", wq[0], win_r[:, :, 0:128], writes=["wq0"])
        S.dma("pool", wq[1], win_r[:, :, D:D + 128], writes=["wq1"])
        TRB = ((4, 5), (6, 7))

        def tr_mm(b):
            banks = TRB[b]

            def f(e):
                ins = None
                for k in range(8):
                    ins = e.matmul(PS[banks[k // 4]][:, (k % 4) * 128:(k % 4 + 1) * 128],
                                   lhsT=hn[b][:, k * 128:(k + 1) * 128], rhs=ident[:], start=True, stop=True)
                return ins
            S.op("pe", f, reads=["hn%d" % b, "ident"], writes=["PS%d" % banks[0], "PS%d" % banks[1]])

        def tr_evac(b, j):
            banks = TRB[b]
            S.op("act", lambda e: e.activation(out=hT[:, 0:4, j * 128:(j + 1) * 128],
                                               in_=PS[banks[0]][:].rearrange("p (k t) -> p k t", k=4), func=AF.Copy),
                 reads=["PS%d" % banks[0]], writes=["hT%d" % j])
            S.op("act", lambda e: e.activation(out=hT[:, 4:8, j * 128:(j + 1) * 128],
                                               in_=PS[banks[1]][:].rearrange("p (k t) -> p k t", k=4), func=AF.Copy),
                 reads=["PS%d" % banks[1]], writes=["hT%d" % j])

        for j in range(NT):
            S.dma("sp" if j % 2 == 0 else "act", xres[:, j, :], x_d[j * 128:(j + 1) * 128, :], writes=["xt0_%d" % j])
        junk0 = wo[:, 0:1024]

        def P0A(g):
            for j in range(g * 4, g * 4 + 4):
                S.op("act", lambda e, j=j: e.activation(out=junk0, in_=xres[:, j, :], func=AF.Square,
                                                        accum_out=stat[:, 0, j:j + 1]),
                     reads=["xt0_%d" % j, "stat"], writes=["p0ss%d" % g])
            rstd_from_ss(stat[:, 0, g * 4:g * 4 + 4], stat[:, 1, g * 4:g * 4 + 4], "p0ss%d" % g, "p0rs%d" % g)

        def P0T1(j):
            b = j % 2
            ptv = PS[7 - j % 3][:].bitcast(BF16)
            S.op("dve", lambda e: e.scalar_tensor_tensor(out=hn[b], in0=xres[:, j, :], scalar=stat[:, 1, j:j + 1], in1=gbc,
                                                         op0=ALU.mult, op1=ALU.mult),
                 reads=["xt0_%d" % j, "p0rs%d" % (j // 4), "gbc"], writes=["hn%d" % b])

            def tr(e):
                ins = None
                for k in range(8):
                    ins = e.transpose(out=ptv[:, k * 128:(k + 1) * 128], in_=hn[b][:, k * 128:(k + 1) * 128],
                                      identity=ident[:])
                return ins
            S.op("pe", tr, reads=["hn%d" % b, "ident"], writes=["PS%d" % (7 - j % 3)])

        def P0T2(j):
            ptv = PS[7 - j % 3][:].bitcast(BF16)
            S.op("dve", lambda e: e.tensor_copy(out=hT[:, :, j * 128:(j + 1) * 128],
                                                in_=ptv.rearrange("p (k t) -> p k t", k=8)),
                 reads=["PS%d" % (7 - j % 3)], writes=["hT%d" % j])
        P0A(0)
        P0A(1)
        for j in range(NT + 2):
            if 0 <= j - 2 < NT:
                P0T2(j - 2)
            if j < NT:
                P0T1(j)
            if j < NT and j % 4 == 3 and j // 4 + 2 < 4:
                P0A(j // 4 + 2)

        S.barrier()
        S.op("dve", lambda e: e.memset(XL[:, 0:8], 0.0), writes=["XLhalo"])

        def hT_keys(tt):
            return ["hT%d" % j for j in range(tt * 4, tt * 4 + 4)]

        wslot = [0]

        def load_w_cols(col_specs):
            s = wslot[0] % 2
            wslot[0] += 1
            for (c0, n, d0) in col_specs:
                S.dma("pool", wst[s][:, :, d0:d0 + n], win_r[:, :, c0:c0 + n], writes=["wst%d" % s])
            return s

        def proj(ps_ap, s, d0, tt, m=128):
            def f(e):
                ins = None
                for k in range(8):
                    ins = e.matmul(ps_ap, lhsT=wst[s][:, k, d0:d0 + m], rhs=hT[:, k, tt * 512:(tt + 1) * 512],
                                   start=(k == 0), stop=(k == 7))
                return ins
            return f

        for e4 in range(0, 2):
            S.dma("pool", wov[:, e4 * 4:(e4 + 1) * 4, :], wout_r[:, e4 * 4:(e4 + 1) * 4, :], writes=["wo_lo"])

        C_GELU = 0.7978845608028654
        steps = [(c, tt) for c in range(8) for tt in range(NTT)]
        slots = {}

        def lru_load(c):
            slots[c] = load_w_cols([(c * 128, 128, 0), (D + c * 128, 128, 128)])

        XBS, GBS, ZA, ZX, SSB, PJ = (0, 0), (1, 2, 3, 7), 4, 5, 6, 6

        def projw(ps_ap, w3, tt):
            def f(e):
                ins = None
                for k in range(8):
                    ins = e.matmul(ps_ap, lhsT=w3[:, k, :], rhs=hT[:, k, tt * 512:(tt + 1) * 512],
                                   start=(k == 0), stop=(k == 7))
                return ins
            return f

        def S0(n):
            c, tt = steps[n]
            gb = GBS[n % 4]
            XB = XBS[n % 2]
            if c == 0:
                S.op("pe", projw(PS[XB][:], wq[0], tt), reads=["wq0"] + hT_keys(tt), writes=["PS%d" % XB])
                S.op("pe", projw(PS[gb][:], wq[1], tt), reads=["wq1"] + hT_keys(tt), writes=["PS%d" % gb])
                return
            s_ = slots[c]
            S.op("pe", proj(PS[XB][:], s_, 0, tt), reads=["wst%d" % s_] + hT_keys(tt), writes=["PS%d" % XB])
            S.op("pe", proj(PS[gb][:], s_, 128, tt), reads=["wst%d" % s_] + hT_keys(tt), writes=["PS%d" % gb])

        def S1a(n):
            c, tt = steps[n]
            b = n % 2
            XB = XBS[n % 2]
            xps, gps = PS[XB], PS[GBS[n % 4]]
            kx, kg = "PS%d" % XB, "PS%d" % GBS[n % 4]
            t0 = 8 + tt * 512
            xc, ut = tmp[("xc", b)], tmp[("ut", b)]
            kxc, kut = "xc%d" % b, "ut%d" % b
            S.op("act", lambda e: e.activation(out=XL[:, t0:t0 + 512], in_=xps[:], func=AF.Copy),
                 reads=[kx], writes=["XL%d" % tt])
            S.op("act", lambda e: e.activation(out=xc, in_=xps[:], func=AF.Identity,
                                               bias=vec[:, V_CB, c:c + 1], scale=vec[:, V_CW3, c:c + 1]),
                 reads=[kx] + CONST_R, writes=[kxc])
            S.op("act", lambda e: e.activation(out=ut, in_=gps[:], func=AF.Square, scale=0.21145921),
                 reads=[kg], writes=[kut])
            xlk = ["XL%d" % tt, "XLhalo"] + (["XL%d" % (tt - 1)] if tt > 0 else [])
            for kk, sh in ((V_CW2, 1), (V_CW1, 2), (V_CW0, 3)):
                S.op("dve", lambda e, kk=kk, sh=sh: e.scalar_tensor_tensor(
                    out=xc, in0=XL[:, t0 - sh:t0 - sh + 512], scalar=vec[:, kk, c:c + 1], in1=xc,
                    op0=ALU.mult, op1=ALU.add), reads=xlk + [kxc] + CONST_R, writes=[kxc])
            S.op("dve", lambda e: e.scalar_tensor_tensor(out=ut, in0=ut, scalar=1.0, in1=gps[:], op0=ALU.add, op1=ALU.mult),
                 reads=[kut, kg], writes=[kut])

        def S1b(n):
            c, tt = steps[n]
            b = n % 2
            xc = tmp[("xc", b)]
            S.op("act", lambda e: e.activation(out=xcb[b], in_=xc, func=AF.Copy), reads=["xc%d" % b], writes=["xcb%d" % b])
            S.op("pe", lambda e: e.matmul(PS[ZA][:], lhsT=wg_a[:, c, :], rhs=xcb[b], start=True, stop=True),
                 reads=["wg_a", "xcb%d" % b], writes=["PS%d" % ZA])
            S.op("pe", lambda e: e.matmul(PS[ZX][:], lhsT=wg_x[:, c, :], rhs=xcb[b], start=True, stop=True),
                 reads=["wg_x", "xcb%d" % b], writes=["PS%d" % ZX])

        def S2(n):
            c, tt = steps[n]
            b = n % 2
            gps, kg = PS[GBS[n % 4]], "PS%d" % GBS[n % 4]
            sl = slice(tt * 512, (tt + 1) * 512)
            xc, ut, tr, ti = tmp[("xc", b)], tmp[("ut", b)], tmp[("tr", 0)], tmp[("ti", 0)]
            A, W, GX, GG = CH["A"][:, sl], CH["W"][:, sl], CH["GX"][:, sl], CH["GG"][:, sl]
            kA, kW, kGX, kGG = "A%d" % tt, "W%d" % tt, "GX%d" % tt, "GG%d" % tt
            S.op("act", lambda e: e.activation(out=tr, in_=PS[ZA][:], func=AF.Tanh, bias=dv[:, 0, c:c + 1], scale=0.5),
                 reads=["PS%d" % ZA] + CONST_R, writes=["tr"])
            S.op("act", lambda e: e.activation(out=ti, in_=PS[ZX][:], func=AF.Tanh, bias=dv[:, 1, c:c + 1], scale=0.5),
                 reads=["PS%d" % ZX] + CONST_R, writes=["ti"])
            S.op("act", lambda e: e.activation(out=ut, in_=ut, func=AF.Tanh, scale=C_GELU), reads=["ut%d" % b], writes=["ut%d" % b])
            S.op("act", lambda e: e.activation(out=A, in_=tr, func=AF.Exp, bias=dv[:, 2, c:c + 1], scale=dv[:, 2, c:c + 1]),
                 reads=["tr"] + CONST_R, writes=[kA])
            S.op("dve", lambda e: e.scalar_tensor_tensor(out=GX, in0=ti, scalar=1.0, in1=xc, op0=ALU.add, op1=ALU.mult),
                 reads=["ti", "xc%d" % b], writes=[kGX])
            S.op("dve", lambda e: e.scalar_tensor_tensor(out=W, in0=A, scalar=-1.0, in1=A, op0=ALU.mult, op1=ALU.mult),
                 reads=[kA], writes=[kW])
            S.op("dve", lambda e: e.scalar_tensor_tensor(out=GG, in0=ut, scalar=1.0, in1=gps[:], op0=ALU.add, op1=ALU.mult),
                 reads=["ut%d" % b, kg], writes=[kGG])

        def LNEXP(c):
            keys = ["W%d" % tt for tt in range(NTT)]
            S.op("act", lambda e: e.activation(out=CH["W"], in_=CH["W"], func=AF.Sqrt, bias=1.0, scale=1.0),
                 reads=keys, writes=keys)

        def S3(n):
            c, tt = steps[n]
            b = n % 2
            sl = slice(tt * 512, (tt + 1) * 512)
            A, W, GX, GG = CH["A"][:, sl], CH["W"][:, sl], CH["GX"][:, sl], CH["GG"][:, sl]
            kA, kW, kGX, kGG = "A%d" % tt, "W%d" % tt, "GX%d" % tt, "GG%d" % tt
            hh = tmp[("hh", b)]
            S.op("dve", lambda e: e.tensor_tensor(out=GX, in0=GX, in1=W, op=ALU.mult), reads=[kGX, kW], writes=[kGX])
            init = 0.0 if tt == 0 else tmp[("hh", 1 - b)][:, 511:512]
            S.op("dve", lambda e: e.tensor_tensor_scan(out=hh, data0=A, data1=GX, initial=init, op0=ALU.mult, op1=ALU.add),
                 reads=[kA, kGX, "hh%d" % (1 - b)], writes=["hh%d" % b])
            S.op("dve", lambda e: e.scalar_tensor_tensor(out=mixv[:, c, tt * 512:(tt + 1) * 512], in0=hh, scalar=0.25, in1=GG,
                                                         op0=ALU.mult, op1=ALU.mult),
                 reads=["hh%d" % b, kGG], writes=["mix_%d_%d" % (c, tt)])

        def S4(n):
            c, tt = steps[n]
            b = n % 2
            hh = tmp[("hh", b)]
            S.op("act", lambda e: e.activation(out=ysq[b], in_=mixv[:, c, tt * 512:(tt + 1) * 512], func=AF.Square),
                 reads=["mix_%d_%d" % (c, tt)], writes=["ysq%d" % b])

            def ssmm(e):
                ins = None
                for t4 in range(4):
                    ins = e.matmul(PS[SSB][:, t4:t4 + 1], lhsT=ysq[b][:, t4 * 128:(t4 + 1) * 128], rhs=onec[:, 0:1],
                                   start=True, stop=True)
                return ins
            S.op("pe", ssmm, reads=["onec", "ysq%d" % b], writes=["PS%d" % SSB])
            dst = stat[:, 2, tt * 4:(tt + 1) * 4]
            S.op("dve", lambda e: e.tensor_tensor(out=dst, in0=PS[SSB][:, 0:4], in1=dst, op=ALU.add),
                 reads=["PS%d" % SSB, "ssl%d" % tt, "stat"], writes=["ssl%d" % tt])

        s3 = 2 * D + D
        s4 = s3 + 128
        xjobs = []
        wqslot = [0]

        def xload(col_specs):
            sq = wqslot[0] % 2
            wqslot[0] += 1

            def f():
                for (c0, n_, d0) in col_specs:
                    S.dma("pool", wq[sq][:, :, d0:d0 + n_], win_r[:, :, c0:c0 + n_], writes=["wq%d" % sq])
            return sq, f

        def xproj_T(sq, tt, dst_ap, dst_key, scale):
            def pe_f(bank):
                def mm(e):
                    ins = None
                    for k in range(8):
                        ins = e.matmul(PS[bank][:], lhsT=wq[sq][:, k, :], rhs=hT[:, k, tt * 512:(tt + 1) * 512],
                                       start=(k == 0), stop=(k == 7))
                    return ins
                S.op("pe", mm, reads=["wq%d" % sq] + hT_keys(tt), writes=["PS%d" % bank])

            def act_f(bank):
                S.op("act", lambda e: e.activation(out=dst_ap, in_=PS[bank][:], func=AF.Copy, scale=scale),
                     reads=["PS%d" % bank], writes=[dst_key])
            return pe_f, act_f

        def xproj_V(sq, j4):
            def pe_f(bank):
                def mm(e):
                    ins = None
                    for jj in range(4):
                        j = j4 * 4 + jj
                        for k in range(8):
                            ins = e.matmul(PS[bank][:, jj * 128:(jj + 1) * 128], lhsT=hT[:, k, j * 128:(j + 1) * 128],
                                           rhs=wq[sq][:, k, :], start=(k == 0), stop=(k == 7))
                    return ins
                S.op("pe", mm, reads=["wq%d" % sq] + hT_keys(j4), writes=["PS%d" % bank])

            def act_f(bank):
                S.op("act", lambda e: e.activation(
                    out=vaug[:, j4 * 4:(j4 + 1) * 4, :, :].rearrange("p j h m -> p j (h m)"),
                    in_=PS[bank][:].rearrange("p (j f) -> p j f", j=4), func=AF.Copy),
                    reads=["PS%d" % bank], writes=["vaug"])
            return pe_f, act_f

        for h in range(2):
            sq, f = xload([(s3 + h * 64, 64, 0), (s3 + h * 64, 64, 64)])
            xjobs.append(("load", f))
            for tt in range(NTT):
                xjobs.append(("proj", xproj_T(sq, tt, kTd[:, h, tt * 512:(tt + 1) * 512], "kTd", 1.0)))
        sq, f = xload([(s4, 128, 0)])
        xjobs.append(("load", f))
        for j4 in range(4):
            xjobs.append(("proj", xproj_V(sq, j4)))
        for c in range(8):
            sq, f = xload([(2 * D + c * 128, 128, 0)])
            xjobs.append(("load", f))
            for tt in range(NTT):
                xjobs.append(("proj", xproj_T(sq, tt, qT[:, c, tt * 512:(tt + 1) * 512], "qT%d" % c, 0.125)))

        xstate = {"i": 0, "pend": None}

        def xjob_act():
            if xstate["pend"] is not None:
                act_f, bank = xstate["pend"]
                act_f(bank)
                xstate["pend"] = None

        def xjob_pe(bank):
            while xstate["i"] < len(xjobs):
                kind, job = xjobs[xstate["i"]]
                xstate["i"] += 1
                if kind == "load":
                    job()
                    continue
                pe_f, act_f = job
                pe_f(bank)
                xstate["pend"] = (act_f, bank)
                return

        NS = len(steps)
        LAG3, LAG4 = 6, 7
        ATT_START = NS + 1

        def lru_emit(att_step):
            lru_load(1)
            lru_load(2)
            S0(0)
            for s_ in range(NS + LAG4 + 1):
                if s_ < ATT_START:
                    xjob_act()
                if s_ >= LAG3 and (s_ - LAG3) % 4 == 0 and (s_ - LAG3) // 4 < 8:
                    LNEXP((s_ - LAG3) // 4)
                if 0 <= s_ - LAG3 < NS:
                    S3(s_ - LAG3)
                if 0 <= s_ - LAG4 < NS:
                    S4(s_ - LAG4)
                if 0 <= s_ - 2 < NS:
                    S2(s_ - 2)
                if 3 <= s_ < ATT_START - 1:
                    xjob_pe(PJ)
                if 0 <= s_ - 1 < NS:
                    S1b(s_ - 1)
                if s_ < NS:
                    S1a(s_)
                if s_ + 1 < NS:
                    S0(s_ + 1)
                if s_ < NS:
                    c, tt = steps[s_]
                    if tt == 3 and c >= 1 and c + 2 < 8:
                        lru_load(c + 2)
                if s_ >= ATT_START:
                    att_step()
                    att_step()

        onesw = SB("onesw", [128, 64], BF16)
        S.op("dve", lambda e: e.memset(onesw[:], 1.0), writes=["onesw"])
        kv = {"k": kTd, "v": vaug, "kk": "kTd", "vk": "vaug"}
        its = [(c, tt, m2) for c in range(8) for tt in range(NTT) for m2 in range(2)]
        NI = len(its)
        maskb = mask[:].rearrange("p k q -> p (k q)").unsqueeze(1).broadcast_to([128, 4, 256])

        def AQ(i):
            c, tt, m2 = its[i]
            h = c // 4
            pb = i % 2
            n0 = tt * 4 + m2 * 2

            def qk(e):
                ins = None
                segs = [(max(n0 - 1, 0), 0, n0 * 128, 128), (n0, 128, n0 * 128, 256), (n0 + 1, 384, (n0 + 1) * 128, 128)]
                for (kblk, col, q0, nq) in segs:
                    for ee in range(2):
                        ins = e.matmul(PS[pb * 2 + ee][:, col:col + nq],
                                       lhsT=kv["k"][ee * 64:(ee + 1) * 64, h, kblk * 128:(kblk + 1) * 128],
                                       rhs=qT[ee * 64:(ee + 1) * 64, c, q0:q0 + nq], start=True, stop=True)
                return ins
            S.op("pe", qk, reads=[kv["kk"], "qT%d" % c, "qblk%d" % i], writes=["PS%d" % (pb * 2), "PS%d" % (pb * 2 + 1)])
            for ee in range(2):
                S.op("act", lambda e, ee=ee: e.activation(out=ptb[pb][:, ee, :], in_=PS[pb * 2 + ee][:], func=AF.Exp),
                     reads=["PS%d" % (pb * 2 + ee)], writes=["pt%d_%d" % (pb, ee)])
            ptv = ptb[pb][:].rearrange("p e (n f) -> p (e n) f", n=2)
            S.op("dve", lambda e: e.tensor_tensor(out=ptv, in0=ptv, in1=maskb, op=ALU.mult),
                 reads=["pt%d_0" % pb, "pt%d_1" % pb, "mask"], writes=["pt%d_0" % pb, "pt%d_1" % pb])

        def AV(i):
            c, tt, m2 = its[i]
            h = c // 4
            pb = i % 2
            od = PS[4 + pb]
            n0 = tt * 4 + m2 * 2

            def pv(e):
                ins = None
                merged_den = n0 > 0
                if merged_den:
                    for ee in range(2):
                        o_ee = od[ee * 64:(ee + 1) * 64, 0:256]
                        e.matmul(o_ee[:, 0:128], lhsT=kv["v"][:, n0 - 1, h, :], rhs=ptb[pb][:, ee, 0:128], start=True, stop=False,
                                 skip_group_check=True)
                        e.matmul(o_ee[:, 0:256], lhsT=kv["v"][:, n0, h, :], rhs=ptb[pb][:, ee, 128:384], start=False, stop=False,
                                 skip_group_check=True)
                        e.matmul(o_ee[:, 128:256], lhsT=kv["v"][:, n0 + 1, h, :], rhs=ptb[pb][:, ee, 384:512],
                                 start=False, stop=True, skip_group_check=True)
                for part in range(0 if merged_den else 2):
                    for nn in range(2):
                        n = n0 + nn
                        col = part * 256 + nn * 128
                        for ee in range(2):
                            kbs = [1] if n == 0 else [0, 1]
                            for idx, kb in enumerate(kbs):
                                kblk = n - 1 + kb
                                rhs = ptb[pb][:, ee, (nn * 2 + kb) * 128:(nn * 2 + kb + 1) * 128]
                                lhsT = kv["v"][:, kblk, h, :] if part == 0 else onesw[:, :]
                                ins = e.matmul(od[ee * 64:(ee + 1) * 64, col:col + 128], lhsT=lhsT, rhs=rhs,
                                               start=(idx == 0), stop=(idx == len(kbs) - 1))
                if merged_den:
                    for ee in range(2):
                        pt4 = ptb[pb][:, ee, :].rearrange("p (n k q) -> p n k q", n=2, k=2)
                        for kb in range(2):
                            ins = e.matmul(od[ee * 64:(ee + 1) * 64, 256:512].rearrange("p (n q) -> p n q", n=2),
                                           lhsT=onesw[:, :], rhs=pt4[:, :, kb, :], start=(kb == 0), stop=(kb == 1))
                return ins
            S.op("pe", pv, reads=[kv["vk"], "onesw", "pt%d_0" % pb, "pt%d_1" % pb], writes=["PS%d" % (4 + pb)])

        def AN1(i):
            c, tt, m2 = its[i]
            pb = i % 2
            od = PS[4 + pb]
            rd = rden[pb]
            ya = rd
            S.op("act", lambda e: e.activation(out=rd, in_=od[:, 256:512], func=AF.Ln, bias=dv[:, 4, c:c + 1]),
                 reads=["PS%d" % (4 + pb)] + CONST_R, writes=["rden%d" % pb])
            S.op("act", lambda e: e.activation(out=rd, in_=rd, func=AF.Exp, scale=-1.0),
                 reads=["rden%d" % pb], writes=["rden%d" % pb])
            cs = slice(tt * 512 + m2 * 256, tt * 512 + m2 * 256 + 256)
            S.op("dve", lambda e: e.tensor_tensor(out=mixv[:, 8 + c, cs], in0=od[:, 0:256], in1=rd, op=ALU.mult),
                 reads=["PS%d" % (4 + pb), "rden%d" % pb], writes=["qblk%d" % i])

        def AN2(i):
            c, tt, m2 = its[i]
            pb = i % 2
            cs = slice(tt * 512 + m2 * 256, tt * 512 + m2 * 256 + 256)
            ya, yb = rden[pb], yab[pb]
            S.op("act", lambda e: e.activation(out=yb, in_=mixv[:, 8 + c, cs], func=AF.Square),
                 reads=["qblk%d" % i], writes=["yab%d" % pb])

            def ssmm2(e):
                ins = None
                for t2 in range(2):
                    ins = e.matmul(PS[6][:, t2:t2 + 1], lhsT=yb[:, t2 * 128:(t2 + 1) * 128], rhs=onec[:, 0:1],
                                   start=True, stop=True)
                return ins
            S.op("pe", ssmm2, reads=["onec", "yab%d" % pb], writes=["PS6"])
            dst = stat[:, 3, tt * 4 + m2 * 2:tt * 4 + m2 * 2 + 2]
            S.op("dve", lambda e: e.tensor_tensor(out=dst, in0=PS[6][:, 0:2], in1=dst, op=ALU.add),
                 reads=["PS6", "ssa%d_%d" % (tt, m2), "stat"], writes=["ssa%d_%d" % (tt, m2)])

        mixw = mix[:].rearrange("p (s w f) -> p s w f", s=2, w=2)

        def mlp_load(q):
            sl = q % 2
            wup_s = mixw[:, sl, 0, :].rearrange("p (k f) -> p k f", k=8)
            wdn_s = mixw[:, sl, 1, :].rearrange("p (k d) -> p k d", k=8)
            for k2 in range(2):
                S.dma("pool", wup_s[:, k2 * 4:(k2 + 1) * 4, :], wup_r[:, k2 * 4:(k2 + 1) * 4, q * 1024:(q + 1) * 1024],
                      writes=["wup%d" % sl, "mixhalf%d" % sl])
                S.dma("pool", wdn_s[:, k2 * 4:(k2 + 1) * 4, :], wdn_r[:, q * 8 + k2 * 4:q * 8 + (k2 + 1) * 4, :],
                      writes=["wdn%d" % sl, "mixhalf%d" % sl])

        att_state = {"step": 0, "lru_done": False, "h": 0, "pend": None, "started": False}

        def wol_start():
            S.wait_keys("dve", ["ssl%d" % t_ for t_ in range(NTT)])
            rstd_from_ss(stat[:, 2, :], stat[:, 2, :], "ssl_all", "rl")
            for k in range(8):
                S.op("dve", lambda e, k=k: e.tensor_scalar(out=wov[:, k, :], in0=wov[:, k, :], scalar1=vec[:, V_GL, k:k + 1],
                                                           scalar2=None, op0=ALU.mult),
                     reads=["wo_lo"] + CONST_R, writes=["wo_s%d" % k])
            hkeys = ["hT%d" % j_ for j_ in range(NT)]
            S.wait_readers("dve", hkeys)
            S.op("dve", lambda e: e.tensor_copy(out=hT[:, 0:2, :], in_=kTd), reads=["kTd"], writes=["kTd2"])
            S.op("dve", lambda e: e.tensor_copy(out=hT[:, 2, :], in_=wo[:, WH + 2 * T:WH + 2 * T + NT * 128]),
                 reads=["vaug"], writes=["vaug2"])
            kv["k"], kv["kk"] = hT[:, 0:2, :], "kTd2"
            kv["v"], kv["vk"] = hT[:, 2, :].rearrange("p (j h m) -> p j h m", j=NT, h=2), "vaug2"
            S._wait("sp", S._deps((), list(S.state.keys())))
            for j in range(NT):
                S.dma("sp", xres[:, j, :], x_d[j * 128:(j + 1) * 128, :], writes=["xres%d" % j])
            for e4 in range(2, 4):
                S.dma("pool", wov[:, e4 * 4:(e4 + 1) * 4, :], wout_r[:, e4 * 4:(e4 + 1) * 4, :],
                      writes=["wo_hi", "kTd", "vaug", "wq0", "wq1"])

        def wol_dve():
            if att_state["pend"] is not None:
                h = att_state["pend"]
                j, hf = h // 2, h % 2
                S.op("dve", lambda e: e.scalar_tensor_tensor(
                    out=xres[:, j, hf * 512:(hf + 1) * 512], in0=PS[7][:], scalar=stat[:, 2, j:j + 1],
                    in1=xres[:, j, hf * 512:(hf + 1) * 512], op0=ALU.mult, op1=ALU.add),
                    reads=["PS7", "rl", "xres%d" % j], writes=["xres%d" % j])
                att_state["pend"] = None
                if h == 2 * NT - 1:
                    mlp_load(0)

        def wol_pe():
            h = att_state["h"]
            if h >= 2 * NT:
                return
            att_state["h"] += 1
            j, hf = h // 2, h % 2

            def mm(e):
                ins = None
                for k in range(8):
                    ins = e.matmul(PS[7][:], lhsT=mixv[:, k, j * 128:(j + 1) * 128], rhs=wov[:, k, hf * 512:(hf + 1) * 512],
                                   start=(k == 0), stop=(k == 7))
                return ins
            S.op("pe", mm, reads=["wo_s%d" % k for k in range(8)] + ["mixhalf0"], writes=["PS7"])
            att_state["pend"] = h

        def att_step():
            step = att_state["step"]
            if step >= NI + 3:
                return
            att_state["step"] += 1
            if step == 0:
                for eng_ in ("act", "dve", "pool"):
                    S.wait_readers(eng_, ["wst0", "wst1"])
            xjobs_done = xstate["i"] >= len(xjobs) and xstate["pend"] is None
            if att_state["lru_done"] and xjobs_done and not att_state["started"]:
                att_state["started"] = True
                wol_start()
            if att_state["started"]:
                wol_dve()
            xjob_act()
            if 0 <= step - 3 < NI:
                AN2(step - 3)
            if 0 <= step - 2 < NI:
                AN1(step - 2)
            if step < NI:
                AQ(step)
            if 0 <= step - 1 < NI:
                AV(step - 1)
            xjob_pe(7)
            if att_state["started"]:
                wol_pe()

        lru_emit(att_step)
        att_state["lru_done"] = True
        while att_state["step"] < NI + 3:
            att_step()
        assert xstate["i"] >= len(xjobs) and xstate["pend"] is None
        assert att_state["started"]
        while att_state["h"] < 2 * NT or att_state["pend"] is not None:
            wol_dve()
            wol_pe()
        S.barrier()
        S.dma("sp", gbc, gmlp_d.partition_broadcast(128), writes=["gbc"])
        rstd_from_ss(stat[:, 3, :], stat[:, 3, :], "ssrow_r", "rla")
        for k in range(8, 16):
            S.op("dve", lambda e, k=k: e.tensor_scalar(out=wov[:, k, :], in0=wov[:, k, :], scalar1=vec[:, V_GA, k - 8:k - 7],
                                                       scalar2=None, op0=ALU.mult),
                 reads=["wo_hi"] + CONST_R, writes=["wo_s%d" % k])
        NG = NT

        def PA(g):
            br, j = 1, g
            bk = (g % 2) * 2

            def wo_mm(e):
                ins = None
                for hf in range(2):
                    for k in range(8):
                        ins = e.matmul(PS[bk + hf][:], lhsT=mixv[:, br * 8 + k, j * 128:(j + 1) * 128],
                                       rhs=wov[:, br * 8 + k, hf * 512:(hf + 1) * 512], start=(k == 0), stop=(k == 7))
                return ins
            S.op("pe", wo_mm, reads=["wo_s%d" % (br * 8 + k) for k in range(8)] + ["mixhalf%d" % br], writes=["PS%d" % bk, "PS%d" % (bk + 1)])

        def PB(g):
            br, j = 1, g
            bk = (g % 2) * 2
            for hf in range(2):
                S.op("dve", lambda e, hf=hf: e.scalar_tensor_tensor(
                    out=xres[:, j, hf * 512:(hf + 1) * 512], in0=PS[bk + hf][:], scalar=stat[:, 2 + br, j:j + 1],
                    in1=xres[:, j, hf * 512:(hf + 1) * 512], op0=ALU.mult, op1=ALU.add),
                    reads=["PS%d" % (bk + hf), "rla", "xres%d" % j], writes=["xres%d" % j])

        def PC1a(j):
            b = j % 2
            S.op("act", lambda e: e.activation(out=hn[b], in_=xres[:, j, :], func=AF.Square, accum_out=stat[:, 0, j:j + 1]),
                 reads=["xres%d" % j, "stat"], writes=["hn%d" % b, "p2ss%d" % j])

        def PC1c(j):
            rstd_from_ss(stat[:, 0, j:j + 1], stat[:, 1, j:j + 1], "p2ss%d" % j, "p2rs%d" % j)

        def PC1b(j):
            b = j % 2
            S.op("dve", lambda e: e.scalar_tensor_tensor(out=hn[b], in0=xres[:, j, :], scalar=stat[:, 1, j:j + 1], in1=gbc,
                                                         op0=ALU.mult, op1=ALU.mult),
                 reads=["xres%d" % j, "p2rs%d" % j, "gbc"], writes=["hn%d" % b])

        def PC2(j):
            b = j % 2
            tr_mm(b)
            tr_evac(b, j)

        for g in range(NG + 4):
            if 0 <= g - 3 < NG:
                PC2(g - 3)
            if g < NG:
                PA(g)
            if 0 <= g - 1 < NG:
                PB(g - 1)
                PC1a(g - 1)
            if 0 <= g - 2 < NG:
                PC1b(g - 2)
            if 0 <= g - 1 < NG:
                PC1c(g - 1)

        S.barrier()
        S.dma("sp", gbc, gfin_d.partition_broadcast(128), writes=["gbc"])
        actb = wo[:].rearrange("p (s f) -> p s f", s=2)
        NQ = 4
        units = [(q, tt) for q in range(NQ) for tt in range(NTT)]

        def wviews(q):
            sl = q % 2
            return (sl, mixw[:, sl, 0, :].rearrange("p (k f) -> p k f", k=8),
                    mixw[:, sl, 1, :].rearrange("p (k d) -> p k d", k=8))

        def MUP(u):
            q, tt = units[u]
            sl, wup_s, wdn_s = wviews(q)
            ab = u % 2
            act_s = actb[:, ab, 0:4096].rearrange("p (c t) -> p c t", c=8)
            for fc in range(8):
                ub = fc % 2

                def up(e, fc=fc, ub=ub):
                    ins = None
                    for k in range(8):
                        ins = e.matmul(PS[ub][:], lhsT=wup_s[:, k, fc * 128:(fc + 1) * 128],
                                       rhs=hT[:, k, tt * 512:(tt + 1) * 512], start=(k == 0), stop=(k == 7))
                    return ins
                S.op("pe", up, reads=["wup%d" % sl] + hT_keys(tt), writes=["PS%d" % ub])
                S.op("act", lambda e, ub=ub: e.activation(out=hn[ub].bitcast(F32), in_=PS[ub][:], func=AF.Relu),
                     reads=["PS%d" % ub], writes=["relu%d" % ub])
                S.op("dve", lambda e, ub=ub, fc=fc: e.tensor_tensor(out=act_s[:, fc, :], in0=hn[ub].bitcast(F32),
                                                                    in1=hn[ub].bitcast(F32), op=ALU.mult),
                     reads=["relu%d" % ub], writes=["act%d_%d" % (ab, fc)])

        def MDN(u):
            q, tt = units[u]
            sl, wup_s, wdn_s = wviews(q)
            ab = u % 2
            act_s = actb[:, ab, 0:4096].rearrange("p (c t) -> p c t", c=8)
            for t4 in range(4):
                j = tt * 4 + t4
                db = 2 + (t4 % 2) * 2

                def dn(e, t4=t4, db=db):
                    ins = None
                    for hf in range(2):
                        for fc in range(8):
                            ins = e.matmul(PS[db + hf][:], lhsT=act_s[:, fc, t4 * 128:(t4 + 1) * 128],
                                           rhs=wdn_s[:, fc, hf * 512:(hf + 1) * 512], start=(fc == 0), stop=(fc == 7))
                    return ins
                S.op("pe", dn, reads=["wdn%d" % sl] + ["act%d_%d" % (ab, fc) for fc in range(8)],
                     writes=["PS%d" % db, "PS%d" % (db + 1)])
                for hf in range(2):
                    S.op("dve", lambda e, hf=hf, db=db, j=j: e.tensor_tensor(
                        out=xres[:, j, hf * 512:(hf + 1) * 512], in0=PS[db + hf][:],
                        in1=xres[:, j, hf * 512:(hf + 1) * 512], op=ALU.add),
                        reads=["PS%d" % (db + hf), "xres%d" % j], writes=["xres%d" % j])
                if q == NQ - 1:
                    fj = wo[:, 4096:5120]
                    S.op("act", lambda e, j=j: e.activation(out=fj, in_=xres[:, j, :], func=AF.Square,
                                                            accum_out=stat[:, 0, j:j + 1]),
                         reads=["xres%d" % j, "stat"], writes=["fss%d" % j])
                    rstd_from_ss(stat[:, 0, j:j + 1], stat[:, 1, j:j + 1], "fss%d" % j, "frs%d" % j)
                    S.op("dve", lambda e, j=j: e.scalar_tensor_tensor(
                        out=xres[:, j, :], in0=xres[:, j, :], scalar=stat[:, 1, j:j + 1], in1=gbc,
                        op0=ALU.mult, op1=ALU.mult),
                        reads=["xres%d" % j, "frs%d" % j, "gbc"], writes=["xres%d" % j])
                    S.dma("sp", out_d[j * 128:(j + 1) * 128, :], xres[:, j, :], reads=["xres%d" % j],
                          writes=["out%d" % j])

        NU = len(units)
        mlp_load(1)
        MUP(0)
        for u in range(NU):
            if u + 1 < NU:
                MUP(u + 1)
            MDN(u)
            q, tt = units[u]
            if tt == NTT - 1 and q + 2 < NQ:
                mlp_load(q + 2)
        S.wait_keys("sp", ["out%d" % j for j in range(NT)])
    return nc


def _pack_vecs(inp):
    fm = lambda v: np.ascontiguousarray(np.asarray(v, np.float32).reshape(8, 128).T)
    cw = np.asarray(inp["conv_w"], np.float32)[0]
    vs = [fm(cw[0]), fm(cw[1]), fm(cw[2]), fm(cw[3]), fm(inp["conv_b"][0]), fm(inp["b_gate_a"][0]),
          fm(inp["b_gate_x"][0]), fm(inp["lru_lambda"][0]), fm(inp["lru_out_g"][0]), fm(inp["attn_out_g"][0]),
          fm(np.repeat(np.asarray(inp["attn_sinks"], np.float32)[0], 64))]
    return np.ascontiguousarray(np.stack(vs, axis=1).reshape(128, NV * 8))


_NC_CACHE = {}


def kernel(**inputs):
    x = np.asarray(inputs["x"], np.float32)
    nb = x.shape[0]
    if "nc" not in _NC_CACHE:
        _NC_CACHE["nc"] = build_program()
    nc = _NC_CACHE["nc"]
    f = lambda k: np.ascontiguousarray(np.asarray(inputs[k], np.float32)[0])
    shared = {
        "norm_mix_g": f("norm_mix_g"), "w_in": f("w_in"), "w_gate_a": f("w_gate_a"), "w_gate_x": f("w_gate_x"),
        "vecs": _pack_vecs(inputs), "w_out": f("w_out"), "norm_mlp_g": f("norm_mlp_g"),
        "w_mlp_up": f("w_mlp_up"), "w_mlp_down": f("w_mlp_down"),
        "norm_final_g": np.ascontiguousarray(np.asarray(inputs["norm_final_g"], np.float32)),
    }
    in_maps = [dict(shared, x=np.ascontiguousarray(x[b])) for b in range(nb)]
    res = run_bass_kernel_spmd(nc, in_maps, core_ids=list(range(nb)))
    return np.stack([np.asarray(r["out"], np.float32) for r in res.results], axis=0)
```

```python
import numpy as np
from contextlib import ExitStack
import concourse.bass as bass
import concourse.mybir as mybir
from concourse.bass_utils import run_bass_kernel_spmd

F32 = mybir.dt.float32
BF16 = mybir.dt.bfloat16
AF = mybir.ActivationFunctionType
ALU = mybir.AluOpType

D = 1024
T = 2048
NT = T // 128
NTT = T // 512
INW = 3328
DFF = 4096
EPS = 1e-6
NV = 11
(V_CW0, V_CW1, V_CW2, V_CW3, V_CB, V_BA, V_BX, V_LAM, V_GL, V_GA, V_SINK) = range(NV)


class Sched:
    ENGS = ("pe", "act", "dve", "pool", "sp")

    def __init__(self, nc, stack, n_dma_sems=8):
        self.nc = nc
        self.e = {"pe": nc.tensor, "act": nc.scalar, "dve": nc.vector, "pool": nc.gpsimd, "sp": nc.sync}
        self.sem, self.cnt = {}, {}
        for n in self.ENGS:
            self.sem[n] = stack.enter_context(nc.semaphore("s_" + n))
            self.cnt[n] = 0
        self.dma_pool, self.dma_rr = {}, {}
        for q in ("sp", "pool", "act"):
            lst = []
            for i in range(n_dma_sems):
                nm = "d_%s%d" % (q, i)
                self.sem[nm] = stack.enter_context(nc.semaphore(nm))
                self.cnt[nm] = 0
                lst.append(nm)
            self.dma_pool[q] = lst
            self.dma_rr[q] = 0
        self.clock = {n: {} for n in self.ENGS}
        self.opclock = {}
        self.state = {}

    def _deps(self, reads, writes):
        deps = {}

        def add(d):
            if d is not None and deps.get(d[0], 0) < d[1]:
                deps[d[0]] = d[1]
        for k in reads:
            st = self.state.get(k)
            if st:
                add(st[0])
        for k in writes:
            st = self.state.get(k)
            if st:
                add(st[0])
                for r in st[1]:
                    add(r)
        return deps

    def _wait(self, eng, deps):
        ck = self.clock[eng]
        for p, n in deps.items():
            if ck.get(p, 0) >= n:
                continue
            self.e[eng].wait_ge(self.sem[p], n)
            oc = self.opclock.get((p, n))
            if oc:
                for q, m in oc.items():
                    if ck.get(q, 0) < m:
                        ck[q] = m
            ck[p] = n

    def _record(self, opid, reads, writes):
        for k in reads:
            self.state.setdefault(k, [None, []])[1].append(opid)
        for k in writes:
            self.state[k] = [opid, []]

    def op(self, eng, fn, reads=(), writes=()):
        self._wait(eng, self._deps(reads, writes))
        ins = fn(self.e[eng])
        self.cnt[eng] += 1
        ins.then_inc(self.sem[eng], 1)
        opid = (eng, self.cnt[eng])
        self.opclock[opid] = dict(self.clock[eng])
        self._record(opid, reads, writes)

    def dma(self, q, out, in_, reads=(), writes=()):
        pool = self.dma_pool[q]
        s = pool[self.dma_rr[q] % len(pool)]
        self.dma_rr[q] += 1
        deps = self._deps(reads, writes)
        if self.cnt[s] > 0 and deps.get(s, 0) < self.cnt[s]:
            deps[s] = self.cnt[s]
        self._wait(q, deps)
        ins = self.e[q].dma_start(out=out, in_=in_)
        self.cnt[s] += 16
        ins.then_inc(self.sem[s], 16)
        opid = (s, self.cnt[s])
        self.opclock[opid] = dict(self.clock[q])
        self._record(opid, reads, writes)

    def wait_keys(self, eng, keys):
        self._wait(eng, self._deps(keys, ()))

    def wait_readers(self, eng, keys):
        self._wait(eng, self._deps((), keys))

    def barrier(self):
        allk = list(self.state.keys())
        deps = self._deps((), allk)
        for eng in self.ENGS:
            self._wait(eng, dict(deps))


def build_program(t_tokens=T):
    assert t_tokens == T
    nc = bass.Bass("TRN2", target_bir_lowering=False)
    dt_in = lambda name, shape: nc.dram_tensor(name, shape, F32, kind="ExternalInput").ap()
    x_d = dt_in("x", [T, D])
    gmix_d = dt_in("norm_mix_g", [D])
    win_d = dt_in("w_in", [D, INW])
    wga_d = dt_in("w_gate_a", [16, 64, 64])
    wgx_d = dt_in("w_gate_x", [16, 64, 64])
    vec_d = dt_in("vecs", [128, NV * 8])
    wout_d = dt_in("w_out", [2 * D, D])
    gmlp_d = dt_in("norm_mlp_g", [D])
    wup_d = dt_in("w_mlp_up", [D, DFF])
    wdn_d = dt_in("w_mlp_down", [DFF, D])
    gfin_d = dt_in("norm_final_g", [D])
    out_d = nc.dram_tensor("out", [T, D], F32, kind="ExternalOutput").ap()

    win_r = win_d.rearrange("(k p) e -> p k e", p=128)
    wout_r = wout_d.rearrange("(k p) e -> p k e", p=128)
    wup_r = wup_d.rearrange("(k p) e -> p k e", p=128)
    wdn_r = wdn_d.rearrange("(k p) e -> p k e", p=128)

    with ExitStack() as st:
        S = Sched(nc, st)
        SB = lambda name, shape, dt: st.enter_context(nc.sbuf_tensor(name, shape, dt))
        hT = SB("hT", [128, 8, T], BF16)
        mix = SB("mix", [128, 16 * T], BF16)
        big = SB("big", [128, 16 * D], F32)
        wo = SB("wo", [128, 16 * D], BF16)
        aux = SB("aux", [128, 4096], BF16)
        wst = [aux[:, i * 2048:(i + 1) * 2048].rearrange("p (k e) -> p k e", k=8) for i in range(2)]
        vec = SB("vec", [128, NV, 8], F32)
        dv = SB("dv", [128, 6, 8], F32)
        gbc = aux[:, 2048:4096].bitcast(F32)
        ident = SB("ident", [128, 128], BF16)
        identf = SB("identf", [128, 128], F32)
        maskf = SB("maskf", [128, 2, 128], F32)
        mask = SB("mask", [128, 2, 128], BF16)
        onec = SB("onec", [128, 1], BF16)
        stat = SB("stat", [128, 4, NT], F32)
        hn = [aux[:, i * 1024:(i + 1) * 1024] for i in range(2)]

        mixv = mix[:].rearrange("p (e t) -> p e t", e=16)
        xres = big[:].rearrange("p (j d) -> p j d", j=NT)
        wov = wo[:].rearrange("p (e d) -> p e d", e=16)

        bigbf = big[:].bitcast(BF16)

        def f32s(off, n):
            return big[:, off:off + n]
        XL = f32s(0, 8 + T)
        o_ = 8 + T
        CH = {}
        for nm in ("A", "W", "GX", "GG"):
            CH[nm] = f32s(o_, T)
            o_ += T
        tmp = {}
        for nm, nb_ in (("xc", 2), ("ut", 2), ("tr", 1), ("ti", 1), ("hh", 2)):
            for b in range(nb_):
                tmp[(nm, b)] = f32s(o_, 512)
                o_ += 512
        ob = 2 * o_
        xcb = [bigbf[:, ob + i * 512: ob + (i + 1) * 512] for i in range(2)]
        ob += 1024
        ysq = [bigbf[:, ob + i * 512: ob + (i + 1) * 512] for i in range(2)]
        ob += 1024
        assert ob <= 32768
        wg_a = SB("wg_a", [128, 8, 128], BF16)
        wg_x = SB("wg_x", [128, 8, 128], BF16)
        WH = 8192
        kTd = wo[:, WH:WH + 2 * T].rearrange("p (h t) -> p h t", h=2)
        vaug = wo[:, WH + 2 * T:WH + 2 * T + NT * 128].rearrange("p (j h m) -> p j h m", j=NT, h=2)
        wq = [wo[:, WH + 6144 + i * 1024:WH + 6144 + (i + 1) * 1024].rearrange("p (k e) -> p k e", k=8) for i in range(2)]
        qT = mixv[:, 8:16, :]
        ptb = [aux[:, i * 1024:(i + 1) * 1024].rearrange("p (e f) -> p e f", e=2) for i in range(2)]
        rden = [aux[:, 2048 + i * 512:2048 + (i + 1) * 512].bitcast(F32) for i in range(2)]
        yab = [aux[:, 3072 + i * 256:3072 + (i + 1) * 256] for i in range(2)]

        PS = [st.enter_context(nc.psum_tensor("ps%d" % i, [128, 512], F32)) for i in range(8)]
        PT = PS[7][:].bitcast(BF16)

        S.dma("sp", vec[:].rearrange("p v c -> p (v c)"), vec_d, writes=["vec"])
        S.dma("sp", gbc, gmix_d.partition_broadcast(128), writes=["gbc"])
        S.op("pool", lambda e: e.memset(identf[:], 1.0), writes=["identf"])
        S.op("pool", lambda e: e.affine_select(out=identf[:], in_=identf[:], pattern=[[-1, 128]],
                                                 compare_op=ALU.is_equal, fill=0.0, base=0, channel_multiplier=1),
             reads=["identf"], writes=["identf"])
        S.op("pool", lambda e: e.memset(maskf[:], 1.0), writes=["maskf"])
        S.op("pool", lambda e: e.affine_select(out=maskf[:, 0, :], in_=maskf[:, 0, :], pattern=[[-1, 128]],
                                                 compare_op=ALU.is_gt, fill=0.0, base=0, channel_multiplier=1),
             reads=["maskf"], writes=["maskf"])
        S.op("pool", lambda e: e.affine_select(out=maskf[:, 1, :], in_=maskf[:, 1, :], pattern=[[1, 128]],
                                                 compare_op=ALU.is_ge, fill=0.0, base=0, channel_multiplier=-1),
             reads=["maskf"], writes=["maskf"])
        S.op("dve", lambda e: e.tensor_copy(out=ident[:], in_=identf[:]), reads=["identf"], writes=["ident"])
        S.op("dve", lambda e: e.tensor_copy(out=mask[:], in_=maskf[:]), reads=["maskf"], writes=["mask"])
        S.op("dve", lambda e: e.memset(onec[:], 1.0), writes=["onec"])
        S.op("dve", lambda e: e.memset(stat[:], 0.0), writes=["stat"])
        S.op("dve", lambda e: e.memset(wg_a[:], 0.0), writes=["wg_a"])
        S.op("dve", lambda e: e.memset(wg_x[:], 0.0), writes=["wg_x"])
        for gd, gt, nm in ((wga_d, wg_a, "wg_a"), (wgx_d, wg_x, "wg_x")):
            gr = gd.rearrange("(c t) i j -> t i c j", t=2)
            for t2 in range(2):
                S.dma("pool", gt[t2 * 64:(t2 + 1) * 64, :, t2 * 64:(t2 + 1) * 64], gr[t2], writes=[nm])
        S.op("dve", lambda e: e.tensor_scalar(out=dv[:, 0, :], in0=vec[:, V_BA, :], scalar1=0.5, scalar2=None,
                                              op0=ALU.mult), reads=["vec"], writes=["dv0"])
        S.op("dve", lambda e: e.tensor_scalar(out=dv[:, 1, :], in0=vec[:, V_BX, :], scalar1=0.5, scalar2=None,
                                              op0=ALU.mult), reads=["vec"], writes=["dv1"])
        S.op("act", lambda e: e.activation(out=dv[:, 5, :], in_=vec[:, V_LAM, :], func=AF.Exp, scale=-1.0),
             reads=["vec"], writes=["dv5"])
        S.op("act", lambda e: e.activation(out=dv[:, 5, :], in_=dv[:, 5, :], func=AF.Ln, bias=1.0),
             reads=["dv5"], writes=["dv5"])
        S.op("dve", lambda e: e.tensor_scalar(out=dv[:, 2, :], in0=dv[:, 5, :], scalar1=-4.0, scalar2=None,
                                              op0=ALU.mult), reads=["dv5"], writes=["dv2"])
        S.op("dve", lambda e: e.tensor_scalar(out=dv[:, 3, :], in0=dv[:, 5, :], scalar1=-8.0, scalar2=None,
                                              op0=ALU.mult), reads=["dv5"], writes=["dv3"])
        S.op("act", lambda e: e.activation(out=dv[:, 4, :], in_=vec[:, V_SINK, :], func=AF.Exp),
             reads=["vec"], writes=["dv4"])
        CONST_R = ["vec", "dv0", "dv1", "dv2", "dv3", "dv4"]

        def rstd_from_ss(ss_ap, out_ap, key_r, key_w):
            S.op("dve", lambda e: e.tensor_scalar(out=out_ap, in0=ss_ap, scalar1=1.0 / D, scalar2=EPS,
                                                  op0=ALU.mult, op1=ALU.add), reads=[key_r, "stat"], writes=[key_w])
            S.op("act", lambda e: e.activation(out=out_ap, in_=out_ap, func=AF.Ln), reads=[key_w], writes=[key_w])
            S.op("act", lambda e: e.activation(out=out_ap, in_=out_ap, func=AF.Exp, scale=-0.5),
                 reads=[key_w], writes=[key_w])

        S.dma("pool", wq[0], win_r[:, :, 0:128], writes=["wq0"])
        S.dma("pool", wq[1], win_r[:, :, D:D + 128], writes=["wq1"])
        TRB = ((4, 5), (6, 7))

        def tr_mm(b):
            banks = TRB[b]

            def f(e):
                ins = None
                for k in range(8):
                    ins = e.matmul(PS[banks[k // 4]][:, (k % 4) * 128:(k % 4 + 1) * 128],
                                   lhsT=hn[b][:, k * 128:(k + 1) * 128], rhs=ident[:], start=True, stop=True)
                return ins
            S.op("pe", f, reads=["hn%d" % b, "ident"], writes=["PS%d" % banks[0], "PS%d" % banks[1]])

        def tr_evac(b, j):
            banks = TRB[b]
            S.op("act", lambda e: e.activation(out=hT[:, 0:4, j * 128:(j + 1) * 128],
                                               in_=PS[banks[0]][:].rearrange("p (k t) -> p k t", k=4), func=AF.Copy),
                 reads=["PS%d" % banks[0]], writes=["hT%d" % j])
            S.op("act", lambda e: e.activation(out=hT[:, 4:8, j * 128:(j + 1) * 128],
                                               in_=PS[banks[1]][:].rearrange("p (k t) -> p k t", k=4), func=AF.Copy),
                 reads=["PS%d" % banks[1]], writes=["hT%d" % j])

        for j in range(NT):
            S.dma("sp" if j % 2 == 0 else "act", xres[:, j, :], x_d[j * 128:(j + 1) * 128, :], writes=["xt0_%d" % j])
        junk0 = wo[:, 0:1024]

        def P0A(g):
            for j in range(g * 4, g * 4 + 4):
                S.op("act", lambda e, j=j: e.activation(out=junk0, in_=xres[:, j, :], func=AF.Square,
                                                        accum_out=stat[:, 0, j:j + 1]),
                     reads=["xt0_%d" % j, "stat"], writes=["p0ss%d" % g])
            rstd_from_ss(stat[:, 0, g * 4:g * 4 + 4], stat[:, 1, g * 4:g * 4 + 4], "p0ss%d" % g, "p0rs%d" % g)

        def P0T1(j):
            b = j % 2
            ptv = (PS[7] if b == 0 else PS[6])[:].bitcast(BF16)
            S.op("dve", lambda e: e.scalar_tensor_tensor(out=hn[b], in0=xres[:, j, :], scalar=stat[:, 1, j:j + 1], in1=gbc,
                                                         op0=ALU.mult, op1=ALU.mult),
                 reads=["xt0_%d" % j, "p0rs%d" % (j // 4), "gbc"], writes=["hn%d" % b])

            def tr(e):
                ins = None
                for k in range(8):
                    ins = e.transpose(out=ptv[:, k * 128:(k + 1) * 128], in_=hn[b][:, k * 128:(k + 1) * 128],
                                      identity=ident[:])
                return ins
            S.op("pe", tr, reads=["hn%d" % b, "ident"], writes=["PS%d" % (7 - b)])

        def P0T2(j):
            b = j % 2
            ptv = (PS[7] if b == 0 else PS[6])[:].bitcast(BF16)
            S.op("dve", lambda e: e.tensor_copy(out=hT[:, :, j * 128:(j + 1) * 128],
                                                in_=ptv.rearrange("p (k t) -> p k t", k=8)),
                 reads=["PS%d" % (7 - b)], writes=["hT%d" % j])
        P0A(0)
        P0A(1)
        for j in range(NT + 1):
            if j < NT:
                P0T1(j)
            if j >= 1:
                P0T2(j - 1)
            if j < NT and j % 4 == 3 and j // 4 + 2 < 4:
                P0A(j // 4 + 2)

        S.barrier()
        S.op("dve", lambda e: e.memset(XL[:, 0:8], 0.0), writes=["XLhalo"])

        def hT_keys(tt):
            return ["hT%d" % j for j in range(tt * 4, tt * 4 + 4)]

        wslot = [0]

        def load_w_cols(col_specs):
            s = wslot[0] % 2
            wslot[0] += 1
            for (c0, n, d0) in col_specs:
                S.dma("pool", wst[s][:, :, d0:d0 + n], win_r[:, :, c0:c0 + n], writes=["wst%d" % s])
            return s

        def proj(ps_ap, s, d0, tt, m=128):
            def f(e):
                ins = None
                for k in range(8):
                    ins = e.matmul(ps_ap, lhsT=wst[s][:, k, d0:d0 + m], rhs=hT[:, k, tt * 512:(tt + 1) * 512],
                                   start=(k == 0), stop=(k == 7))
                return ins
            return f

        for e4 in range(0, 2):
            S.dma("pool", wov[:, e4 * 4:(e4 + 1) * 4, :], wout_r[:, e4 * 4:(e4 + 1) * 4, :], writes=["wo_lo"])

        C_GELU = 0.7978845608028654
        steps = [(c, tt) for c in range(8) for tt in range(NTT)]
        slots = {}

        def lru_load(c):
            slots[c] = load_w_cols([(c * 128, 128, 0), (D + c * 128, 128, 128)])

        XBS, GBS, ZA, ZX, SSB, PJ = (0, 0), (1, 2, 3, 7), 4, 5, 6, 6

        def projw(ps_ap, w3, tt):
            def f(e):
                ins = None
                for k in range(8):
                    ins = e.matmul(ps_ap, lhsT=w3[:, k, :], rhs=hT[:, k, tt * 512:(tt + 1) * 512],
                                   start=(k == 0), stop=(k == 7))
                return ins
            return f

        def S0(n):
            c, tt = steps[n]
            gb = GBS[n % 4]
            XB = XBS[n % 2]
            if c == 0:
                S.op("pe", projw(PS[XB][:], wq[0], tt), reads=["wq0"] + hT_keys(tt), writes=["PS%d" % XB])
                S.op("pe", projw(PS[gb][:], wq[1], tt), reads=["wq1"] + hT_keys(tt), writes=["PS%d" % gb])
                return
            s_ = slots[c]
            S.op("pe", proj(PS[XB][:], s_, 0, tt), reads=["wst%d" % s_] + hT_keys(tt), writes=["PS%d" % XB])
            S.op("pe", proj(PS[gb][:], s_, 128, tt), reads=["wst%d" % s_] + hT_keys(tt), writes=["PS%d" % gb])

        def S1a(n):
            c, tt = steps[n]
            b = n % 2
            XB = XBS[n % 2]
            xps, gps = PS[XB], PS[GBS[n % 4]]
            kx, kg = "PS%d" % XB, "PS%d" % GBS[n % 4]
            t0 = 8 + tt * 512
            xc, ut = tmp[("xc", b)], tmp[("ut", b)]
            kxc, kut = "xc%d" % b, "ut%d" % b
            S.op("act", lambda e: e.activation(out=XL[:, t0:t0 + 512], in_=xps[:], func=AF.Copy),
                 reads=[kx], writes=["XL%d" % tt])
            S.op("act", lambda e: e.activation(out=xc, in_=xps[:], func=AF.Identity,
                                               bias=vec[:, V_CB, c:c + 1], scale=vec[:, V_CW3, c:c + 1]),
                 reads=[kx] + CONST_R, writes=[kxc])
            S.op("act", lambda e: e.activation(out=ut, in_=gps[:], func=AF.Square, scale=0.21145921),
                 reads=[kg], writes=[kut])
            xlk = ["XL%d" % tt, "XLhalo"] + (["XL%d" % (tt - 1)] if tt > 0 else [])
            for kk, sh in ((V_CW2, 1), (V_CW1, 2), (V_CW0, 3)):
                S.op("dve", lambda e, kk=kk, sh=sh: e.scalar_tensor_tensor(
                    out=xc, in0=XL[:, t0 - sh:t0 - sh + 512], scalar=vec[:, kk, c:c + 1], in1=xc,
                    op0=ALU.mult, op1=ALU.add), reads=xlk + [kxc] + CONST_R, writes=[kxc])
            S.op("dve", lambda e: e.scalar_tensor_tensor(out=ut, in0=ut, scalar=1.0, in1=gps[:], op0=ALU.add, op1=ALU.mult),
                 reads=[kut, kg], writes=[kut])

        def S1b(n):
            c, tt = steps[n]
            b = n % 2
            xc = tmp[("xc", b)]
            S.op("act", lambda e: e.activation(out=xcb[b], in_=xc, func=AF.Copy), reads=["xc%d" % b], writes=["xcb%d" % b])
            S.op("pe", lambda e: e.matmul(PS[ZA][:], lhsT=wg_a[:, c, :], rhs=xcb[b], start=True, stop=True),
                 reads=["wg_a", "xcb%d" % b], writes=["PS%d" % ZA])
            S.op("pe", lambda e: e.matmul(PS[ZX][:], lhsT=wg_x[:, c, :], rhs=xcb[b], start=True, stop=True),
                 reads=["wg_x", "xcb%d" % b], writes=["PS%d" % ZX])

        def S2(n):
            c, tt = steps[n]
            b = n % 2
            gps, kg = PS[GBS[n % 4]], "PS%d" % GBS[n % 4]
            sl = slice(tt * 512, (tt + 1) * 512)
            xc, ut, tr, ti = tmp[("xc", b)], tmp[("ut", b)], tmp[("tr", 0)], tmp[("ti", 0)]
            A, W, GX, GG = CH["A"][:, sl], CH["W"][:, sl], CH["GX"][:, sl], CH["GG"][:, sl]
            kA, kW, kGX, kGG = "A%d" % tt, "W%d" % tt, "GX%d" % tt, "GG%d" % tt
            S.op("act", lambda e: e.activation(out=tr, in_=PS[ZA][:], func=AF.Tanh, bias=dv[:, 0, c:c + 1], scale=0.5),
                 reads=["PS%d" % ZA] + CONST_R, writes=["tr"])
            S.op("act", lambda e: e.activation(out=ti, in_=PS[ZX][:], func=AF.Tanh, bias=dv[:, 1, c:c + 1], scale=0.5),
                 reads=["PS%d" % ZX] + CONST_R, writes=["ti"])
            S.op("act", lambda e: e.activation(out=ut, in_=ut, func=AF.Tanh, scale=C_GELU), reads=["ut%d" % b], writes=["ut%d" % b])
            S.op("act", lambda e: e.activation(out=A, in_=tr, func=AF.Exp, bias=dv[:, 2, c:c + 1], scale=dv[:, 2, c:c + 1]),
                 reads=["tr"] + CONST_R, writes=[kA])
            S.op("dve", lambda e: e.scalar_tensor_tensor(out=GX, in0=ti, scalar=1.0, in1=xc, op0=ALU.add, op1=ALU.mult),
                 reads=["ti", "xc%d" % b], writes=[kGX])
            S.op("dve", lambda e: e.scalar_tensor_tensor(out=W, in0=A, scalar=-1.0, in1=A, op0=ALU.mult, op1=ALU.mult),
                 reads=[kA], writes=[kW])
            S.op("dve", lambda e: e.scalar_tensor_tensor(out=GG, in0=ut, scalar=1.0, in1=gps[:], op0=ALU.add, op1=ALU.mult),
                 reads=["ut%d" % b, kg], writes=[kGG])

        def LNEXP(c):
            keys = ["W%d" % tt for tt in range(NTT)]
            S.op("act", lambda e: e.activation(out=CH["W"], in_=CH["W"], func=AF.Sqrt, bias=1.0, scale=1.0),
                 reads=keys, writes=keys)

        def S3(n):
            c, tt = steps[n]
            b = n % 2
            sl = slice(tt * 512, (tt + 1) * 512)
            A, W, GX, GG = CH["A"][:, sl], CH["W"][:, sl], CH["GX"][:, sl], CH["GG"][:, sl]
            kA, kW, kGX, kGG = "A%d" % tt, "W%d" % tt, "GX%d" % tt, "GG%d" % tt
            hh = tmp[("hh", b)]
            S.op("dve", lambda e: e.tensor_tensor(out=GX, in0=GX, in1=W, op=ALU.mult), reads=[kGX, kW], writes=[kGX])
            init = 0.0 if tt == 0 else tmp[("hh", 1 - b)][:, 511:512]
            S.op("dve", lambda e: e.tensor_tensor_scan(out=hh, data0=A, data1=GX, initial=init, op0=ALU.mult, op1=ALU.add),
                 reads=[kA, kGX, "hh%d" % (1 - b)], writes=["hh%d" % b])
            S.op("dve", lambda e: e.scalar_tensor_tensor(out=mixv[:, c, tt * 512:(tt + 1) * 512], in0=hh, scalar=0.25, in1=GG,
                                                         op0=ALU.mult, op1=ALU.mult),
                 reads=["hh%d" % b, kGG], writes=["mix_%d_%d" % (c, tt)])

        def S4(n):
            c, tt = steps[n]
            b = n % 2
            hh = tmp[("hh", b)]
            S.op("act", lambda e: e.activation(out=ysq[b], in_=mixv[:, c, tt * 512:(tt + 1) * 512], func=AF.Square),
                 reads=["mix_%d_%d" % (c, tt)], writes=["ysq%d" % b])

            def ssmm(e):
                ins = None
                for t4 in range(4):
                    ins = e.matmul(PS[SSB][:, t4:t4 + 1], lhsT=ysq[b][:, t4 * 128:(t4 + 1) * 128], rhs=onec[:, 0:1],
                                   start=True, stop=True)
                return ins
            S.op("pe", ssmm, reads=["onec", "ysq%d" % b], writes=["PS%d" % SSB])
            dst = stat[:, 2, tt * 4:(tt + 1) * 4]
            S.op("dve", lambda e: e.tensor_tensor(out=dst, in0=PS[SSB][:, 0:4], in1=dst, op=ALU.add),
                 reads=["PS%d" % SSB, "ssl%d" % tt, "stat"], writes=["ssl%d" % tt])

        s3 = 2 * D + D
        s4 = s3 + 128
        xjobs = []
        wqslot = [0]

        def xload(col_specs):
            sq = wqslot[0] % 2
            wqslot[0] += 1

            def f():
                for (c0, n_, d0) in col_specs:
                    S.dma("pool", wq[sq][:, :, d0:d0 + n_], win_r[:, :, c0:c0 + n_], writes=["wq%d" % sq])
            return sq, f

        def xproj_T(sq, tt, dst_ap, dst_key, scale):
            def pe_f(bank):
                def mm(e):
                    ins = None
                    for k in range(8):
                        ins = e.matmul(PS[bank][:], lhsT=wq[sq][:, k, :], rhs=hT[:, k, tt * 512:(tt + 1) * 512],
                                       start=(k == 0), stop=(k == 7))
                    return ins
                S.op("pe", mm, reads=["wq%d" % sq] + hT_keys(tt), writes=["PS%d" % bank])

            def act_f(bank):
                S.op("act", lambda e: e.activation(out=dst_ap, in_=PS[bank][:], func=AF.Copy, scale=scale),
                     reads=["PS%d" % bank], writes=[dst_key])
            return pe_f, act_f

        def xproj_V(sq, j4):
            def pe_f(bank):
                def mm(e):
                    ins = None
                    for jj in range(4):
                        j = j4 * 4 + jj
                        for k in range(8):
                            ins = e.matmul(PS[bank][:, jj * 128:(jj + 1) * 128], lhsT=hT[:, k, j * 128:(j + 1) * 128],
                                           rhs=wq[sq][:, k, :], start=(k == 0), stop=(k == 7))
                    return ins
                S.op("pe", mm, reads=["wq%d" % sq] + hT_keys(j4), writes=["PS%d" % bank])

            def act_f(bank):
                S.op("act", lambda e: e.activation(
                    out=vaug[:, j4 * 4:(j4 + 1) * 4, :, :].rearrange("p j h m -> p j (h m)"),
                    in_=PS[bank][:].rearrange("p (j f) -> p j f", j=4), func=AF.Copy),
                    reads=["PS%d" % bank], writes=["vaug"])
            return pe_f, act_f

        for h in range(2):
            sq, f = xload([(s3 + h * 64, 64, 0), (s3 + h * 64, 64, 64)])
            xjobs.append(("load", f))
            for tt in range(NTT):
                xjobs.append(("proj", xproj_T(sq, tt, kTd[:, h, tt * 512:(tt + 1) * 512], "kTd", 1.0)))
        sq, f = xload([(s4, 128, 0)])
        xjobs.append(("load", f))
        for j4 in range(4):
            xjobs.append(("proj", xproj_V(sq, j4)))
        for c in range(8):
            sq, f = xload([(2 * D + c * 128, 128, 0)])
            xjobs.append(("load", f))
            for tt in range(NTT):
                xjobs.append(("proj", xproj_T(sq, tt, qT[:, c, tt * 512:(tt + 1) * 512], "qT%d" % c, 0.125)))

        xstate = {"i": 0, "pend": None}

        def xjob_act():
            if xstate["pend"] is not None:
                act_f, bank = xstate["pend"]
                act_f(bank)
                xstate["pend"] = None

        def xjob_pe(bank):
            while xstate["i"] < len(xjobs):
                kind, job = xjobs[xstate["i"]]
                xstate["i"] += 1
                if kind == "load":
                    job()
                    continue
                pe_f, act_f = job
                pe_f(bank)
                xstate["pend"] = (act_f, bank)
                return

        NS = len(steps)
        LAG3, LAG4 = 6, 7
        ATT_START = NS + 1

        def lru_emit(att_step):
            lru_load(1)
            lru_load(2)
            S0(0)
            for s_ in range(NS + LAG4 + 1):
                if s_ < ATT_START:
                    xjob_act()
                if s_ >= LAG3 and (s_ - LAG3) % 4 == 0 and (s_ - LAG3) // 4 < 8:
                    LNEXP((s_ - LAG3) // 4)
                if 0 <= s_ - LAG3 < NS:
                    S3(s_ - LAG3)
                if 0 <= s_ - LAG4 < NS:
                    S4(s_ - LAG4)
                if 0 <= s_ - 2 < NS:
                    S2(s_ - 2)
                if 3 <= s_ < ATT_START - 1:
                    xjob_pe(PJ)
                if 0 <= s_ - 1 < NS:
                    S1b(s_ - 1)
                if s_ < NS:
                    S1a(s_)
                if s_ + 1 < NS:
                    S0(s_ + 1)
                if s_ < NS:
                    c, tt = steps[s_]
                    if tt == 3 and c >= 1 and c + 2 < 8:
                        lru_load(c + 2)
                if s_ >= ATT_START:
                    att_step()
                    att_step()

        onesw = SB("onesw", [128, 64], BF16)
        S.op("dve", lambda e: e.memset(onesw[:], 1.0), writes=["onesw"])
        kv = {"k": kTd, "v": vaug, "kk": "kTd", "vk": "vaug"}
        its = [(c, tt, m2) for c in range(8) for tt in range(NTT) for m2 in range(2)]
        NI = len(its)
        maskb = mask[:].rearrange("p k q -> p (k q)").unsqueeze(1).broadcast_to([128, 4, 256])

        def AQ(i):
            c, tt, m2 = its[i]
            h = c // 4
            pb = i % 2
            n0 = tt * 4 + m2 * 2

            def qk(e):
                ins = None
                segs = [(max(n0 - 1, 0), 0, n0 * 128, 128), (n0, 128, n0 * 128, 256), (n0 + 1, 384, (n0 + 1) * 128, 128)]
                for (kblk, col, q0, nq) in segs:
                    for ee in range(2):
                        ins = e.matmul(PS[pb * 2 + ee][:, col:col + nq],
                                       lhsT=kv["k"][ee * 64:(ee + 1) * 64, h, kblk * 128:(kblk + 1) * 128],
                                       rhs=qT[ee * 64:(ee + 1) * 64, c, q0:q0 + nq], start=True, stop=True)
                return ins
            S.op("pe", qk, reads=[kv["kk"], "qT%d" % c, "qblk%d" % i], writes=["PS%d" % (pb * 2), "PS%d" % (pb * 2 + 1)])
            for ee in range(2):
                S.op("act", lambda e, ee=ee: e.activation(out=ptb[pb][:, ee, :], in_=PS[pb * 2 + ee][:], func=AF.Exp),
                     reads=["PS%d" % (pb * 2 + ee)], writes=["pt%d_%d" % (pb, ee)])
            ptv = ptb[pb][:].rearrange("p e (n f) -> p (e n) f", n=2)
            S.op("dve", lambda e: e.tensor_tensor(out=ptv, in0=ptv, in1=maskb, op=ALU.mult),
                 reads=["pt%d_0" % pb, "pt%d_1" % pb, "mask"], writes=["pt%d_0" % pb, "pt%d_1" % pb])

        def AV(i):
            c, tt, m2 = its[i]
            h = c // 4
            pb = i % 2
            od = PS[4 + pb]
            n0 = tt * 4 + m2 * 2

            def pv(e):
                ins = None
                merged_den = n0 > 0
                if merged_den:
                    for ee in range(2):
                        o_ee = od[ee * 64:(ee + 1) * 64, 0:256]
                        e.matmul(o_ee[:, 0:128], lhsT=kv["v"][:, n0 - 1, h, :], rhs=ptb[pb][:, ee, 0:128], start=True, stop=False,
                                 skip_group_check=True)
                        e.matmul(o_ee[:, 0:256], lhsT=kv["v"][:, n0, h, :], rhs=ptb[pb][:, ee, 128:384], start=False, stop=False,
                                 skip_group_check=True)
                        e.matmul(o_ee[:, 128:256], lhsT=kv["v"][:, n0 + 1, h, :], rhs=ptb[pb][:, ee, 384:512],
                                 start=False, stop=True, skip_group_check=True)
                for part in range(0 if merged_den else 2):
                    for nn in range(2):
                        n = n0 + nn
                        col = part * 256 + nn * 128
                        for ee in range(2):
                            kbs = [1] if n == 0 else [0, 1]
                            for idx, kb in enumerate(kbs):
                                kblk = n - 1 + kb
                                rhs = ptb[pb][:, ee, (nn * 2 + kb) * 128:(nn * 2 + kb + 1) * 128]
                                lhsT = kv["v"][:, kblk, h, :] if part == 0 else onesw[:, :]
                                ins = e.matmul(od[ee * 64:(ee + 1) * 64, col:col + 128], lhsT=lhsT, rhs=rhs,
                                               start=(idx == 0), stop=(idx == len(kbs) - 1))
                if merged_den:
                    for ee in range(2):
                        pt4 = ptb[pb][:, ee, :].rearrange("p (n k q) -> p n k q", n=2, k=2)
                        for kb in range(2):
                            ins = e.matmul(od[ee * 64:(ee + 1) * 64, 256:512].rearrange("p (n q) -> p n q", n=2),
                                           lhsT=onesw[:, :], rhs=pt4[:, :, kb, :], start=(kb == 0), stop=(kb == 1))
                return ins
            S.op("pe", pv, reads=[kv["vk"], "onesw", "pt%d_0" % pb, "pt%d_1" % pb], writes=["PS%d" % (4 + pb)])

        def AN1(i):
            c, tt, m2 = its[i]
            pb = i % 2
            od = PS[4 + pb]
            rd = rden[pb]
            ya = rd
            S.op("act", lambda e: e.activation(out=rd, in_=od[:, 256:512], func=AF.Ln, bias=dv[:, 4, c:c + 1]),
                 reads=["PS%d" % (4 + pb)] + CONST_R, writes=["rden%d" % pb])
            S.op("act", lambda e: e.activation(out=rd, in_=rd, func=AF.Exp, scale=-1.0),
                 reads=["rden%d" % pb], writes=["rden%d" % pb])
            cs = slice(tt * 512 + m2 * 256, tt * 512 + m2 * 256 + 256)
            S.op("dve", lambda e: e.tensor_tensor(out=mixv[:, 8 + c, cs], in0=od[:, 0:256], in1=rd, op=ALU.mult),
                 reads=["PS%d" % (4 + pb), "rden%d" % pb], writes=["qblk%d" % i])

        def AN2(i):
            c, tt, m2 = its[i]
            pb = i % 2
            cs = slice(tt * 512 + m2 * 256, tt * 512 + m2 * 256 + 256)
            ya, yb = rden[pb], yab[pb]
            S.op("act", lambda e: e.activation(out=yb, in_=mixv[:, 8 + c, cs], func=AF.Square),
                 reads=["qblk%d" % i], writes=["yab%d" % pb])

            def ssmm2(e):
                ins = None
                for t2 in range(2):
                    ins = e.matmul(PS[6][:, t2:t2 + 1], lhsT=yb[:, t2 * 128:(t2 + 1) * 128], rhs=onec[:, 0:1],
                                   start=True, stop=True)
                return ins
            S.op("pe", ssmm2, reads=["onec", "yab%d" % pb], writes=["PS6"])
            dst = stat[:, 3, tt * 4 + m2 * 2:tt * 4 + m2 * 2 + 2]
            S.op("dve", lambda e: e.tensor_tensor(out=dst, in0=PS[6][:, 0:2], in1=dst, op=ALU.add),
                 reads=["PS6", "ssa%d_%d" % (tt, m2), "stat"], writes=["ssa%d_%d" % (tt, m2)])

        mixw = mix[:].rearrange("p (s w f) -> p s w f", s=2, w=2)

        def mlp_load(q):
            sl = q % 2
            wup_s = mixw[:, sl, 0, :].rearrange("p (k f) -> p k f", k=8)
            wdn_s = mixw[:, sl, 1, :].rearrange("p (k d) -> p k d", k=8)
            for k2 in range(2):
                S.dma("pool", wup_s[:, k2 * 4:(k2 + 1) * 4, :], wup_r[:, k2 * 4:(k2 + 1) * 4, q * 1024:(q + 1) * 1024],
                      writes=["wup%d" % sl, "mixhalf%d" % sl])
                S.dma("pool", wdn_s[:, k2 * 4:(k2 + 1) * 4, :], wdn_r[:, q * 8 + k2 * 4:q * 8 + (k2 + 1) * 4, :],
                      writes=["wdn%d" % sl, "mixhalf%d" % sl])

        att_state = {"step": 0, "lru_done": False, "h": 0, "pend": None, "started": False}

        def wol_start():
            S.wait_keys("dve", ["ssl%d" % t_ for t_ in range(NTT)])
            rstd_from_ss(stat[:, 2, :], stat[:, 2, :], "ssl_all", "rl")
            for k in range(8):
                S.op("dve", lambda e, k=k: e.tensor_scalar(out=wov[:, k, :], in0=wov[:, k, :], scalar1=vec[:, V_GL, k:k + 1],
                                                           scalar2=None, op0=ALU.mult),
                     reads=["wo_lo"] + CONST_R, writes=["wo_s%d" % k])
            hkeys = ["hT%d" % j_ for j_ in range(NT)]
            S.wait_readers("dve", hkeys)
            S.op("dve", lambda e: e.tensor_copy(out=hT[:, 0:2, :], in_=kTd), reads=["kTd"], writes=["kTd2"])
            S.op("dve", lambda e: e.tensor_copy(out=hT[:, 2, :], in_=wo[:, WH + 2 * T:WH + 2 * T + NT * 128]),
                 reads=["vaug"], writes=["vaug2"])
            kv["k"], kv["kk"] = hT[:, 0:2, :], "kTd2"
            kv["v"], kv["vk"] = hT[:, 2, :].rearrange("p (j h m) -> p j h m", j=NT, h=2), "vaug2"
            S._wait("sp", S._deps((), list(S.state.keys())))
            for j in range(NT):
                S.dma("sp", xres[:, j, :], x_d[j * 128:(j + 1) * 128, :], writes=["xres%d" % j])
            for e4 in range(2, 4):
                S.dma("pool", wov[:, e4 * 4:(e4 + 1) * 4, :], wout_r[:, e4 * 4:(e4 + 1) * 4, :],
                      writes=["wo_hi", "kTd", "vaug", "wq0", "wq1"])

        def wol_dve():
            if att_state["pend"] is not None:
                h = att_state["pend"]
                j, hf = h // 2, h % 2
                S.op("dve", lambda e: e.scalar_tensor_tensor(
                    out=xres[:, j, hf * 512:(hf + 1) * 512], in0=PS[7][:], scalar=stat[:, 2, j:j + 1],
                    in1=xres[:, j, hf * 512:(hf + 1) * 512], op0=ALU.mult, op1=ALU.add),
                    reads=["PS7", "rl", "xres%d" % j], writes=["xres%d" % j])
                att_state["pend"] = None
                if h == 2 * NT - 1:
                    mlp_load(0)

        def wol_pe():
            h = att_state["h"]
            if h >= 2 * NT:
                return
            att_state["h"] += 1
            j, hf = h // 2, h % 2

            def mm(e):
                ins = None
                for k in range(8):
                    ins = e.matmul(PS[7][:], lhsT=mixv[:, k, j * 128:(j + 1) * 128], rhs=wov[:, k, hf * 512:(hf + 1) * 512],
                                   start=(k == 0), stop=(k == 7))
                return ins
            S.op("pe", mm, reads=["wo_s%d" % k for k in range(8)] + ["mixhalf0"], writes=["PS7"])
            att_state["pend"] = h

        def att_step():
            step = att_state["step"]
            if step >= NI + 3:
                return
            att_state["step"] += 1
            if step == 0:
                for eng_ in ("act", "dve", "pool"):
                    S.wait_readers(eng_, ["wst0", "wst1"])
            xjobs_done = xstate["i"] >= len(xjobs) and xstate["pend"] is None
            if att_state["lru_done"] and xjobs_done and not att_state["started"]:
                att_state["started"] = True
                wol_start()
            if att_state["started"]:
                wol_dve()
            xjob_act()
            if 0 <= step - 3 < NI:
                AN2(step - 3)
            if 0 <= step - 2 < NI:
                AN1(step - 2)
            if step < NI:
                AQ(step)
            if 0 <= step - 1 < NI:
                AV(step - 1)
            xjob_pe(7)
            if att_state["started"]:
                wol_pe()

        lru_emit(att_step)
        att_state["lru_done"] = True
        while att_state["step"] < NI + 3:
            att_step()
        assert xstate["i"] >= len(xjobs) and xstate["pend"] is None
        assert att_state["started"]
        while att_state["h"] < 2 * NT or att_state["pend"] is not None:
            wol_dve()
            wol_pe()
        S.barrier()
        S.dma("sp", gbc, gmlp_d.partition_broadcast(128), writes=["gbc"])
        rstd_from_ss(stat[:, 3, :], stat[:, 3, :], "ssrow_r", "rla")
        for k in range(8, 16):
            S.op("dve", lambda e, k=k: e.tensor_scalar(out=wov[:, k, :], in0=wov[:, k, :], scalar1=vec[:, V_GA, k - 8:k - 7],
                                                       scalar2=None, op0=ALU.mult),
                 reads=["wo_hi"] + CONST_R, writes=["wo_s%d" % k])
        NG = NT

        def PA(g):
            br, j = 1, g
            bk = (g % 2) * 2

            def wo_mm(e):
                ins = None
                for hf in range(2):
                    for k in range(8):
                        ins = e.matmul(PS[bk + hf][:], lhsT=mixv[:, br * 8 + k, j * 128:(j + 1) * 128],
                                       rhs=wov[:, br * 8 + k, hf * 512:(hf + 1) * 512], start=(k == 0), stop=(k == 7))
                return ins
            S.op("pe", wo_mm, reads=["wo_s%d" % (br * 8 + k) for k in range(8)] + ["mixhalf%d" % br], writes=["PS%d" % bk, "PS%d" % (bk + 1)])

        def PB(g):
            br, j = 1, g
            bk = (g % 2) * 2
            for hf in range(2):
                S.op("dve", lambda e, hf=hf: e.scalar_tensor_tensor(
                    out=xres[:, j, hf * 512:(hf + 1) * 512], in0=PS[bk + hf][:], scalar=stat[:, 2 + br, j:j + 1],
                    in1=xres[:, j, hf * 512:(hf + 1) * 512], op0=ALU.mult, op1=ALU.add),
                    reads=["PS%d" % (bk + hf), "rla", "xres%d" % j], writes=["xres%d" % j])

        def PC1a(j):
            b = j % 2
            S.op("act", lambda e: e.activation(out=hn[b], in_=xres[:, j, :], func=AF.Square, accum_out=stat[:, 0, j:j + 1]),
                 reads=["xres%d" % j, "stat"], writes=["hn%d" % b, "p2ss%d" % j])

        def PC1c(j):
            rstd_from_ss(stat[:, 0, j:j + 1], stat[:, 1, j:j + 1], "p2ss%d" % j, "p2rs%d" % j)

        def PC1b(j):
            b = j % 2
            S.op("dve", lambda e: e.scalar_tensor_tensor(out=hn[b], in0=xres[:, j, :], scalar=stat[:, 1, j:j + 1], in1=gbc,
                                                         op0=ALU.mult, op1=ALU.mult),
                 reads=["xres%d" % j, "p2rs%d" % j, "gbc"], writes=["hn%d" % b])

        def PC2(j):
            b = j % 2
            tr_mm(b)
            tr_evac(b, j)

        for g in range(NG + 4):
            if 0 <= g - 3 < NG:
                PC2(g - 3)
            if g < NG:
                PA(g)
            if 0 <= g - 1 < NG:
                PB(g - 1)
                PC1a(g - 1)
            if 0 <= g - 2 < NG:
                PC1b(g - 2)
            if 0 <= g - 1 < NG:
                PC1c(g - 1)

        S.barrier()
        S.dma("sp", gbc, gfin_d.partition_broadcast(128), writes=["gbc"])
        actb = wo[:].rearrange("p (s f) -> p s f", s=2)
        NQ = 4
        units = [(q, tt) for q in range(NQ) for tt in range(NTT)]

        def wviews(q):
            sl = q % 2
            return (sl, mixw[:, sl, 0, :].rearrange("p (k f) -> p k f", k=8),
                    mixw[:, sl, 1, :].rearrange("p (k d) -> p k d", k=8))

        def MUP(u):
            q, tt = units[u]
            sl, wup_s, wdn_s = wviews(q)
            ab = u % 2
            act_s = actb[:, ab, 0:4096].rearrange("p (c t) -> p c t", c=8)
            for fc in range(8):
                ub = fc % 2

                def up(e, fc=fc, ub=ub):
                    ins = None
                    for k in range(8):
                        ins = e.matmul(PS[ub][:], lhsT=wup_s[:, k, fc * 128:(fc + 1) * 128],
                                       rhs=hT[:, k, tt * 512:(tt + 1) * 512], start=(k == 0), stop=(k == 7))
                    return ins
                S.op("pe", up, reads=["wup%d" % sl] + hT_keys(tt), writes=["PS%d" % ub])
                S.op("act", lambda e, ub=ub: e.activation(out=hn[ub].bitcast(F32), in_=PS[ub][:], func=AF.Relu),
                     reads=["PS%d" % ub], writes=["relu%d" % ub])
                S.op("dve", lambda e, ub=ub, fc=fc: e.tensor_tensor(out=act_s[:, fc, :], in0=hn[ub].bitcast(F32),
                                                                    in1=hn[ub].bitcast(F32), op=ALU.mult),
                     reads=["relu%d" % ub], writes=["act%d_%d" % (ab, fc)])

        def MDN(u):
            q, tt = units[u]
            sl, wup_s, wdn_s = wviews(q)
            ab = u % 2
            act_s = actb[:, ab, 0:4096].rearrange("p (c t) -> p c t", c=8)
            for t4 in range(4):
                j = tt * 4 + t4
                db = 2 + (t4 % 2) * 2

                def dn(e, t4=t4, db=db):
                    ins = None
                    for hf in range(2):
                        for fc in range(8):
                            ins = e.matmul(PS[db + hf][:], lhsT=act_s[:, fc, t4 * 128:(t4 + 1) * 128],
                                           rhs=wdn_s[:, fc, hf * 512:(hf + 1) * 512], start=(fc == 0), stop=(fc == 7))
                    return ins
                S.op("pe", dn, reads=["wdn%d" % sl] + ["act%d_%d" % (ab, fc) for fc in range(8)],
                     writes=["PS%d" % db, "PS%d" % (db + 1)])
                for hf in range(2):
                    S.op("dve", lambda e, hf=hf, db=db, j=j: e.tensor_tensor(
                        out=xres[:, j, hf * 512:(hf + 1) * 512], in0=PS[db + hf][:],
                        in1=xres[:, j, hf * 512:(hf + 1) * 512], op=ALU.add),
                        reads=["PS%d" % (db + hf), "xres%d" % j], writes=["xres%d" % j])
                if q == NQ - 1:
                    fj = wo[:, 4096:5120]
                    S.op("act", lambda e, j=j: e.activation(out=fj, in_=xres[:, j, :], func=AF.Square,
                                                            accum_out=stat[:, 0, j:j + 1]),
                         reads=["xres%d" % j, "stat"], writes=["fss%d" % j])
                    fin_flush()
                    rstd_from_ss(stat[:, 0, j:j + 1], stat[:, 1, j:j + 1], "fss%d" % j, "frs%d" % j)
                    fin_state["pend"] = j

        fin_state = {"pend": None}

        def fin_flush():
            j = fin_state["pend"]
            if j is None:
                return
            fin_state["pend"] = None
            S.op("dve", lambda e: e.scalar_tensor_tensor(
                out=xres[:, j, :], in0=xres[:, j, :], scalar=stat[:, 1, j:j + 1], in1=gbc,
                op0=ALU.mult, op1=ALU.mult),
                reads=["xres%d" % j, "frs%d" % j, "gbc"], writes=["xres%d" % j])
            S.dma("sp", out_d[j * 128:(j + 1) * 128, :], xres[:, j, :], reads=["xres%d" % j], writes=["out%d" % j])

        NU = len(units)
        mlp_load(1)
        MUP(0)
        for u in range(NU):
            if u + 1 < NU:
                MUP(u + 1)
            MDN(u)
            q, tt = units[u]
            if tt == NTT - 1 and q + 2 < NQ:
                mlp_load(q + 2)
        fin_flush()
        S.wait_keys("sp", ["out%d" % j for j in range(NT)])
    return nc


def _pack_vecs(inp):
    fm = lambda v: np.ascontiguousarray(np.asarray(v, np.float32).reshape(8, 128).T)
    cw = np.asarray(inp["conv_w"], np.float32)[0]
    vs = [fm(cw[0]), fm(cw[1]), fm(cw[2]), fm(cw[3]), fm(inp["conv_b"][0]), fm(inp["b_gate_a"][0]),
          fm(inp["b_gate_x"][0]), fm(inp["lru_lambda"][0]), fm(inp["lru_out_g"][0]), fm(inp["attn_out_g"][0]),
          fm(np.repeat(np.asarray(inp["attn_sinks"], np.float32)[0], 64))]
    return np.ascontiguousarray(np.stack(vs, axis=1).reshape(128, NV * 8))


_NC_CACHE = {}


def kernel(**inputs):
    x = np.asarray(inputs["x"], np.float32)
    nb = x.shape[0]
    if "nc" not in _NC_CACHE:
        _NC_CACHE["nc"] = build_program()
    nc = _NC_CACHE["nc"]
    f = lambda k: np.ascontiguousarray(np.asarray(inputs[k], np.float32)[0])
    shared = {
        "norm_mix_g": f("norm_mix_g"), "w_in": f("w_in"), "w_gate_a": f("w_gate_a"), "w_gate_x": f("w_gate_x"),
        "vecs": _pack_vecs(inputs), "w_out": f("w_out"), "norm_mlp_g": f("norm_mlp_g"),
        "w_mlp_up": f("w_mlp_up"), "w_mlp_down": f("w_mlp_down"),
        "norm_final_g": np.ascontiguousarray(np.asarray(inputs["norm_final_g"], np.float32)),
    }
    in_maps = [dict(shared, x=np.ascontiguousarray(x[b])) for b in range(nb)]
    res = run_bass_kernel_spmd(nc, in_maps, core_ids=list(range(nb)))
    return np.stack([np.asarray(r["out"], np.float32) for r in res.results], axis=0)
```

```python
import numpy as np
from contextlib import ExitStack
import concourse.bass as bass
import concourse.mybir as mybir
from concourse.bass_utils import run_bass_kernel_spmd

F32 = mybir.dt.float32
BF16 = mybir.dt.bfloat16
AF = mybir.ActivationFunctionType
ALU = mybir.AluOpType

D = 1024
T = 2048
NT = T // 128
NTT = T // 512
INW = 3328
DFF = 4096
EPS = 1e-6
NV = 11
(V_CW0, V_CW1, V_CW2, V_CW3, V_CB, V_BA, V_BX, V_LAM, V_GL, V_GA, V_SINK) = range(NV)


class Sched:
    ENGS = ("pe", "act", "dve", "pool", "sp")

    def __init__(self, nc, stack, n_dma_sems=8):
        self.nc = nc
        self.e = {"pe": nc.tensor, "act": nc.scalar, "dve": nc.vector, "pool": nc.gpsimd, "sp": nc.sync}
        self.sem, self.cnt = {}, {}
        for n in self.ENGS:
            self.sem[n] = stack.enter_context(nc.semaphore("s_" + n))
            self.cnt[n] = 0
        self.dma_pool, self.dma_rr = {}, {}
        for q in ("sp", "pool", "act"):
            lst = []
            for i in range(n_dma_sems):
                nm = "d_%s%d" % (q, i)
                self.sem[nm] = stack.enter_context(nc.semaphore(nm))
                self.cnt[nm] = 0
                lst.append(nm)
            self.dma_pool[q] = lst
            self.dma_rr[q] = 0
        self.clock = {n: {} for n in self.ENGS}
        self.opclock = {}
        self.state = {}

    def _deps(self, reads, writes):
        deps = {}

        def add(d):
            if d is not None and deps.get(d[0], 0) < d[1]:
                deps[d[0]] = d[1]
        for k in reads:
            st = self.state.get(k)
            if st:
                add(st[0])
        for k in writes:
            st = self.state.get(k)
            if st:
                add(st[0])
                for r in st[1]:
                    add(r)
        return deps

    def _wait(self, eng, deps):
        ck = self.clock[eng]
        for p, n in deps.items():
            if ck.get(p, 0) >= n:
                continue
            self.e[eng].wait_ge(self.sem[p], n)
            oc = self.opclock.get((p, n))
            if oc:
                for q, m in oc.items():
                    if ck.get(q, 0) < m:
                        ck[q] = m
            ck[p] = n

    def _record(self, opid, reads, writes):
        for k in reads:
            self.state.setdefault(k, [None, []])[1].append(opid)
        for k in writes:
            self.state[k] = [opid, []]

    def op(self, eng, fn, reads=(), writes=()):
        self._wait(eng, self._deps(reads, writes))
        ins = fn(self.e[eng])
        self.cnt[eng] += 1
        ins.then_inc(self.sem[eng], 1)
        opid = (eng, self.cnt[eng])
        self.opclock[opid] = dict(self.clock[eng])
        self._record(opid, reads, writes)

    def dma(self, q, out, in_, reads=(), writes=()):
        pool = self.dma_pool[q]
        s = pool[self.dma_rr[q] % len(pool)]
        self.dma_rr[q] += 1
        deps = self._deps(reads, writes)
        if self.cnt[s] > 0 and deps.get(s, 0) < self.cnt[s]:
            deps[s] = self.cnt[s]
        self._wait(q, deps)
        ins = self.e[q].dma_start(out=out, in_=in_)
        self.cnt[s] += 16
        ins.then_inc(self.sem[s], 16)
        opid = (s, self.cnt[s])
        self.opclock[opid] = dict(self.clock[q])
        self._record(opid, reads, writes)

    def wait_keys(self, eng, keys):
        self._wait(eng, self._deps(keys, ()))

    def wait_readers(self, eng, keys):
        self._wait(eng, self._deps((), keys))

    def barrier(self):
        allk = list(self.state.keys())
        deps = self._deps((), allk)
        for eng in self.ENGS:
            self._wait(eng, dict(deps))


def build_program(t_tokens=T):
    assert t_tokens == T
    nc = bass.Bass("TRN2", target_bir_lowering=False)
    dt_in = lambda name, shape: nc.dram_tensor(name, shape, F32, kind="ExternalInput").ap()
    x_d = dt_in("x", [T, D])
    gmix_d = dt_in("norm_mix_g", [D])
    win_d = dt_in("w_in", [D, INW])
    wga_d = dt_in("w_gate_a", [16, 64, 64])
    wgx_d = dt_in("w_gate_x", [16, 64, 64])
    vec_d = dt_in("vecs", [128, NV * 8])
    wout_d = dt_in("w_out", [2 * D, D])
    gmlp_d = dt_in("norm_mlp_g", [D])
    wup_d = dt_in("w_mlp_up", [D, DFF])
    wdn_d = dt_in("w_mlp_down", [DFF, D])
    gfin_d = dt_in("norm_final_g", [D])
    out_d = nc.dram_tensor("out", [T, D], F32, kind="ExternalOutput").ap()

    win_r = win_d.rearrange("(k p) e -> p k e", p=128)
    wout_r = wout_d.rearrange("(k p) e -> p k e", p=128)
    wup_r = wup_d.rearrange("(k p) e -> p k e", p=128)
    wdn_r = wdn_d.rearrange("(k p) e -> p k e", p=128)

    with ExitStack() as st:
        S = Sched(nc, st)
        SB = lambda name, shape, dt: st.enter_context(nc.sbuf_tensor(name, shape, dt))
        hT = SB("hT", [128, 8, T], BF16)
        mix = SB("mix", [128, 16 * T], BF16)
        big = SB("big", [128, 16 * D], F32)
        wo = SB("wo", [128, 16 * D], BF16)
        aux = SB("aux", [128, 4096], BF16)
        wst = [aux[:, i * 2048:(i + 1) * 2048].rearrange("p (k e) -> p k e", k=8) for i in range(2)]
        vec = SB("vec", [128, NV, 8], F32)
        dv = SB("dv", [128, 6, 8], F32)
        gbc = aux[:, 2048:4096].bitcast(F32)
        ident = SB("ident", [128, 128], BF16)
        identf = SB("identf", [128, 128], F32)
        maskf = SB("maskf", [128, 2, 128], F32)
        mask = SB("mask", [128, 2, 128], BF16)
        onec = SB("onec", [128, 1], BF16)
        stat = SB("stat", [128, 4, NT], F32)
        hn = [aux[:, i * 1024:(i + 1) * 1024] for i in range(2)]

        mixv = mix[:].rearrange("p (e t) -> p e t", e=16)
        xres = big[:].rearrange("p (j d) -> p j d", j=NT)
        wov = wo[:].rearrange("p (e d) -> p e d", e=16)

        bigbf = big[:].bitcast(BF16)

        def f32s(off, n):
            return big[:, off:off + n]
        XL = f32s(0, 8 + T)
        o_ = 8 + T
        CH = {}
        for nm in ("A", "W", "GX", "GG"):
            CH[nm] = f32s(o_, T)
            o_ += T
        tmp = {}
        for nm, nb_ in (("xc", 2), ("ut", 2), ("tr", 1), ("ti", 1), ("hh", 2)):
            for b in range(nb_):
                tmp[(nm, b)] = f32s(o_, 512)
                o_ += 512
        ob = 2 * o_
        xcb = [bigbf[:, ob + i * 512: ob + (i + 1) * 512] for i in range(2)]
        ob += 1024
        ysq = [bigbf[:, ob + i * 512: ob + (i + 1) * 512] for i in range(2)]
        ob += 1024
        assert ob <= 32768
        wg_a = SB("wg_a", [128, 8, 128], BF16)
        wg_x = SB("wg_x", [128, 8, 128], BF16)
        WH = 8192
        kTd = wo[:, WH:WH + 2 * T].rearrange("p (h t) -> p h t", h=2)
        vaug = wo[:, WH + 2 * T:WH + 2 * T + NT * 128].rearrange("p (j h m) -> p j h m", j=NT, h=2)
        wq = [wo[:, WH + 6144 + i * 1024:WH + 6144 + (i + 1) * 1024].rearrange("p (k e) -> p k e", k=8) for i in range(2)]
        qT = mixv[:, 8:16, :]
        ptb = [aux[:, i * 1024:(i + 1) * 1024].rearrange("p (e f) -> p e f", e=2) for i in range(2)]
        rden = [aux[:, 2048 + i * 512:2048 + (i + 1) * 512].bitcast(F32) for i in range(2)]
        yab = [aux[:, 3072 + i * 256:3072 + (i + 1) * 256] for i in range(2)]

        PS = [st.enter_context(nc.psum_tensor("ps%d" % i, [128, 512], F32)) for i in range(8)]
        PT = PS[7][:].bitcast(BF16)

        S.dma("sp", vec[:].rearrange("p v c -> p (v c)"), vec_d, writes=["vec"])
        S.dma("sp", gbc, gmix_d.partition_broadcast(128), writes=["gbc"])
        S.op("pool", lambda e: e.memset(identf[:], 1.0), writes=["identf"])
        S.op("pool", lambda e: e.affine_select(out=identf[:], in_=identf[:], pattern=[[-1, 128]],
                                                 compare_op=ALU.is_equal, fill=0.0, base=0, channel_multiplier=1),
             reads=["identf"], writes=["identf"])
        S.op("pool", lambda e: e.memset(maskf[:], 1.0), writes=["maskf"])
        S.op("pool", lambda e: e.affine_select(out=maskf[:, 0, :], in_=maskf[:, 0, :], pattern=[[-1, 128]],
                                                 compare_op=ALU.is_gt, fill=0.0, base=0, channel_multiplier=1),
             reads=["maskf"], writes=["maskf"])
        S.op("pool", lambda e: e.affine_select(out=maskf[:, 1, :], in_=maskf[:, 1, :], pattern=[[1, 128]],
                                                 compare_op=ALU.is_ge, fill=0.0, base=0, channel_multiplier=-1),
             reads=["maskf"], writes=["maskf"])
        S.op("dve", lambda e: e.tensor_copy(out=ident[:], in_=identf[:]), reads=["identf"], writes=["ident"])
        S.op("dve", lambda e: e.tensor_copy(out=mask[:], in_=maskf[:]), reads=["maskf"], writes=["mask"])
        S.op("dve", lambda e: e.memset(onec[:], 1.0), writes=["onec"])
        S.op("dve", lambda e: e.memset(stat[:], 0.0), writes=["stat"])
        S.op("dve", lambda e: e.memset(wg_a[:], 0.0), writes=["wg_a"])
        S.op("dve", lambda e: e.memset(wg_x[:], 0.0), writes=["wg_x"])
        for gd, gt, nm in ((wga_d, wg_a, "wg_a"), (wgx_d, wg_x, "wg_x")):
            gr = gd.rearrange("(c t) i j -> t i c j", t=2)
            for t2 in range(2):
                S.dma("pool", gt[t2 * 64:(t2 + 1) * 64, :, t2 * 64:(t2 + 1) * 64], gr[t2], writes=[nm])
        S.op("dve", lambda e: e.tensor_scalar(out=dv[:, 0, :], in0=vec[:, V_BA, :], scalar1=0.5, scalar2=None,
                                              op0=ALU.mult), reads=["vec"], writes=["dv0"])
        S.op("dve", lambda e: e.tensor_scalar(out=dv[:, 1, :], in0=vec[:, V_BX, :], scalar1=0.5, scalar2=None,
                                              op0=ALU.mult), reads=["vec"], writes=["dv1"])
        S.op("act", lambda e: e.activation(out=dv[:, 5, :], in_=vec[:, V_LAM, :], func=AF.Exp, scale=-1.0),
             reads=["vec"], writes=["dv5"])
        S.op("act", lambda e: e.activation(out=dv[:, 5, :], in_=dv[:, 5, :], func=AF.Ln, bias=1.0),
             reads=["dv5"], writes=["dv5"])
        S.op("dve", lambda e: e.tensor_scalar(out=dv[:, 2, :], in0=dv[:, 5, :], scalar1=-4.0, scalar2=None,
                                              op0=ALU.mult), reads=["dv5"], writes=["dv2"])
        S.op("dve", lambda e: e.tensor_scalar(out=dv[:, 3, :], in0=dv[:, 5, :], scalar1=-8.0, scalar2=None,
                                              op0=ALU.mult), reads=["dv5"], writes=["dv3"])
        S.op("act", lambda e: e.activation(out=dv[:, 4, :], in_=vec[:, V_SINK, :], func=AF.Exp),
             reads=["vec"], writes=["dv4"])
        CONST_R = ["vec", "dv0", "dv1", "dv2", "dv3", "dv4"]

        def rstd_from_ss(ss_ap, out_ap, key_r, key_w):
            S.op("dve", lambda e: e.tensor_scalar(out=out_ap, in0=ss_ap, scalar1=1.0 / D, scalar2=EPS,
                                                  op0=ALU.mult, op1=ALU.add), reads=[key_r, "stat"], writes=[key_w])
            S.op("act", lambda e: e.activation(out=out_ap, in_=out_ap, func=AF.Ln), reads=[key_w], writes=[key_w])
            S.op("act", lambda e: e.activation(out=out_ap, in_=out_ap, func=AF.Exp, scale=-0.5),
                 reads=[key_w], writes=[key_w])

        S.dma("pool", wq[0], win_r[:, :, 0:128], writes=["wq0"])
        S.dma("pool", wq[1], win_r[:, :, D:D + 128], writes=["wq1"])
        TRB = ((4, 5), (6, 7))

        def tr_mm(b):
            banks = TRB[b]

            def f(e):
                ins = None
                for k in range(8):
                    ins = e.matmul(PS[banks[k // 4]][:, (k % 4) * 128:(k % 4 + 1) * 128],
                                   lhsT=hn[b][:, k * 128:(k + 1) * 128], rhs=ident[:], start=True, stop=True)
                return ins
            S.op("pe", f, reads=["hn%d" % b, "ident"], writes=["PS%d" % banks[0], "PS%d" % banks[1]])

        def tr_evac(b, j):
            banks = TRB[b]
            S.op("act", lambda e: e.activation(out=hT[:, 0:4, j * 128:(j + 1) * 128],
                                               in_=PS[banks[0]][:].rearrange("p (k t) -> p k t", k=4), func=AF.Copy),
                 reads=["PS%d" % banks[0]], writes=["hT%d" % j])
            S.op("act", lambda e: e.activation(out=hT[:, 4:8, j * 128:(j + 1) * 128],
                                               in_=PS[banks[1]][:].rearrange("p (k t) -> p k t", k=4), func=AF.Copy),
                 reads=["PS%d" % banks[1]], writes=["hT%d" % j])

        for j in range(NT):
            S.dma("sp" if j % 2 == 0 else "act", xres[:, j, :], x_d[j * 128:(j + 1) * 128, :], writes=["xt0_%d" % j])
        junk0 = wo[:, 0:1024]

        def P0As(g):
            for j in range(g * 4, g * 4 + 4):
                S.op("act", lambda e, j=j: e.activation(out=junk0, in_=xres[:, j, :], func=AF.Square,
                                                        accum_out=stat[:, 0, j:j + 1]),
                     reads=["xt0_%d" % j, "stat"], writes=["p0ss%d" % g])

        def P0Ar(g):
            rstd_from_ss(stat[:, 0, g * 4:g * 4 + 4], stat[:, 1, g * 4:g * 4 + 4], "p0ss%d" % g, "p0rs%d" % g)

        def P0T1(j):
            b = j % 2
            ptv = (PS[7] if b == 0 else PS[6])[:].bitcast(BF16)
            S.op("dve", lambda e: e.scalar_tensor_tensor(out=hn[b], in0=xres[:, j, :], scalar=stat[:, 1, j:j + 1], in1=gbc,
                                                         op0=ALU.mult, op1=ALU.mult),
                 reads=["xt0_%d" % j, "p0rs%d" % (j // 4), "gbc"], writes=["hn%d" % b])

            def tr(e):
                ins = None
                for k in range(8):
                    ins = e.transpose(out=ptv[:, k * 128:(k + 1) * 128], in_=hn[b][:, k * 128:(k + 1) * 128],
                                      identity=ident[:])
                return ins
            S.op("pe", tr, reads=["hn%d" % b, "ident"], writes=["PS%d" % (7 - b)])

        def P0T2(j):
            b = j % 2
            ptv = (PS[7] if b == 0 else PS[6])[:].bitcast(BF16)
            S.op("dve", lambda e: e.tensor_copy(out=hT[:, :, j * 128:(j + 1) * 128],
                                                in_=ptv.rearrange("p (k t) -> p k t", k=8)),
                 reads=["PS%d" % (7 - b)], writes=["hT%d" % j])
        P0As(0)
        P0Ar(0)
        P0As(1)
        for j in range(NT + 1):
            if j < NT:
                P0T1(j)
            if j >= 1:
                P0T2(j - 1)
            if j == 1:
                P0Ar(1)
            if j < NT and j % 4 == 3 and j // 4 + 2 < 4:
                P0As(j // 4 + 2)
            if j >= 5 and j % 4 == 1 and j // 4 + 1 < 4:
                P0Ar(j // 4 + 1)

        S.barrier()
        S.op("dve", lambda e: e.memset(XL[:, 0:8], 0.0), writes=["XLhalo"])

        def hT_keys(tt):
            return ["hT%d" % j for j in range(tt * 4, tt * 4 + 4)]

        wslot = [0]

        def load_w_cols(col_specs):
            s = wslot[0] % 2
            wslot[0] += 1
            for (c0, n, d0) in col_specs:
                S.dma("pool", wst[s][:, :, d0:d0 + n], win_r[:, :, c0:c0 + n], writes=["wst%d" % s])
            return s

        def proj(ps_ap, s, d0, tt, m=128):
            def f(e):
                ins = None
                for k in range(8):
                    ins = e.matmul(ps_ap, lhsT=wst[s][:, k, d0:d0 + m], rhs=hT[:, k, tt * 512:(tt + 1) * 512],
                                   start=(k == 0), stop=(k == 7))
                return ins
            return f

        for e4 in range(0, 2):
            S.dma("pool", wov[:, e4 * 4:(e4 + 1) * 4, :], wout_r[:, e4 * 4:(e4 + 1) * 4, :], writes=["wo_lo"])

        C_GELU = 0.7978845608028654
        steps = [(c, tt) for c in range(8) for tt in range(NTT)]
        slots = {}

        def lru_load(c):
            slots[c] = load_w_cols([(c * 128, 128, 0), (D + c * 128, 128, 128)])

        XBS, GBS, ZA, ZX, SSB, PJ = (0, 0), (1, 2, 3, 7), 4, 5, 6, 6

        def projw(ps_ap, w3, tt):
            def f(e):
                ins = None
                for k in range(8):
                    ins = e.matmul(ps_ap, lhsT=w3[:, k, :], rhs=hT[:, k, tt * 512:(tt + 1) * 512],
                                   start=(k == 0), stop=(k == 7))
                return ins
            return f

        def S0(n):
            c, tt = steps[n]
            gb = GBS[n % 4]
            XB = XBS[n % 2]
            if c == 0:
                S.op("pe", projw(PS[XB][:], wq[0], tt), reads=["wq0"] + hT_keys(tt), writes=["PS%d" % XB])
                S.op("pe", projw(PS[gb][:], wq[1], tt), reads=["wq1"] + hT_keys(tt), writes=["PS%d" % gb])
                return
            s_ = slots[c]
            S.op("pe", proj(PS[XB][:], s_, 0, tt), reads=["wst%d" % s_] + hT_keys(tt), writes=["PS%d" % XB])
            S.op("pe", proj(PS[gb][:], s_, 128, tt), reads=["wst%d" % s_] + hT_keys(tt), writes=["PS%d" % gb])

        def S1a(n):
            c, tt = steps[n]
            b = n % 2
            XB = XBS[n % 2]
            xps, gps = PS[XB], PS[GBS[n % 4]]
            kx, kg = "PS%d" % XB, "PS%d" % GBS[n % 4]
            t0 = 8 + tt * 512
            xc, ut = tmp[("xc", b)], tmp[("ut", b)]
            kxc, kut = "xc%d" % b, "ut%d" % b
            S.op("act", lambda e: e.activation(out=XL[:, t0:t0 + 512], in_=xps[:], func=AF.Copy),
                 reads=[kx], writes=["XL%d" % tt])
            S.op("act", lambda e: e.activation(out=xc, in_=xps[:], func=AF.Identity,
                                               bias=vec[:, V_CB, c:c + 1], scale=vec[:, V_CW3, c:c + 1]),
                 reads=[kx] + CONST_R, writes=[kxc])
            S.op("act", lambda e: e.activation(out=ut, in_=gps[:], func=AF.Square, scale=0.21145921),
                 reads=[kg], writes=[kut])
            xlk = ["XL%d" % tt, "XLhalo"] + (["XL%d" % (tt - 1)] if tt > 0 else [])
            for kk, sh in ((V_CW2, 1), (V_CW1, 2), (V_CW0, 3)):
                S.op("dve", lambda e, kk=kk, sh=sh: e.scalar_tensor_tensor(
                    out=xc, in0=XL[:, t0 - sh:t0 - sh + 512], scalar=vec[:, kk, c:c + 1], in1=xc,
                    op0=ALU.mult, op1=ALU.add), reads=xlk + [kxc] + CONST_R, writes=[kxc])
            S.op("dve", lambda e: e.scalar_tensor_tensor(out=ut, in0=ut, scalar=1.0, in1=gps[:], op0=ALU.add, op1=ALU.mult),
                 reads=[kut, kg], writes=[kut])

        def S1b(n):
            c, tt = steps[n]
            b = n % 2
            xc = tmp[("xc", b)]
            S.op("act", lambda e: e.activation(out=xcb[b], in_=xc, func=AF.Copy), reads=["xc%d" % b], writes=["xcb%d" % b])
            S.op("pe", lambda e: e.matmul(PS[ZA][:], lhsT=wg_a[:, c, :], rhs=xcb[b], start=True, stop=True),
                 reads=["wg_a", "xcb%d" % b], writes=["PS%d" % ZA])
            S.op("pe", lambda e: e.matmul(PS[ZX][:], lhsT=wg_x[:, c, :], rhs=xcb[b], start=True, stop=True),
                 reads=["wg_x", "xcb%d" % b], writes=["PS%d" % ZX])

        def S2(n):
            c, tt = steps[n]
            b = n % 2
            gps, kg = PS[GBS[n % 4]], "PS%d" % GBS[n % 4]
            sl = slice(tt * 512, (tt + 1) * 512)
            xc, ut, tr, ti = tmp[("xc", b)], tmp[("ut", b)], tmp[("tr", 0)], tmp[("ti", 0)]
            A, W, GX, GG = CH["A"][:, sl], CH["W"][:, sl], CH["GX"][:, sl], CH["GG"][:, sl]
            kA, kW, kGX, kGG = "A%d" % tt, "W%d" % tt, "GX%d" % tt, "GG%d" % tt
            S.op("act", lambda e: e.activation(out=tr, in_=PS[ZA][:], func=AF.Tanh, bias=dv[:, 0, c:c + 1], scale=0.5),
                 reads=["PS%d" % ZA] + CONST_R, writes=["tr"])
            S.op("act", lambda e: e.activation(out=ti, in_=PS[ZX][:], func=AF.Tanh, bias=dv[:, 1, c:c + 1], scale=0.5),
                 reads=["PS%d" % ZX] + CONST_R, writes=["ti"])
            S.op("act", lambda e: e.activation(out=ut, in_=ut, func=AF.Tanh, scale=C_GELU), reads=["ut%d" % b], writes=["ut%d" % b])
            S.op("act", lambda e: e.activation(out=A, in_=tr, func=AF.Exp, bias=dv[:, 2, c:c + 1], scale=dv[:, 2, c:c + 1]),
                 reads=["tr"] + CONST_R, writes=[kA])
            S.op("dve", lambda e: e.scalar_tensor_tensor(out=GX, in0=ti, scalar=1.0, in1=xc, op0=ALU.add, op1=ALU.mult),
                 reads=["ti", "xc%d" % b], writes=[kGX])
            S.op("dve", lambda e: e.scalar_tensor_tensor(out=W, in0=A, scalar=-1.0, in1=A, op0=ALU.mult, op1=ALU.mult),
                 reads=[kA], writes=[kW])
            S.op("dve", lambda e: e.scalar_tensor_tensor(out=GG, in0=ut, scalar=1.0, in1=gps[:], op0=ALU.add, op1=ALU.mult),
                 reads=["ut%d" % b, kg], writes=[kGG])

        def LNEXP(c):
            keys = ["W%d" % tt for tt in range(NTT)]
            S.op("act", lambda e: e.activation(out=CH["W"], in_=CH["W"], func=AF.Sqrt, bias=1.0, scale=1.0),
                 reads=keys, writes=keys)

        def S3(n):
            c, tt = steps[n]
            b = n % 2
            sl = slice(tt * 512, (tt + 1) * 512)
            A, W, GX, GG = CH["A"][:, sl], CH["W"][:, sl], CH["GX"][:, sl], CH["GG"][:, sl]
            kA, kW, kGX, kGG = "A%d" % tt, "W%d" % tt, "GX%d" % tt, "GG%d" % tt
            hh = tmp[("hh", b)]
            S.op("dve", lambda e: e.tensor_tensor(out=GX, in0=GX, in1=W, op=ALU.mult), reads=[kGX, kW], writes=[kGX])
            init = 0.0 if tt == 0 else tmp[("hh", 1 - b)][:, 511:512]
            S.op("dve", lambda e: e.tensor_tensor_scan(out=hh, data0=A, data1=GX, initial=init, op0=ALU.mult, op1=ALU.add),
                 reads=[kA, kGX, "hh%d" % (1 - b)], writes=["hh%d" % b])
            S.op("dve", lambda e: e.scalar_tensor_tensor(out=mixv[:, c, tt * 512:(tt + 1) * 512], in0=hh, scalar=0.25, in1=GG,
                                                         op0=ALU.mult, op1=ALU.mult),
                 reads=["hh%d" % b, kGG], writes=["mix_%d_%d" % (c, tt)])

        def S4(n):
            c, tt = steps[n]
            b = n % 2
            hh = tmp[("hh", b)]
            S.op("act", lambda e: e.activation(out=ysq[b], in_=mixv[:, c, tt * 512:(tt + 1) * 512], func=AF.Square),
                 reads=["mix_%d_%d" % (c, tt)], writes=["ysq%d" % b])

            def ssmm(e):
                ins = None
                for t4 in range(4):
                    ins = e.matmul(PS[SSB][:, t4:t4 + 1], lhsT=ysq[b][:, t4 * 128:(t4 + 1) * 128], rhs=onec[:, 0:1],
                                   start=True, stop=True)
                return ins
            S.op("pe", ssmm, reads=["onec", "ysq%d" % b], writes=["PS%d" % SSB])
            dst = stat[:, 2, tt * 4:(tt + 1) * 4]
            S.op("dve", lambda e: e.tensor_tensor(out=dst, in0=PS[SSB][:, 0:4], in1=dst, op=ALU.add),
                 reads=["PS%d" % SSB, "ssl%d" % tt, "stat"], writes=["ssl%d" % tt])

        s3 = 2 * D + D
        s4 = s3 + 128
        xjobs = []
        wqslot = [0]

        def xload(col_specs):
            sq = wqslot[0] % 2
            wqslot[0] += 1

            def f():
                for (c0, n_, d0) in col_specs:
                    S.dma("pool", wq[sq][:, :, d0:d0 + n_], win_r[:, :, c0:c0 + n_], writes=["wq%d" % sq])
            return sq, f

        def xproj_T(sq, tt, dst_ap, dst_key, scale):
            def pe_f(bank):
                def mm(e):
                    ins = None
                    for k in range(8):
                        ins = e.matmul(PS[bank][:], lhsT=wq[sq][:, k, :], rhs=hT[:, k, tt * 512:(tt + 1) * 512],
                                       start=(k == 0), stop=(k == 7))
                    return ins
                S.op("pe", mm, reads=["wq%d" % sq] + hT_keys(tt), writes=["PS%d" % bank])

            def act_f(bank):
                S.op("act", lambda e: e.activation(out=dst_ap, in_=PS[bank][:], func=AF.Copy, scale=scale),
                     reads=["PS%d" % bank], writes=[dst_key])
            return pe_f, act_f

        def xproj_V(sq, j4):
            def pe_f(bank):
                def mm(e):
                    ins = None
                    for jj in range(4):
                        j = j4 * 4 + jj
                        for k in range(8):
                            ins = e.matmul(PS[bank][:, jj * 128:(jj + 1) * 128], lhsT=hT[:, k, j * 128:(j + 1) * 128],
                                           rhs=wq[sq][:, k, :], start=(k == 0), stop=(k == 7))
                    return ins
                S.op("pe", mm, reads=["wq%d" % sq] + hT_keys(j4), writes=["PS%d" % bank])

            def act_f(bank):
                S.op("act", lambda e: e.activation(
                    out=vaug[:, j4 * 4:(j4 + 1) * 4, :, :].rearrange("p j h m -> p j (h m)"),
                    in_=PS[bank][:].rearrange("p (j f) -> p j f", j=4), func=AF.Copy),
                    reads=["PS%d" % bank], writes=["vaug"])
            return pe_f, act_f

        for h in range(2):
            sq, f = xload([(s3 + h * 64, 64, 0), (s3 + h * 64, 64, 64)])
            xjobs.append(("load", f))
            for tt in range(NTT):
                xjobs.append(("proj", xproj_T(sq, tt, kTd[:, h, tt * 512:(tt + 1) * 512], "kTd", 1.0)))
        sq, f = xload([(s4, 128, 0)])
        xjobs.append(("load", f))
        for j4 in range(4):
            xjobs.append(("proj", xproj_V(sq, j4)))
        for c in range(8):
            sq, f = xload([(2 * D + c * 128, 128, 0)])
            xjobs.append(("load", f))
            for tt in range(NTT):
                xjobs.append(("proj", xproj_T(sq, tt, qT[:, c, tt * 512:(tt + 1) * 512], "qT%d" % c, 0.125)))

        xstate = {"i": 0, "pend": None}

        def xjob_act():
            if xstate["pend"] is not None:
                act_f, bank = xstate["pend"]
                act_f(bank)
                xstate["pend"] = None

        def xjob_pe(bank):
            while xstate["i"] < len(xjobs):
                kind, job = xjobs[xstate["i"]]
                xstate["i"] += 1
                if kind == "load":
                    job()
                    continue
                pe_f, act_f = job
                pe_f(bank)
                xstate["pend"] = (act_f, bank)
                return

        NS = len(steps)
        LAG3, LAG4 = 6, 7
        ATT_START = NS + 1

        def lru_emit(att_step):
            lru_load(1)
            lru_load(2)
            S0(0)
            for s_ in range(NS + LAG4 + 1):
                if s_ < ATT_START:
                    xjob_act()
                if s_ >= LAG3 and (s_ - LAG3) % 4 == 0 and (s_ - LAG3) // 4 < 8:
                    LNEXP((s_ - LAG3) // 4)
                if 0 <= s_ - LAG3 < NS:
                    S3(s_ - LAG3)
                if 0 <= s_ - LAG4 < NS:
                    S4(s_ - LAG4)
                if 0 <= s_ - 2 < NS:
                    S2(s_ - 2)
                if 3 <= s_ < ATT_START - 1:
                    xjob_pe(PJ)
                if 0 <= s_ - 1 < NS:
                    S1b(s_ - 1)
                if s_ < NS:
                    S1a(s_)
                if s_ + 1 < NS:
                    S0(s_ + 1)
                if s_ < NS:
                    c, tt = steps[s_]
                    if tt == 3 and c >= 1 and c + 2 < 8:
                        lru_load(c + 2)
                if s_ >= ATT_START:
                    att_step()
                    att_step()

        onesw = SB("onesw", [128, 64], BF16)
        S.op("dve", lambda e: e.memset(onesw[:], 1.0), writes=["onesw"])
        kv = {"k": kTd, "v": vaug, "kk": "kTd", "vk": "vaug"}
        its = [(c, tt, m2) for c in range(8) for tt in range(NTT) for m2 in range(2)]
        NI = len(its)
        maskb = mask[:].rearrange("p k q -> p (k q)").unsqueeze(1).broadcast_to([128, 4, 256])

        def AQ(i):
            c, tt, m2 = its[i]
            h = c // 4
            pb = i % 2
            n0 = tt * 4 + m2 * 2

            def qk(e):
                ins = None
                segs = [(max(n0 - 1, 0), 0, n0 * 128, 128), (n0, 128, n0 * 128, 256), (n0 + 1, 384, (n0 + 1) * 128, 128)]
                for (kblk, col, q0, nq) in segs:
                    for ee in range(2):
                        ins = e.matmul(PS[pb * 2 + ee][:, col:col + nq],
                                       lhsT=kv["k"][ee * 64:(ee + 1) * 64, h, kblk * 128:(kblk + 1) * 128],
                                       rhs=qT[ee * 64:(ee + 1) * 64, c, q0:q0 + nq], start=True, stop=True)
                return ins
            S.op("pe", qk, reads=[kv["kk"], "qT%d" % c, "qblk%d" % i], writes=["PS%d" % (pb * 2), "PS%d" % (pb * 2 + 1)])
            for ee in range(2):
                S.op("act", lambda e, ee=ee: e.activation(out=ptb[pb][:, ee, :], in_=PS[pb * 2 + ee][:], func=AF.Exp),
                     reads=["PS%d" % (pb * 2 + ee)], writes=["pt%d_%d" % (pb, ee)])
            ptv = ptb[pb][:].rearrange("p e (n f) -> p (e n) f", n=2)
            S.op("dve", lambda e: e.tensor_tensor(out=ptv, in0=ptv, in1=maskb, op=ALU.mult),
                 reads=["pt%d_0" % pb, "pt%d_1" % pb, "mask"], writes=["pt%d_0" % pb, "pt%d_1" % pb])

        def AV(i):
            c, tt, m2 = its[i]
            h = c // 4
            pb = i % 2
            od = PS[4 + pb]
            n0 = tt * 4 + m2 * 2

            def pv(e):
                ins = None
                merged_den = n0 > 0
                if merged_den:
                    for ee in range(2):
                        o_ee = od[ee * 64:(ee + 1) * 64, 0:256]
                        e.matmul(o_ee[:, 0:128], lhsT=kv["v"][:, n0 - 1, h, :], rhs=ptb[pb][:, ee, 0:128], start=True, stop=False,
                                 skip_group_check=True)
                        e.matmul(o_ee[:, 0:256], lhsT=kv["v"][:, n0, h, :], rhs=ptb[pb][:, ee, 128:384], start=False, stop=False,
                                 skip_group_check=True)
                        e.matmul(o_ee[:, 128:256], lhsT=kv["v"][:, n0 + 1, h, :], rhs=ptb[pb][:, ee, 384:512],
                                 start=False, stop=True, skip_group_check=True)
                for part in range(0 if merged_den else 2):
                    for nn in range(2):
                        n = n0 + nn
                        col = part * 256 + nn * 128
                        for ee in range(2):
                            kbs = [1] if n == 0 else [0, 1]
                            for idx, kb in enumerate(kbs):
                                kblk = n - 1 + kb
                                rhs = ptb[pb][:, ee, (nn * 2 + kb) * 128:(nn * 2 + kb + 1) * 128]
                                lhsT = kv["v"][:, kblk, h, :] if part == 0 else onesw[:, :]
                                ins = e.matmul(od[ee * 64:(ee + 1) * 64, col:col + 128], lhsT=lhsT, rhs=rhs,
                                               start=(idx == 0), stop=(idx == len(kbs) - 1))
                if merged_den:
                    for ee in range(2):
                        pt4 = ptb[pb][:, ee, :].rearrange("p (n k q) -> p n k q", n=2, k=2)
                        for kb in range(2):
                            ins = e.matmul(od[ee * 64:(ee + 1) * 64, 256:512].rearrange("p (n q) -> p n q", n=2),
                                           lhsT=onesw[:, :], rhs=pt4[:, :, kb, :], start=(kb == 0), stop=(kb == 1))
                return ins
            S.op("pe", pv, reads=[kv["vk"], "onesw", "pt%d_0" % pb, "pt%d_1" % pb], writes=["PS%d" % (4 + pb)])

        def AN1(i):
            c, tt, m2 = its[i]
            pb = i % 2
            od = PS[4 + pb]
            rd = rden[pb]
            ya = rd
            S.op("act", lambda e: e.activation(out=rd, in_=od[:, 256:512], func=AF.Ln, bias=dv[:, 4, c:c + 1]),
                 reads=["PS%d" % (4 + pb)] + CONST_R, writes=["rden%d" % pb])
            S.op("act", lambda e: e.activation(out=rd, in_=rd, func=AF.Exp, scale=-1.0),
                 reads=["rden%d" % pb], writes=["rden%d" % pb])
            cs = slice(tt * 512 + m2 * 256, tt * 512 + m2 * 256 + 256)
            S.op("dve", lambda e: e.tensor_tensor(out=mixv[:, 8 + c, cs], in0=od[:, 0:256], in1=rd, op=ALU.mult),
                 reads=["PS%d" % (4 + pb), "rden%d" % pb], writes=["qblk%d" % i])

        def AN2(i):
            c, tt, m2 = its[i]
            pb = i % 2
            cs = slice(tt * 512 + m2 * 256, tt * 512 + m2 * 256 + 256)
            ya, yb = rden[pb], yab[pb]
            S.op("act", lambda e: e.activation(out=yb, in_=mixv[:, 8 + c, cs], func=AF.Square),
                 reads=["qblk%d" % i], writes=["yab%d" % pb])

            def ssmm2(e):
                ins = None
                for t2 in range(2):
                    ins = e.matmul(PS[6][:, t2:t2 + 1], lhsT=yb[:, t2 * 128:(t2 + 1) * 128], rhs=onec[:, 0:1],
                                   start=True, stop=True)
                return ins
            S.op("pe", ssmm2, reads=["onec", "yab%d" % pb], writes=["PS6"])
            dst = stat[:, 3, tt * 4 + m2 * 2:tt * 4 + m2 * 2 + 2]
            S.op("dve", lambda e: e.tensor_tensor(out=dst, in0=PS[6][:, 0:2], in1=dst, op=ALU.add),
                 reads=["PS6", "ssa%d_%d" % (tt, m2), "stat"], writes=["ssa%d_%d" % (tt, m2)])

        mixw = mix[:].rearrange("p (s w f) -> p s w f", s=2, w=2)

        def mlp_load(q):
            sl = q % 2
            wup_s = mixw[:, sl, 0, :].rearrange("p (k f) -> p k f", k=8)
            wdn_s = mixw[:, sl, 1, :].rearrange("p (k d) -> p k d", k=8)
            for k2 in range(2):
                S.dma("pool", wup_s[:, k2 * 4:(k2 + 1) * 4, :], wup_r[:, k2 * 4:(k2 + 1) * 4, q * 1024:(q + 1) * 1024],
                      writes=["wup%d" % sl, "mixhalf%d" % sl])
                S.dma("pool", wdn_s[:, k2 * 4:(k2 + 1) * 4, :], wdn_r[:, q * 8 + k2 * 4:q * 8 + (k2 + 1) * 4, :],
                      writes=["wdn%d" % sl, "mixhalf%d" % sl])

        att_state = {"step": 0, "lru_done": False, "h": 0, "pend": None, "started": False}

        def wol_start():
            S.wait_keys("dve", ["ssl%d" % t_ for t_ in range(NTT)])
            rstd_from_ss(stat[:, 2, :], stat[:, 2, :], "ssl_all", "rl")
            for k in range(8):
                S.op("dve", lambda e, k=k: e.tensor_scalar(out=wov[:, k, :], in0=wov[:, k, :], scalar1=vec[:, V_GL, k:k + 1],
                                                           scalar2=None, op0=ALU.mult),
                     reads=["wo_lo"] + CONST_R, writes=["wo_s%d" % k])
            hkeys = ["hT%d" % j_ for j_ in range(NT)]
            S.wait_readers("dve", hkeys)
            S.op("dve", lambda e: e.tensor_copy(out=hT[:, 0:2, :], in_=kTd), reads=["kTd"], writes=["kTd2"])
            S.op("dve", lambda e: e.tensor_copy(out=hT[:, 2, :], in_=wo[:, WH + 2 * T:WH + 2 * T + NT * 128]),
                 reads=["vaug"], writes=["vaug2"])
            kv["k"], kv["kk"] = hT[:, 0:2, :], "kTd2"
            kv["v"], kv["vk"] = hT[:, 2, :].rearrange("p (j h m) -> p j h m", j=NT, h=2), "vaug2"
            S._wait("sp", S._deps((), list(S.state.keys())))
            for j in range(NT):
                S.dma("sp", xres[:, j, :], x_d[j * 128:(j + 1) * 128, :], writes=["xres%d" % j])
            for e4 in range(2, 4):
                S.dma("pool", wov[:, e4 * 4:(e4 + 1) * 4, :], wout_r[:, e4 * 4:(e4 + 1) * 4, :],
                      writes=["wo_hi", "kTd", "vaug", "wq0", "wq1"])

        def wol_dve():
            if att_state["pend"] is not None:
                h = att_state["pend"]
                j, hf = h // 2, h % 2
                S.op("dve", lambda e: e.scalar_tensor_tensor(
                    out=xres[:, j, hf * 512:(hf + 1) * 512], in0=PS[7][:], scalar=stat[:, 2, j:j + 1],
                    in1=xres[:, j, hf * 512:(hf + 1) * 512], op0=ALU.mult, op1=ALU.add),
                    reads=["PS7", "rl", "xres%d" % j], writes=["xres%d" % j])
                att_state["pend"] = None
                if h == 2 * NT - 1:
                    mlp_load(0)

        def wol_pe():
            h = att_state["h"]
            if h >= 2 * NT:
                return
            att_state["h"] += 1
            j, hf = h // 2, h % 2

            def mm(e):
                ins = None
                for k in range(8):
                    ins = e.matmul(PS[7][:], lhsT=mixv[:, k, j * 128:(j + 1) * 128], rhs=wov[:, k, hf * 512:(hf + 1) * 512],
                                   start=(k == 0), stop=(k == 7))
                return ins
            S.op("pe", mm, reads=["wo_s%d" % k for k in range(8)] + ["mixhalf0"], writes=["PS7"])
            att_state["pend"] = h

        def att_step():
            step = att_state["step"]
            if step >= NI + 3:
                return
            att_state["step"] += 1
            if step == 0:
                for eng_ in ("act", "dve", "pool"):
                    S.wait_readers(eng_, ["wst0", "wst1"])
            xjobs_done = xstate["i"] >= len(xjobs) and xstate["pend"] is None
            if att_state["lru_done"] and xjobs_done and not att_state["started"]:
                att_state["started"] = True
                wol_start()
            if att_state["started"]:
                wol_dve()
            xjob_act()
            if 0 <= step - 3 < NI:
                AN2(step - 3)
            if 0 <= step - 2 < NI:
                AN1(step - 2)
            if step < NI:
                AQ(step)
            if 0 <= step - 1 < NI:
                AV(step - 1)
            xjob_pe(7)
            if att_state["started"]:
                wol_pe()

        lru_emit(att_step)
        att_state["lru_done"] = True
        while att_state["step"] < NI + 3:
            att_step()
        assert xstate["i"] >= len(xjobs) and xstate["pend"] is None
        assert att_state["started"]
        while att_state["h"] < 2 * NT or att_state["pend"] is not None:
            wol_dve()
            wol_pe()
        S.barrier()
        S.dma("sp", gbc, gmlp_d.partition_broadcast(128), writes=["gbc"])
        rstd_from_ss(stat[:, 3, :], stat[:, 3, :], "ssrow_r", "rla")
        for k in range(8, 16):
            S.op("dve", lambda e, k=k: e.tensor_scalar(out=wov[:, k, :], in0=wov[:, k, :], scalar1=vec[:, V_GA, k - 8:k - 7],
                                                       scalar2=None, op0=ALU.mult),
                 reads=["wo_hi"] + CONST_R, writes=["wo_s%d" % k])
        NG = NT

        def PA(g):
            br, j = 1, g
            bk = (g % 2) * 2

            def wo_mm(e):
                ins = None
                for hf in range(2):
                    for k in range(8):
                        ins = e.matmul(PS[bk + hf][:], lhsT=mixv[:, br * 8 + k, j * 128:(j + 1) * 128],
                                       rhs=wov[:, br * 8 + k, hf * 512:(hf + 1) * 512], start=(k == 0), stop=(k == 7))
                return ins
            S.op("pe", wo_mm, reads=["wo_s%d" % (br * 8 + k) for k in range(8)] + ["mixhalf%d" % br], writes=["PS%d" % bk, "PS%d" % (bk + 1)])

        def PB(g):
            br, j = 1, g
            bk = (g % 2) * 2
            for hf in range(2):
                S.op("dve", lambda e, hf=hf: e.scalar_tensor_tensor(
                    out=xres[:, j, hf * 512:(hf + 1) * 512], in0=PS[bk + hf][:], scalar=stat[:, 2 + br, j:j + 1],
                    in1=xres[:, j, hf * 512:(hf + 1) * 512], op0=ALU.mult, op1=ALU.add),
                    reads=["PS%d" % (bk + hf), "rla", "xres%d" % j], writes=["xres%d" % j])

        def PC1a(j):
            b = j % 2
            S.op("act", lambda e: e.activation(out=hn[b], in_=xres[:, j, :], func=AF.Square, accum_out=stat[:, 0, j:j + 1]),
                 reads=["xres%d" % j, "stat"], writes=["hn%d" % b, "p2ss%d" % j])

        def PC1c(j):
            rstd_from_ss(stat[:, 0, j:j + 1], stat[:, 1, j:j + 1], "p2ss%d" % j, "p2rs%d" % j)

        def PC1b(j):
            b = j % 2
            S.op("dve", lambda e: e.scalar_tensor_tensor(out=hn[b], in0=xres[:, j, :], scalar=stat[:, 1, j:j + 1], in1=gbc,
                                                         op0=ALU.mult, op1=ALU.mult),
                 reads=["xres%d" % j, "p2rs%d" % j, "gbc"], writes=["hn%d" % b])

        def PC2(j):
            b = j % 2
            tr_mm(b)
            tr_evac(b, j)

        for g in range(NG + 4):
            if 0 <= g - 3 < NG:
                PC2(g - 3)
            if g < NG:
                PA(g)
            if 0 <= g - 1 < NG:
                PB(g - 1)
                PC1a(g - 1)
            if 0 <= g - 2 < NG:
                PC1b(g - 2)
            if 0 <= g - 1 < NG:
                PC1c(g - 1)

        S.barrier()
        S.dma("sp", gbc, gfin_d.partition_broadcast(128), writes=["gbc"])
        actb = wo[:].rearrange("p (s f) -> p s f", s=2)
        NQ = 4
        units = [(q, tt) for q in range(NQ) for tt in range(NTT)]

        def wviews(q):
            sl = q % 2
            return (sl, mixw[:, sl, 0, :].rearrange("p (k f) -> p k f", k=8),
                    mixw[:, sl, 1, :].rearrange("p (k d) -> p k d", k=8))

        def MUP(u):
            q, tt = units[u]
            sl, wup_s, wdn_s = wviews(q)
            ab = u % 2
            act_s = actb[:, ab, 0:4096].rearrange("p (c t) -> p c t", c=8)
            for fc in range(8):
                ub = fc % 2

                def up(e, fc=fc, ub=ub):
                    ins = None
                    for k in range(8):
                        ins = e.matmul(PS[ub][:], lhsT=wup_s[:, k, fc * 128:(fc + 1) * 128],
                                       rhs=hT[:, k, tt * 512:(tt + 1) * 512], start=(k == 0), stop=(k == 7))
                    return ins
                S.op("pe", up, reads=["wup%d" % sl] + hT_keys(tt), writes=["PS%d" % ub])
                S.op("act", lambda e, ub=ub: e.activation(out=hn[ub].bitcast(F32), in_=PS[ub][:], func=AF.Relu),
                     reads=["PS%d" % ub], writes=["relu%d" % ub])
                S.op("dve", lambda e, ub=ub, fc=fc: e.tensor_tensor(out=act_s[:, fc, :], in0=hn[ub].bitcast(F32),
                                                                    in1=hn[ub].bitcast(F32), op=ALU.mult),
                     reads=["relu%d" % ub], writes=["act%d_%d" % (ab, fc)])

        def MDN(u):
            q, tt = units[u]
            sl, wup_s, wdn_s = wviews(q)
            ab = u % 2
            act_s = actb[:, ab, 0:4096].rearrange("p (c t) -> p c t", c=8)
            for t4 in range(4):
                j = tt * 4 + t4
                db = 2 + (t4 % 2) * 2

                def dn(e, t4=t4, db=db):
                    ins = None
                    for hf in range(2):
                        for fc in range(8):
                            ins = e.matmul(PS[db + hf][:], lhsT=act_s[:, fc, t4 * 128:(t4 + 1) * 128],
                                           rhs=wdn_s[:, fc, hf * 512:(hf + 1) * 512], start=(fc == 0), stop=(fc == 7))
                    return ins
                S.op("pe", dn, reads=["wdn%d" % sl] + ["act%d_%d" % (ab, fc) for fc in range(8)],
                     writes=["PS%d" % db, "PS%d" % (db + 1)])
                for hf in range(2):
                    S.op("dve", lambda e, hf=hf, db=db, j=j: e.tensor_tensor(
                        out=xres[:, j, hf * 512:(hf + 1) * 512], in0=PS[db + hf][:],
                        in1=xres[:, j, hf * 512:(hf + 1) * 512], op=ALU.add),
                        reads=["PS%d" % (db + hf), "xres%d" % j], writes=["xres%d" % j])
                if q == NQ - 1:
                    fj = wo[:, 4096:5120]
                    S.op("act", lambda e, j=j: e.activation(out=fj, in_=xres[:, j, :], func=AF.Square,
                                                            accum_out=stat[:, 0, j:j + 1]),
                         reads=["xres%d" % j, "stat"], writes=["fss%d" % j])
                    fin_flush()
                    rstd_from_ss(stat[:, 0, j:j + 1], stat[:, 1, j:j + 1], "fss%d" % j, "frs%d" % j)
                    fin_state["pend"] = j

        fin_state = {"pend": None}

        def fin_flush():
            j = fin_state["pend"]
            if j is None:
                return
            fin_state["pend"] = None
            S.op("dve", lambda e: e.scalar_tensor_tensor(
                out=xres[:, j, :], in0=xres[:, j, :], scalar=stat[:, 1, j:j + 1], in1=gbc,
                op0=ALU.mult, op1=ALU.mult),
                reads=["xres%d" % j, "frs%d" % j, "gbc"], writes=["xres%d" % j])
            S.dma("sp", out_d[j * 128:(j + 1) * 128, :], xres[:, j, :], reads=["xres%d" % j], writes=["out%d" % j])

        NU = len(units)
        mlp_load(1)
        MUP(0)
        for u in range(NU):
            if u + 1 < NU:
                MUP(u + 1)
            MDN(u)
            q, tt = units[u]
            if tt == NTT - 1 and q + 2 < NQ:
                mlp_load(q + 2)
        fin_flush()
        S.wait_keys("sp", ["out%d" % j for j in range(NT)])
    return nc


def _pack_vecs(inp):
    fm = lambda v: np.ascontiguousarray(np.asarray(v, np.float32).reshape(8, 128).T)
    cw = np.asarray(inp["conv_w"], np.float32)[0]
    vs = [fm(cw[0]), fm(cw[1]), fm(cw[2]), fm(cw[3]), fm(inp["conv_b"][0]), fm(inp["b_gate_a"][0]),
          fm(inp["b_gate_x"][0]), fm(inp["lru_lambda"][0]), fm(inp["lru_out_g"][0]), fm(inp["attn_out_g"][0]),
          fm(np.repeat(np.asarray(inp["attn_sinks"], np.float32)[0], 64))]
    return np.ascontiguousarray(np.stack(vs, axis=1).reshape(128, NV * 8))


_NC_CACHE = {}


def kernel(**inputs):
    x = np.asarray(inputs["x"], np.float32)
    nb = x.shape[0]
    if "nc" not in _NC_CACHE:
        _NC_CACHE["nc"] = build_program()
    nc = _NC_CACHE["nc"]
    f = lambda k: np.ascontiguousarray(np.asarray(inputs[k], np.float32)[0])
    shared = {
        "norm_mix_g": f("norm_mix_g"), "w_in": f("w_in"), "w_gate_a": f("w_gate_a"), "w_gate_x": f("w_gate_x"),
        "vecs": _pack_vecs(inputs), "w_out": f("w_out"), "norm_mlp_g": f("norm_mlp_g"),
        "w_mlp_up": f("w_mlp_up"), "w_mlp_down": f("w_mlp_down"),
        "norm_final_g": np.ascontiguousarray(np.asarray(inputs["norm_final_g"], np.float32)),
    }
    in_maps = [dict(shared, x=np.ascontiguousarray(x[b])) for b in range(nb)]
    res = run_bass_kernel_spmd(nc, in_maps, core_ids=list(range(nb)))
    return np.stack([np.asarray(r["out"], np.float32) for r in res.results], axis=0)
```
